# Optimizing a Trainium2 kernel written in Bass

```python
import jax, jax.numpy as jnp
from jax import lax
import numpy as np

D_MODEL = 1024
BATCH = 8
SEQ = 4096
DEPTH = 2

GRID_W = 64
CTX_LEN = 256
H_M = 4
DH_M = 128
M_W = H_M * DH_M
H_A = 8
H_KV = 2
DH_A = 64
A_Q = H_A * DH_A
A_KV = H_KV * DH_A
GQA_GROUP = H_A // H_KV
WINDOW = 128
WIN_BLOCK = 128
ROPE_THETA = 10000.0
AB_WIDTHS = (M_W, M_W, M_W, M_W, 4 * H_M, A_Q, A_KV, A_KV)
MIX_W = M_W + A_Q
H_C = 4
DK_C = 128
DV_C = 256
C_K = H_C * DK_C
C_V = H_C * DV_C
GATE_RANK = 16
GATE_TAU = 16.0
C_WIDTHS = (C_K, C_K, C_V, C_V, 2 * GATE_RANK)
CHUNK = 64
N_KEYS = 128
N_EXPERTS = N_KEYS * N_KEYS
PEER_HEADS = 8
PEER_TOPK = 16
PEER_KEY_DIM = 256
PEER_HALF = PEER_KEY_DIM // 2
PEER_CHUNK = 128
N_EVEN = (DEPTH + 1) // 2
N_ODD = DEPTH // 2
DEEPNORM_ALPHA = (2 * DEPTH) ** 0.25
DEEPNORM_BETA = (8 * DEPTH) ** -0.25
LN_EPS = 1e-5

kernel_name = "hybrid_mlstm_swa_gla_peer_diffusion_block"


def _layernorm(x, w, b):
    xf = x.astype(jnp.float32)
    mu = xf.mean(-1, keepdims=True)
    var = jnp.mean(jnp.square(xf - mu), -1, keepdims=True)
    return ((xf - mu) * lax.rsqrt(var + LN_EPS)).astype(x.dtype) * w + b


def _split(a, widths):
    idx = np.cumsum(widths)[:-1].tolist()
    return jnp.split(a, idx, axis=-1)


def _heads(a, h):
    b, s = a.shape[:2]
    return a.reshape(b, s, h, -1).transpose(0, 2, 1, 3)


def _head_norm(h, w):
    hf = h.astype(jnp.float32)
    hf = hf * lax.rsqrt(jnp.mean(hf * hf, -1, keepdims=True) + LN_EPS)
    b, nh, s, d = h.shape
    return hf.transpose(0, 2, 1, 3).reshape(b, s, nh * d) * w


def _to_chunks(a):
    b, h, s = a.shape[:3]
    a = a.reshape((b, h, s // CHUNK, CHUNK) + a.shape[3:])
    return jnp.moveaxis(a, 2, 0)


def _from_chunks(a):
    a = jnp.moveaxis(a, 0, 2)
    return a.reshape(a.shape[:2] + (-1,) + a.shape[4:])


def _mlstm_scan(q, k, v, log_i, log_f, state):
    causal = jnp.tril(jnp.ones((CHUNK, CHUNK), bool))

    def step(carry, inp):
        c_st, n_st, m_st = carry
        qc, kc, vc, li, lf = inp
        cum = jnp.cumsum(lf, axis=-1)
        dmat = cum[..., :, None] - cum[..., None, :] + li[..., None, :]
        dmat = jnp.where(causal, dmat, -jnp.inf)
        m_inter = cum + m_st[..., None]
        m_t = jnp.maximum(m_inter, dmat.max(-1))
        w = jnp.exp(dmat - m_t[..., None]) * jnp.einsum('bhtd,bhsd->bhts', qc, kc)
        a_inter = jnp.exp(m_inter - m_t)
        num = a_inter[..., None] * jnp.einsum('bhtd,bhde->bhte', qc, c_st) + jnp.einsum('bhts,bhse->bhte', w, vc)
        den = a_inter * jnp.einsum('bhtd,bhd->bht', qc, n_st) + w.sum(-1)
        h = num / jnp.maximum(jnp.abs(den), jnp.exp(-m_t))[..., None]
        total = cum[..., -1]
        decay_s = total[..., None] - cum + li
        m_new = jnp.maximum(total + m_st, decay_s.max(-1))
        ws = jnp.exp(decay_s - m_new[..., None])
        a_st = jnp.exp(total + m_st - m_new)
        c_st = a_st[..., None, None] * c_st + jnp.einsum('bhs,bhsd,bhse->bhde', ws, kc, vc)
        n_st = a_st[..., None] * n_st + jnp.einsum('bhs,bhsd->bhd', ws, kc)
        return (c_st, n_st, m_new), h

    state, hs = lax.scan(step, state, tuple(_to_chunks(a) for a in (q, k, v, log_i, log_f)))
    return _from_chunks(hs), state


def _gla_scan(q, k, v, log_a, state):
    causal = jnp.tril(jnp.ones((CHUNK, CHUNK), bool))

    def step(st, inp):
        qc, kc, vc, la = inp
        cum = jnp.cumsum(la, axis=2)
        rel = cum[:, :, :, None, :] - cum[:, :, None, :, :]
        rel = jnp.where(causal[:, :, None], rel, -jnp.inf)
        scores = jnp.einsum('bhtk,bhsk,bhtsk->bhts', qc, kc, jnp.exp(rel))
        out = jnp.einsum('bhtk,bhkv->bhtv', qc * jnp.exp(cum), st) + jnp.einsum('bhts,bhsv->bhtv', scores, vc)
        total = cum[:, :, -1:, :]
        st = jnp.exp(total[:, :, 0, :, None]) * st + jnp.einsum('bhsk,bhsv->bhkv', kc * jnp.exp(total - cum), vc)
        return st, out

    state, outs = lax.scan(step, state, tuple(_to_chunks(a) for a in (q, k, v, log_a)))
    return _from_chunks(outs), state


def _directional(scan_fn, ctx_seq, lat_seq, init, reverse):
    if reverse:
        ctx_seq = tuple(jnp.flip(a, 2) for a in ctx_seq)
        lat_seq = tuple(jnp.flip(a, 2) for a in lat_seq)
    h_ctx, state = scan_fn(*ctx_seq, init)
    h_lat, _ = scan_fn(*lat_seq, state)
    if reverse:
        h_ctx, h_lat = jnp.flip(h_ctx, 2), jnp.flip(h_lat, 2)
    return h_ctx, h_lat


def _axial_rope(s):
    rows = s // GRID_W
    row = jnp.repeat(jnp.arange(rows), GRID_W).astype(jnp.float32)
    col = jnp.tile(jnp.arange(GRID_W), rows).astype(jnp.float32)
    n_freq = DH_A // 4
    inv = ROPE_THETA ** (-jnp.arange(n_freq, dtype=jnp.float32) / n_freq)
    ang = jnp.concatenate([row[:, None] * inv, col[:, None] * inv], -1)
    return jnp.cos(ang), jnp.sin(ang)


def _rope(x, cos, sin):
    xf = x.astype(jnp.float32)
    x1, x2 = xf[..., 0::2], xf[..., 1::2]
    c = cos[None, :, None, :]
    sn = sin[None, :, None, :]
    return jnp.stack([x1 * c - x2 * sn, x1 * sn + x2 * c], -1).reshape(x.shape).astype(x.dtype)


def _window_attention(q, k, v, k_ctx, v_ctx, sink):
    b, s = q.shape[:2]
    nb = s // WIN_BLOCK
    lc = k_ctx.shape[1]
    qb = jnp.moveaxis(q.reshape(b, nb, WIN_BLOCK, H_KV, GQA_GROUP, DH_A), 1, 0)

    def bands(a):
        ap = jnp.pad(a, ((0, 0), (WIN_BLOCK, WIN_BLOCK), (0, 0), (0, 0))).reshape(b, nb + 2, WIN_BLOCK, H_KV, DH_A)
        return jnp.moveaxis(jnp.concatenate([ap[:, :-2], ap[:, 1:-1], ap[:, 2:]], 2), 1, 0)

    kw, vw = bands(k), bands(v)
    qi = jnp.arange(WIN_BLOCK)[:, None]
    kj = jnp.arange(3 * WIN_BLOCK)[None, :]
    key_pos = jnp.arange(nb)[:, None, None] * WIN_BLOCK + kj - WIN_BLOCK
    valid = (jnp.abs(kj - WIN_BLOCK - qi) <= WINDOW) & (key_pos >= 0) & (key_pos < s)
    scale = DH_A ** -0.5
    sink_l = sink.reshape(H_KV, GQA_GROUP).astype(jnp.float32)

    def block(args):
        qn, kn, vn, mask = args
        s_ctx = jnp.einsum('bqhgd,bchd->bhgqc', qn, k_ctx).astype(jnp.float32) * scale
        s_win = jnp.einsum('bqhgd,bkhd->bhgqk', qn, kn).astype(jnp.float32) * scale
        s_win = jnp.where(mask, s_win, -jnp.inf)
        sink_col = jnp.broadcast_to(sink_l[None, :, :, None, None], s_ctx.shape[:-1] + (1,))
        p = jax.nn.softmax(jnp.concatenate([sink_col, s_ctx, s_win], -1), -1).astype(qn.dtype)
        return (jnp.einsum('bhgqc,bchd->bqhgd', p[..., 1:1 + lc], v_ctx)
                + jnp.einsum('bhgqk,bkhd->bqhgd', p[..., 1 + lc:], vn))

    out = lax.map(block, (qb, kw, vw, valid))
    return jnp.moveaxis(out, 0, 1).reshape(b, s, H_A, DH_A)


def _context_attention(q, k, v, sink):
    b, lc = q.shape[:2]
    qg = q.reshape(b, lc, H_KV, GQA_GROUP, DH_A)
    sc = jnp.einsum('bqhgd,bkhd->bhgqk', qg, k).astype(jnp.float32) * DH_A ** -0.5
    sink_col = jnp.broadcast_to(sink.reshape(H_KV, GQA_GROUP).astype(jnp.float32)[None, :, :, None, None],
                                sc.shape[:-1] + (1,))
    p = jax.nn.softmax(jnp.concatenate([sink_col, sc], -1), -1)[..., 1:].astype(q.dtype)
    return jnp.einsum('bhgqk,bkhd->bqhgd', p, v).reshape(b, lc, H_A, DH_A)


def _ab_streams(h, w_in, gate_b):
    q_m, k_m, v_m, o_m, g_m, q_a, k_a, v_a = _split(h @ w_in, AB_WIDTHS)
    b, s = h.shape[:2]
    g = g_m.reshape(b, s, 2, 2, H_M).astype(jnp.float32) + gate_b.astype(jnp.float32)
    mlstm = (_heads(q_m, H_M).astype(jnp.float32),
             _heads(k_m, H_M).astype(jnp.float32) * DH_M ** -0.5,
             _heads(v_m, H_M).astype(jnp.float32))
    gates = [(g[:, :, d, 0].transpose(0, 2, 1), jax.nn.log_sigmoid(g[:, :, d, 1]).transpose(0, 2, 1))
             for d in range(2)]
    attn = (q_a.reshape(b, s, H_A, DH_A), k_a.reshape(b, s, H_KV, DH_A), v_a.reshape(b, s, H_KV, DH_A))
    return mlstm, gates, o_m, attn


def _merge_ab(m, o, a, norm_w, w_out):
    hm = _head_norm(m, norm_w).astype(o.dtype) * jax.nn.sigmoid(o)
    cat = jnp.concatenate([hm, a.reshape(a.shape[0], a.shape[1], A_Q)], -1)
    return cat @ w_out


def _mixer_ab(hl, hc, w_in, gate_b, norm_w, sink, w_out, ctx_out):
    ml, gl, ol, (ql, kl, vl) = _ab_streams(hl, w_in, gate_b)
    mc, gc, oc, (qc, kc, vc) = _ab_streams(hc, w_in, gate_b)
    b = hl.shape[0]
    init = (jnp.zeros((b, H_M, DH_M, DH_M), jnp.float32), jnp.zeros((b, H_M, DH_M), jnp.float32),
            jnp.zeros((b, H_M), jnp.float32))
    outs = [_directional(_mlstm_scan, mc + gc[d], ml + gl[d], init, d == 1) for d in range(2)]
    cos, sin = _axial_rope(hl.shape[1])
    a_lat = _window_attention(_rope(ql, cos, sin), _rope(kl, cos, sin), vl, kc, vc, sink)
    y_lat = _merge_ab(outs[0][1] + outs[1][1], ol, a_lat, norm_w, w_out)
    if not ctx_out:
        return y_lat, None
    y_ctx = _merge_ab(outs[0][0] + outs[1][0], oc, _context_attention(qc, kc, vc, sink), norm_w, w_out)
    return y_lat, y_ctx


def _c_streams(h, w_in, gate_up, gate_b):
    q, k, v, g, low = _split(h @ w_in, C_WIDTHS)
    b, s = h.shape[:2]
    qkv = (_heads(q, H_C).astype(jnp.float32) * DK_C ** -0.5,
           _heads(k, H_C).astype(jnp.float32),
           _heads(v, H_C).astype(jnp.float32))
    low = low.reshape(b, s, 2, GATE_RANK)
    log_a = [_heads(jax.nn.log_sigmoid((low[:, :, d] @ gate_up[d] + gate_b[d]).astype(jnp.float32)) / GATE_TAU, H_C)
             for d in range(2)]
    return qkv, log_a, g


def _merge_c(h, g, norm_w, w_out):
    return (_head_norm(h, norm_w).astype(g.dtype) * jax.nn.silu(g)) @ w_out


def _mixer_c(hl, hc, w_in, gate_up, gate_b, norm_w, w_out, ctx_out):
    sl, al, gl = _c_streams(hl, w_in, gate_up, gate_b)
    sc, ac, gc = _c_streams(hc, w_in, gate_up, gate_b)
    init = jnp.zeros((hl.shape[0], H_C, DK_C, DV_C), jnp.float32)
    outs = [_directional(_gla_scan, sc + (ac[d],), sl + (al[d],), init, d == 1) for d in range(2)]
    y_lat = _merge_c(outs[0][1] + outs[1][1], gl, norm_w, w_out)
    if not ctx_out:
        return y_lat, None
    return y_lat, _merge_c(outs[0][0] + outs[1][0], gc, norm_w, w_out)


def _peer(h, wq, keys, u, v):
    shape = h.shape
    tokens = h.reshape(-1, PEER_CHUNK, shape[-1])

    def block(hb):
        tc = hb.shape[0]
        q = (hb @ wq).reshape(tc, PEER_HEADS, 2, PEER_HALF)
        s = jnp.einsum('thpk,hpnk->thpn', q, keys)
        s_top, i_top = lax.top_k(s, PEER_TOPK)
        cand = s_top[:, :, 0, :, None] + s_top[:, :, 1, None, :]
        cid = i_top[:, :, 0, :, None] * N_KEYS + i_top[:, :, 1, None, :]
        best, pos = lax.top_k(cand.reshape(tc, PEER_HEADS, -1), PEER_TOPK)
        eid = jnp.take_along_axis(cid.reshape(tc, PEER_HEADS, -1), pos, -1)
        g = jax.nn.softmax(best.astype(jnp.float32), -1).astype(hb.dtype)
        act = jax.nn.gelu(jnp.einsum('td,thed->the', hb, u[eid]), approximate=False)
        return jnp.einsum('the,thed->td', g * act, v[eid])

    return lax.map(block, tokens).reshape(shape)


def setup_inputs(seed: int = 0) -> dict:
    key = jax.random.key(seed)
    ks = jax.random.split(key, 26)
    f32 = jnp.float32

    def nrm(k, shape, s):
        return jax.random.normal(k, shape, f32) * s

    d = D_MODEL
    f_bias = 3.0 + 3.0 * jnp.arange(H_M, dtype=f32) / (H_M - 1)
    gate_offset = jnp.stack([jnp.zeros((H_M,), f32), f_bias])[None, None]
    return {
        "x": nrm(ks[0], (BATCH, SEQ, d), 1.0),
        "c": nrm(ks[1], (BATCH, d), 1.0),
        "ctx": nrm(ks[2], (BATCH, CTX_LEN, d), 1.0),
        "c_ctx": nrm(ks[3], (d,), 1.0),
        "w_mod": nrm(ks[4], (DEPTH, d, 6 * d), 0.5 * d ** -0.5),
        "b_mod": nrm(ks[5], (DEPTH, 6 * d), 0.02),
        "ln_w": 1.0 + nrm(ks[6], (DEPTH, 2, d), 0.02),
        "ln_b": nrm(ks[7], (DEPTH, 2, d), 0.02),
        "ab_w_in": nrm(ks[8], (N_EVEN, d, sum(AB_WIDTHS)), d ** -0.5),
        "ab_gate_b": nrm(ks[9], (N_EVEN, 2, 2, H_M), 0.1) + gate_offset,
        "ab_norm_w": 1.0 + nrm(ks[10], (N_EVEN, M_W), 0.02),
        "ab_sink": nrm(ks[11], (N_EVEN, H_A), 0.5),
        "ab_w_out": nrm(ks[12], (N_EVEN, MIX_W, d), DEEPNORM_BETA * MIX_W ** -0.5),
        "gla_w_in": nrm(ks[13], (N_ODD, d, sum(C_WIDTHS)), d ** -0.5),
        "gla_gate_up": nrm(ks[14], (N_ODD, 2, GATE_RANK, C_K), GATE_RANK ** -0.5),
        "gla_gate_b": 1.0 + nrm(ks[15], (N_ODD, 2, C_K), 0.5),
        "gla_norm_w": 1.0 + nrm(ks[16], (N_ODD, C_V), 0.02),
        "gla_w_out": nrm(ks[17], (N_ODD, C_V, d), DEEPNORM_BETA * C_V ** -0.5),
        "peer_wq": nrm(ks[18], (DEPTH, d, PEER_HEADS * PEER_KEY_DIM), d ** -0.5),
        "peer_keys": nrm(ks[19], (DEPTH, PEER_HEADS, 2, N_KEYS, PEER_HALF), PEER_HALF ** -0.5),
        "peer_u": nrm(ks[20], (DEPTH, N_EXPERTS, d), d ** -0.5),
        "peer_v": nrm(ks[21], (DEPTH, N_EXPERTS, d), DEEPNORM_BETA * (PEER_HEADS * PEER_TOPK) ** -0.5),
    }


def reference(x, c, ctx, c_ctx, w_mod, b_mod, ln_w, ln_b, ab_w_in, ab_gate_b, ab_norm_w, ab_sink, ab_w_out,
              gla_w_in, gla_gate_up, gla_gate_b, gla_norm_w, gla_w_out, peer_wq, peer_keys, peer_u, peer_v):
    for layer in range(DEPTH):
        last = layer == DEPTH - 1
        j = layer // 2
        mod_l = jax.nn.silu(c) @ w_mod[layer] + b_mod[layer]
        mod_c = jax.nn.silu(c_ctx) @ w_mod[layer] + b_mod[layer]
        sh1, sc1, g1, sh2, sc2, g2 = jnp.split(mod_l[:, None, :], 6, axis=-1)
        csh1, csc1, cg1, csh2, csc2, cg2 = jnp.split(mod_c, 6)
        hl = x * (1.0 + sc1) + sh1
        hc = ctx * (1.0 + csc1) + csh1
        if layer % 2 == 0:
            y_lat, y_ctx = _mixer_ab(hl, hc, ab_w_in[j], ab_gate_b[j], ab_norm_w[j], ab_sink[j], ab_w_out[j], not last)
        else:
            y_lat, y_ctx = _mixer_c(hl, hc, gla_w_in[j], gla_gate_up[j], gla_gate_b[j], gla_norm_w[j], gla_w_out[j],
                                    not last)
        x = _layernorm(DEEPNORM_ALPHA * x + g1 * y_lat, ln_w[layer, 0], ln_b[layer, 0])
        f_lat = _peer(x * (1.0 + sc2) + sh2, peer_wq[layer], peer_keys[layer], peer_u[layer], peer_v[layer])
        x = _layernorm(DEEPNORM_ALPHA * x + g2 * f_lat, ln_w[layer, 1], ln_b[layer, 1])
        if not last:
            ctx = _layernorm(DEEPNORM_ALPHA * ctx + cg1 * y_ctx, ln_w[layer, 0], ln_b[layer, 0])
            f_ctx = _peer(ctx * (1.0 + csc2) + csh2, peer_wq[layer], peer_keys[layer], peer_u[layer], peer_v[layer])
            ctx = _layernorm(DEEPNORM_ALPHA * ctx + cg2 * f_ctx, ln_w[layer, 1], ln_b[layer, 1])
    return x
```

```python
from contextlib import ExitStack
import numpy as np
import concourse.bass as bass
import concourse.mybir as mybir
from concourse.bass_utils import run_bass_kernel_spmd

F32 = mybir.dt.float32
BF16 = mybir.dt.bfloat16
U32 = mybir.dt.uint32
AF = mybir.ActivationFunctionType
ALU = mybir.AluOpType
AX = mybir.AxisListType

D = 1024
NKEY = 128
PH = 8
PK = 16


class Prog:
    def __init__(self, nc, n_dma_slots=12):
        self.nc = nc
        self.eng = {"pe": nc.tensor, "act": nc.scalar, "dve": nc.vector, "pool": nc.gpsimd, "sp": nc.sync}
        self.csem = {e: nc.alloc_semaphore("c_" + e) for e in ("pe", "act", "dve", "pool")}
        self.cnt = {e: 0 for e in self.csem}
        self.dslots = {}
        for q, n in (("sp", n_dma_slots), ("pool", 6), ("act", 4)):
            self.dslots[q] = [[nc.alloc_semaphore("d_%s_%d" % (q, i)), 0] for i in range(n)]
        self.dnext = {q: 0 for q in self.dslots}
        self.seen = {e: {} for e in self.eng}
        self.lastw = {}
        self.lastr = {}
        self.multi = {}
        self.n_ops = 0

    def _wait(self, e, tok):
        if tok is None:
            return
        sem, val, src = tok
        key = sem.num if hasattr(sem, "num") else id(sem)
        if self.seen[e].get(key, 0) >= val:
            return
        self.eng[e].wait_ge(sem, val)
        self.seen[e][key] = val

    def _deps(self, e, reads, writes):
        for b in reads:
            for t in self.multi.get(b, ()):
                self._wait(e, t)
            t = self.lastw.get(b)
            if t is not None and not (t[2] == e and e == "pe"):
                self._wait(e, t)
        for b in writes:
            t = self.lastw.get(b)
            if t is not None and t[2] != e:
                self._wait(e, t)
            for t in self.lastr.get(b, ()):
                if t[2] != e:
                    self._wait(e, t)

    def _record(self, tok, reads, writes):
        for b in reads:
            self.lastr.setdefault(b, []).append(tok)
            if len(self.lastr[b]) > 6:
                best = {}
                for t in self.lastr[b]:
                    k = (t[2], t[0].num if hasattr(t[0], "num") else id(t[0]))
                    if k not in best or best[k][1] < t[1]:
                        best[k] = t
                self.lastr[b] = list(best.values())
        for b in writes:
            self.lastw[b] = tok
            self.lastr[b] = []

    def op(self, e, fn, reads=(), writes=(), track=True):
        self._deps(e, reads, writes)
        inst = fn(self.eng[e])
        self.n_ops += 1
        if track:
            self.cnt[e] += 1
            inst.then_inc(self.csem[e], 1)
            tok = (self.csem[e], self.cnt[e], e)
        else:
            tok = (self.csem[e], self.cnt[e] + 1, e)
        self._record(tok, reads, writes)
        return tok

    def dma(self, out, in_, reads=(), writes=(), q="sp", **kw):
        slots = self.dslots[q]
        i = self.dnext[q]
        self.dnext[q] = (i + 1) % len(slots)
        sem, uses = slots[i]
        if uses > 0:
            self._wait(q, (sem, 16 * uses, "dma"))
        self._deps(q, reads, writes)
        self.eng[q].dma_start(out=out, in_=in_, **kw).then_inc(sem, 16)
        self.n_ops += 1
        slots[i][1] = uses + 1
        tok = (sem, 16 * (uses + 1), "dma")
        self._record(tok, reads, writes)
        return tok

    def barrier(self):
        toks = [(self.csem[e], self.cnt[e], e) for e in self.csem if self.cnt[e] > 0]
        for q, slots in self.dslots.items():
            toks += [(sem, 16 * uses, "dma") for sem, uses in slots if uses > 0]
        for e in self.eng:
            for t in toks:
                self._wait(e, t)
        self.lastw = {k: None for k in self.lastw}
        self.lastr = {}
        self.multi = {}

    def finish(self):
        for q, slots in self.dslots.items():
            for sem, uses in slots:
                if uses > 0:
                    self._wait("sp", (sem, 16 * uses, "dma"))


def bc(ap, shape):
    return ap.to_broadcast(shape)


ALPHA = 4.0 ** 0.25
LN_EPS = 1e-5


class Ctx:
    def __init__(self, nc, P, consts_d):
        self.nc = nc
        self.P = P
        self.uid = 0
        self.consts_d = consts_d
        self.ident = nc.alloc_sbuf_tensor("c_ident", [128, 128], F32)
        self.iota = nc.alloc_sbuf_tensor("c_iota", [128, 128], F32)
        P.dma(self.ident[:], consts_d[0], writes=["c_ident"])
        P.dma(self.iota[:], consts_d[1], writes=["c_iota"])

    def name(self, s):
        self.uid += 1
        return "%s_%d" % (s, self.uid)


class V:
    def __init__(self, ap, name):
        self.ap = ap
        self.name = name

    def __getitem__(self, k):
        return self.ap if k == slice(None) else self.ap[k]


def layernorm_rows(P, nc, y, yn, tmp, st, lnw, lnb, tag, out):
    yk, tk, sk = y.name, tmp.name, st.name
    P.op("dve", lambda e: e.reduce_sum(out=st[:, 0:1], in_=y[:], axis=AX.X), reads=[yk], writes=[sk])
    P.op("dve", lambda e: e.tensor_single_scalar(out=st[:, 1:2], in_=st[:, 0:1], scalar=-1.0 / D, op=ALU.mult), reads=[sk], writes=[sk])
    P.op("dve", lambda e: e.tensor_scalar(out=y[:], in0=y[:], scalar1=st[:, 1:2], scalar2=None, op0=ALU.add), reads=[yk, sk], writes=[yk])
    P.op("act", lambda e: e.activation(out=tmp[:], in_=y[:], func=AF.Square, accum_out=st[:, 2:3]), reads=[yk], writes=[tk, sk])
    P.op("dve", lambda e: e.tensor_scalar(out=st[:, 3:4], in0=st[:, 2:3], scalar1=1.0 / D, scalar2=LN_EPS, op0=ALU.mult, op1=ALU.add), reads=[sk], writes=[sk])
    P.op("act", lambda e: e.activation(out=st[:, 3:4], in_=st[:, 3:4], func=AF.Sqrt), reads=[sk], writes=[sk])
    P.op("dve", lambda e: e.reciprocal(out=st[:, 3:4], in_=st[:, 3:4]), reads=[sk], writes=[sk])
    P.op("dve", lambda e: e.tensor_scalar(out=y[:], in0=y[:], scalar1=st[:, 3:4], scalar2=None, op0=ALU.mult), reads=[yk, sk], writes=[yk])
    P.op("dve", lambda e: e.tensor_tensor(out=y[:], in0=y[:], in1=lnw[:], op=ALU.mult), reads=[yk, lnw.name], writes=[yk])
    P.op("dve", lambda e: e.tensor_tensor(out=out[:], in0=y[:], in1=lnb[:], op=ALU.add), reads=[yk, lnb.name], writes=[out.name])


def top16(P, nc, src_ap, srck, scratch, vals_ap, idx_ap, outk):
    sk = scratch.name
    n = src_ap.shape[-1]
    P.op("dve", lambda e: e.max(out=vals_ap[:, 0:8], in_=src_ap), reads=[srck], writes=outk)
    P.op("dve", lambda e: e.max_index(out=idx_ap[:, 0:8], in_max=vals_ap[:, 0:8], in_values=src_ap), reads=[srck] + outk, writes=outk)
    P.op("dve", lambda e: e.match_replace(out=scratch[:, 0:n], in_to_replace=vals_ap[:, 0:8], in_values=src_ap, imm_value=-1e30), reads=[srck] + outk, writes=[sk])
    P.op("dve", lambda e: e.max(out=vals_ap[:, 8:16], in_=scratch[:, 0:n]), reads=[sk], writes=outk)
    P.op("dve", lambda e: e.max_index(out=idx_ap[:, 8:16], in_max=vals_ap[:, 8:16], in_values=scratch[:, 0:n]), reads=[sk] + outk, writes=outk)


def emit_peer(C, T, x_in, xin_key, x_out, xout_key, mod_d, lnw_d, lnb_d, wq_d, keys_d, ure_d, v_d, ubf_d, vbf_d, n_groups=None):
    nc, P = C.nc, C.P
    TG = 256
    NG = T // TG if n_groups is None else n_groups
    with ExitStack() as es:
        def sb(name, shape, dt):
            return es.enter_context(nc.sbuf_tensor(C.name(name), shape, dt))

        def ps(name, shape, dt=F32):
            return es.enter_context(nc.psum_tensor(C.name(name), shape, dt))

        wq = sb("wq", [128, 8, 2048], BF16)
        for dc in range(8):
            P.dma(wq[:, dc, :], wq_d[dc * 128:(dc + 1) * 128, :], writes=["%s.%d" % (wq.name, dc)], q="pool")
        keysT = sb("keysT", [128, 16, 128], BF16)
        S = sb("S", [128, 2048], F32)
        ktmp = V(S[:].rearrange("p (h k) -> p h k", h=16), S.name)
        P.dma(ktmp[:], keys_d.rearrange("h n k -> n h k"), writes=[ktmp.name])
        modr = sb("modr", [128, 3, D], F32)
        for r in range(3):
            P.dma(modr[:, r, :], mod_d[r].partition_broadcast(128), writes=["%s.%d" % (modr.name, r)])
        lnw = sb("lnw", [128, D], F32)
        lnb = sb("lnb", [128, D], F32)
        P.dma(lnw[:], lnw_d.partition_broadcast(128), writes=[lnw.name])
        P.dma(lnb[:], lnb_d.partition_broadcast(128), writes=[lnb.name])

        pbig = [ps("pbig%d" % i, [128, 1024]) for i in range(2)]
        psm = [ps("psm%d" % i, [128, 512]) for i in range(4)]
        for hp in range(16):
            pt = psm[hp % 4]
            P.op("pe", lambda e, pt=pt, hp=hp: e.transpose(out=pt[:, 0:128], in_=ktmp[:, hp, :], identity=C.ident[:]),
                 reads=[ktmp.name, "c_ident"], writes=[pt.name])
            P.op("act", lambda e, pt=pt, hp=hp: e.copy(out=keysT[:, hp, :], in_=pt[:, 0:128]), reads=[pt.name], writes=[keysT.name])

        xt = [sb("xt%d" % i, [128, D], F32) for i in range(2)]
        hm = sb("hm", [128, D], F32)
        hT = sb("hT", [128, 8, TG], BF16)
        qT = sb("qT", [128, 16, TG], BF16)
        scr = sb("scr", [128, 256], F32)
        stop = sb("stop", [128, 16, 16], F32)
        itopu = sb("itopu", [128, 16, 16], U32)
        itopf = sb("itopf", [128, 16, 16], F32)
        cand = sb("cand", [128, 8, 256], F32)
        eq = V(cand[:].rearrange("p h (a b) -> p h a b", a=16), cand.name)
        best = sb("best", [128, 8, 16], F32)
        posu = sb("posu", [128, 8, 16], U32)
        abu = sb("abu", [128, 2, 8, 16], U32)
        abf = sb("abf", [128, 2, 8, 16], F32)
        IJG = sb("IJG", [128, 3, 128], F32)
        IJGT = sb("IJGT", [128, 3, TG], F32)
        sm = sb("sm", [128, 8, 4], F32)
        oh1 = [sb("oh1_%d" % i, [128, 4, 128], F32) for i in range(2)]
        oig = [sb("oig%d" % i, [128, 4, 128], BF16) for i in range(2)]
        oj = [sb("oj%d" % i, [128, 4, 128], BF16) for i in range(2)]
        Gall = sb("Gall", [128, TG, 128], BF16)
        UB = 2
        ubuf = [sb("ubuf%d" % i, [128, UB, 8, 128], BF16) for i in range(2)]
        vbuf = [sb("vbuf%d" % i, [128, UB, D], BF16) for i in range(2)]
        Ag = [sb("Ag%d" % i, [128, TG], F32) for i in range(2)]
        GA = [sb("GA%d" % i, [128, TG], BF16) for i in range(2)]
        ybuf = hm
        ytmp = V(S[:, 0:D], S.name)
        yout = V(S[:, D:2 * D], S.name)
        st = sb("st", [128, 4], F32)

        for g in range(NG):
            t0 = g * TG
            for tt in range(2):
                xk = xt[tt].name
                P.dma(xt[tt][:], x_in[t0 + tt * 128: t0 + (tt + 1) * 128, :], reads=["%s.%d" % (xin_key, (t0 + tt * 128) // 128)], writes=[xk])
                P.op("dve", lambda e, tt=tt: e.tensor_tensor(out=hm[:], in0=xt[tt][:], in1=modr[:, 0, :], op=ALU.mult), reads=[xk, modr.name + ".0"], writes=[hm.name])
                P.op("dve", lambda e: e.tensor_tensor(out=hm[:], in0=hm[:], in1=modr[:, 1, :], op=ALU.add), reads=[hm.name, modr.name + ".1"], writes=[hm.name])
                pb = pbig[tt]
                for dc in range(8):
                    P.op("pe", lambda e, dc=dc, pb=pb: e.transpose(out=pb[:, dc * 128:(dc + 1) * 128], in_=hm[:, dc * 128:(dc + 1) * 128], identity=C.ident[:]),
                         reads=[hm.name, "c_ident"], writes=[pb.name], track=(dc == 7))
                P.op("act", lambda e, tt=tt, pb=pb: e.copy(out=hT[:, :, tt * 128:(tt + 1) * 128], in_=pb[:].rearrange("p (c t) -> p c t", c=8)),
                     reads=[pb.name], writes=[hT.name])
            for hp in range(16):
                pt = psm[hp % 4]
                for dc in range(8):
                    P.op("pe", lambda e, hp=hp, dc=dc, pt=pt: e.matmul(pt[:, 0:TG], lhsT=wq[:, dc, hp * 128:(hp + 1) * 128], rhs=hT[:, dc, :], start=(dc == 0), stop=(dc == 7)),
                         reads=["%s.%d" % (wq.name, dc), hT.name], writes=[pt.name], track=(dc == 7))
                P.op("act", lambda e, hp=hp, pt=pt: e.copy(out=qT[:, hp, :], in_=pt[:, 0:TG]), reads=[pt.name], writes=[qT.name])
            for tt in range(2):
                for hp in range(16):
                    pb = pbig[hp // 8]
                    c0 = (hp % 8) * 128
                    P.op("pe", lambda e, hp=hp, pb=pb, c0=c0, tt=tt: e.matmul(pb[:, c0:c0 + 128], lhsT=qT[:, hp, tt * 128:(tt + 1) * 128], rhs=keysT[:, hp, :], start=True, stop=True),
                         reads=[qT.name, keysT.name], writes=[pb.name], track=(hp % 8 == 7))
                for hb in range(2):
                    P.op("act", lambda e, hb=hb: e.copy(out=S[:, hb * 1024:(hb + 1) * 1024], in_=pbig[hb][:]), reads=[pbig[hb].name], writes=[S.name])
                for hp in range(16):
                    top16(P, nc, S[:, hp * 128:(hp + 1) * 128], S.name, scr, stop[:, hp, :], itopu[:, hp, :], [stop.name, itopu.name])
                P.op("dve", lambda e: e.tensor_copy(out=itopf[:], in_=itopu[:]), reads=[itopu.name], writes=[itopf.name])
                sv = stop[:].rearrange("p (h two) k -> p h two k", two=2)
                P.op("dve", lambda e: e.tensor_tensor(out=cand[:].rearrange("p h (a b) -> p h a b", a=16),
                                                      in0=sv[:, :, 0, :].unsqueeze(3).to_broadcast([128, 8, 16, 16]),
                                                      in1=sv[:, :, 1, :].unsqueeze(2).to_broadcast([128, 8, 16, 16]), op=ALU.add),
                     reads=[stop.name], writes=[cand.name])
                for h in range(8):
                    top16(P, nc, cand[:, h, :], cand.name, scr, best[:, h, :], posu[:, h, :], [best.name, posu.name])
                P.op("dve", lambda e: e.tensor_single_scalar(out=abu[:, 0], in_=posu[:], scalar=4, op=ALU.logical_shift_right), reads=[posu.name], writes=[abu.name])
                P.op("dve", lambda e: e.tensor_single_scalar(out=abu[:, 1], in_=posu[:], scalar=15, op=ALU.bitwise_and), reads=[posu.name], writes=[abu.name])
                P.op("dve", lambda e: e.tensor_copy(out=abf[:], in_=abu[:]), reads=[abu.name], writes=[abf.name])
                iv = itopf[:].rearrange("p (h two) k -> p h two k", two=2)
                for w in range(2):
                    P.op("dve", lambda e, w=w: e.tensor_tensor(out=eq[:], in0=abf[:, w].unsqueeze(3).to_broadcast([128, 8, 16, 16]),
                                                               in1=C.iota[:, 0:16].unsqueeze(1).unsqueeze(1).to_broadcast([128, 8, 16, 16]), op=ALU.is_equal),
                         reads=[abf.name, "c_iota"], writes=[eq.name])
                    P.op("dve", lambda e, w=w: e.tensor_tensor(out=eq[:], in0=eq[:], in1=iv[:, :, w, :].unsqueeze(2).to_broadcast([128, 8, 16, 16]), op=ALU.mult),
                         reads=[eq.name, itopf.name], writes=[eq.name])
                    P.op("dve", lambda e, w=w: e.reduce_sum(out=IJG[:, w, :].rearrange("p (h k) -> p h k", h=8), in_=eq[:], axis=AX.X),
                         reads=[eq.name], writes=[IJG.name])
                gk = IJG[:, 2, :].rearrange("p (h k) -> p h k", h=8)
                P.op("dve", lambda e: e.tensor_tensor(out=gk, in0=best[:], in1=best[:, :, 0:1].to_broadcast([128, 8, 16]), op=ALU.subtract),
                     reads=[best.name], writes=[IJG.name])
                P.op("act", lambda e: e.activation(out=gk, in_=gk, func=AF.Exp), reads=[IJG.name], writes=[IJG.name])
                P.op("dve", lambda e: e.reduce_sum(out=sm[:, :, 0:1], in_=gk, axis=AX.X), reads=[IJG.name], writes=[sm.name])
                P.op("dve", lambda e: e.reciprocal(out=sm[:, :, 1:2], in_=sm[:, :, 0:1]), reads=[sm.name], writes=[sm.name])
                P.op("dve", lambda e: e.tensor_tensor(out=gk, in0=gk, in1=sm[:, :, 1:2].to_broadcast([128, 8, 16]), op=ALU.mult),
                     reads=[IJG.name, sm.name], writes=[IJG.name])
                for w in range(3):
                    pt = psm[w]
                    P.op("pe", lambda e, w=w, pt=pt: e.transpose(out=pt[:, 0:128], in_=IJG[:, w, :], identity=C.ident[:]), reads=[IJG.name, "c_ident"], writes=[pt.name])
                    P.op("act", lambda e, w=w, pt=pt, tt=tt: e.copy(out=IJGT[:, w, tt * 128:(tt + 1) * 128], in_=pt[:, 0:128]), reads=[pt.name], writes=[IJGT.name])
            for tb in range(TG // 4):
                s = tb % 2
                for k in range(4):
                    t = tb * 4 + k
                    P.op("dve", lambda e, s=s, k=k, t=t: e.tensor_scalar(out=oh1[s][:, k, :], in0=C.iota[:], scalar1=IJGT[:, 0, t:t + 1], scalar2=None, op0=ALU.is_equal),
                         reads=["c_iota", IJGT.name], writes=[oh1[s].name])
                    P.op("dve", lambda e, s=s, k=k, t=t: e.tensor_scalar(out=oig[s][:, k, :], in0=oh1[s][:, k, :], scalar1=IJGT[:, 2, t:t + 1], scalar2=None, op0=ALU.mult),
                         reads=[oh1[s].name, IJGT.name], writes=[oig[s].name])
                    P.op("pool", lambda e, s=s, k=k, t=t: e.tensor_scalar(out=oj[s][:, k, :], in0=C.iota[:], scalar1=IJGT[:, 1, t:t + 1], scalar2=None, op0=ALU.is_equal),
                         reads=["c_iota", IJGT.name], writes=[oj[s].name])
                pt = psm[tb % 2]
                for k in range(4):
                    P.op("pe", lambda e, s=s, k=k, pt=pt: e.matmul(pt[:, k * 128:(k + 1) * 128], lhsT=oj[s][:, k, :], rhs=oig[s][:, k, :], start=True, stop=True),
                         reads=[oj[s].name, oig[s].name], writes=[pt.name], track=(k == 3))
                P.op("act", lambda e, tb=tb, pt=pt: e.copy(out=Gall[:, tb * 4:(tb + 1) * 4, :], in_=pt[:].rearrange("p (k i) -> p k i", k=4)),
                     reads=[pt.name], writes=[Gall.name])
            def load_uv(blk):
                s = blk % 2
                P.dma(ubuf[s][:], ubf_d[blk * UB:(blk + 1) * UB].rearrange("i p c e -> p i c e"), reads=["ubf%d" % (blk * UB // 8)], writes=[ubuf[s].name])
                P.dma(vbuf[s][:], vbf_d[blk * UB * 128:(blk + 1) * UB * 128, :].rearrange("(i e) d -> e i d", i=UB), reads=["vbf%d" % (blk * UB // 8)], writes=[vbuf[s].name])

            def a_mm(i):
                s, ii = (i // UB) % 2, i % UB
                pa = psm[2 + i % 2]
                for dc in range(8):
                    P.op("pe", lambda e, s=s, ii=ii, dc=dc, pa=pa: e.matmul(pa[:, 0:TG], lhsT=ubuf[s][:, ii, dc, :], rhs=hT[:, dc, :], start=(dc == 0), stop=(dc == 7)),
                         reads=[ubuf[s].name, hT.name], writes=[pa.name], track=(dc == 7))
                P.op("act", lambda e, i=i, pa=pa: e.activation(out=Ag[i % 2][:], in_=pa[:, 0:TG], func=AF.Gelu), reads=[pa.name], writes=[Ag[i % 2].name])
                P.op("dve", lambda e, i=i: e.tensor_tensor(out=GA[i % 2][:], in0=Ag[i % 2][:], in1=Gall[:, :, i], op=ALU.mult),
                     reads=[Ag[i % 2].name, Gall.name], writes=[GA[i % 2].name])

            def v_mm(i):
                s, ii = (i // UB) % 2, i % UB
                for tt in range(2):
                    for hf in range(2):
                        P.op("pe", lambda e, s=s, ii=ii, tt=tt, hf=hf, i=i: e.matmul(pbig[tt][:, hf * 512:(hf + 1) * 512], lhsT=GA[i % 2][:, tt * 128:(tt + 1) * 128],
                                                                                   rhs=vbuf[s][:, ii, hf * 512:(hf + 1) * 512], start=(i == 0), stop=(i == 127)),
                             reads=[GA[i % 2].name, vbuf[s].name], writes=[pbig[tt].name], track=(tt == 1 and hf == 1))

            load_uv(0)
            load_uv(1)
            a_mm(0)
            for i in range(128):
                if i + 1 < 128:
                    a_mm(i + 1)
                v_mm(i)
                if i % UB == UB - 1 and i // UB + 2 < 128 // UB:
                    load_uv(i // UB + 2)
            for tt in range(2):
                P.op("dve", lambda e, tt=tt: e.tensor_tensor(out=ytmp[:], in0=pbig[tt][:], in1=modr[:, 2, :], op=ALU.mult), reads=[pbig[tt].name, modr.name + ".2"], writes=[ytmp.name])
                P.op("dve", lambda e, tt=tt: e.scalar_tensor_tensor(out=ybuf[:], in0=xt[tt][:], scalar=ALPHA, in1=ytmp[:], op0=ALU.mult, op1=ALU.add),
                     reads=[xt[tt].name, ytmp.name], writes=[ybuf.name])
                layernorm_rows(P, nc, ybuf, None, ytmp, st, lnw, lnb, "pe", yout)
                P.dma(x_out[t0 + tt * 128: t0 + (tt + 1) * 128, :], yout[:], reads=[yout.name], writes=["%s.%d" % (xout_key, (t0 + tt * 128) // 128)])
    C.P.barrier()


def emit_peer_prep(C, ure_d, v_d, ubf_d, vbf_d):
    P = C.P
    for b in range(16):
        P.dma(ubf_d[b * 8:(b + 1) * 8].rearrange("i p c e -> (i p) (c e)"), ure_d[b * 8:(b + 1) * 8].rearrange("i p c e -> (i p) (c e)"),
              reads=["ure"], writes=["ubf%d" % b], q="pool")
        P.dma(vbf_d[b * 1024:(b + 1) * 1024, :], v_d[b * 1024:(b + 1) * 1024, :], reads=["vsrc"], writes=["vbf%d" % b], q="pool")


def emit_mod(C, ccT_d, wmod_d, bmod_d, modd):
    nc, P = C.nc, C.P
    with ExitStack() as es:
        def sb(name, shape, dt):
            return es.enter_context(nc.sbuf_tensor(C.name(name), shape, dt))
        cc = sb("cc", [128, 8, 2], F32)
        P.dma(cc[:], ccT_d, writes=[cc.name])
        P.op("act", lambda e: e.activation(out=cc[:], in_=cc[:], func=AF.Silu), reads=[cc.name], writes=[cc.name])
        wb = [sb("wmod%d" % i, [128, 8, 512], F32) for i in range(2)]
        bm = sb("bm", [2, 6 * D], F32)
        P.dma(bm[:], bmod_d.partition_broadcast(2), writes=[bm.name])
        mo = sb("mo", [2, 6 * D], F32)
        pm = [es.enter_context(nc.psum_tensor(C.name("pmod%d" % i), [128, 512], F32)) for i in range(2)]
        for n in range(12):
            w = wb[n % 2]
            P.dma(w[:], wmod_d[:, n * 512:(n + 1) * 512].rearrange("(c p) n -> p c n", p=128), writes=[w.name])
            pp = pm[n % 2]
            for dc in range(8):
                P.op("pe", lambda e, dc=dc, w=w, pp=pp: e.matmul(pp[0:2, :], lhsT=cc[:, dc, :], rhs=w[:, dc, :], start=(dc == 0), stop=(dc == 7)),
                     reads=[cc.name, w.name], writes=[pp.name], track=(dc == 7))
            P.op("dve", lambda e, n=n, pp=pp: e.tensor_tensor(out=mo[:, n * 512:(n + 1) * 512], in0=pp[0:2, :], in1=bm[:, n * 512:(n + 1) * 512], op=ALU.add),
                 reads=[pp.name, bm.name], writes=[mo.name])
        for k in (1, 4):
            P.op("dve", lambda e, k=k: e.tensor_single_scalar(out=mo[:, k * D:(k + 1) * D], in_=mo[:, k * D:(k + 1) * D], scalar=1.0, op=ALU.add),
                 reads=[mo.name], writes=[mo.name])
        P.dma(modd.rearrange("r k d -> r (k d)"), mo[:], reads=[mo.name], writes=["modd"])
    C.P.barrier()


def emit_scan(C, NT, n_ctx, QKT, KT, VA, ET, HO, DV, aug, masks_d, key):
    nc, P = C.nc, C.P
    DVO = DV - 1 if aug else DV
    with ExitStack() as es:
        def sb(name, shape, dt):
            return es.enter_context(nc.sbuf_tensor(C.name(name), shape, dt))

        def ps(name, shape, dt=F32):
            return es.enter_context(nc.psum_tensor(C.name(name), shape, dt))
        mask = sb("mask", [128, 2, 128], F32)
        P.dma(mask[:, 0, :], masks_d[0], writes=[mask.name + ".0"])
        P.dma(mask[:, 1, :], masks_d[1], writes=[mask.name + ".1"])
        St = [sb("St%d" % d, [128, 4, DV], F32) for d in range(2)]
        Sb = [sb("Sb%d" % d, [128, 4, DV], BF16) for d in range(2)]
        for d in range(2):
            P.op("dve", lambda e, d=d: e.memset(St[d][:], 0.0), writes=[St[d].name])
            P.op("pool", lambda e, d=d: e.memset(Sb[d][:], 0.0), writes=[Sb[d].name])
        qkt = [sb("qkt%d" % i, [128, 2, 4, 128], BF16) for i in range(2)]
        kt = [sb("kt%d" % i, [128, 4, 128], BF16) for i in range(2)]
        va = [sb("va%d" % i, [128, 4, DV], BF16) for i in range(2)]
        et = [sb("et%d" % i, [128, 4], F32) for i in range(2)]
        WT = [sb("WT%d" % i, [128, 4, 128], BF16) for i in range(2)]
        tmp = sb("tmp", [128, 4, DV], F32)
        ho = [sb("ho%d" % i, [128, 4, DVO], F32) for i in range(2)]
        dn = sb("dn", [128, 4, 2], F32)
        pS = [ps("pS%d" % d, [128, 512]) for d in range(2)]
        NB = 2 if DV <= 256 else 4
        pN = ps("pN", [128, 2, 512])
        pD = ps("pD", [128, 2, 512])
        lat = list(range(n_ctx, NT))
        order = [list(range(n_ctx)) + lat, list(range(n_ctx))[::-1] + lat[::-1]]
        for n in range(NT):
            for d in range(2):
                tile = order[d][n]
                b = d
                tk = "%s.%d" % (key, tile)
                qv = QKT[tile].rearrange("p (w dd h) t -> p w dd h t", w=2, dd=2)
                P.dma(qkt[b][:], qv[:, :, d, :, :], reads=[tk], writes=[qkt[b].name])
                P.dma(kt[b][:], KT[tile][:, d * 4:(d + 1) * 4, :], reads=[tk], writes=[kt[b].name])
                P.dma(va[b][:], VA[tile], reads=[tk], writes=[va[b].name])
                P.dma(et[b][:], ET[tile][:, d * 4:(d + 1) * 4], reads=[tk], writes=[et[b].name])
                for h in range(4):
                    P.op("pe", lambda e, h=h, b=b, d=d: e.matmul(pS[d][:, h * 128:(h + 1) * 128], lhsT=qkt[b][:, 1, h, :], rhs=qkt[b][:, 0, h, :], start=True, stop=True),
                         reads=[qkt[b].name], writes=[pS[d].name], track=(h == 3))
                P.op("dve", lambda e, b=b, d=d: e.tensor_tensor(out=WT[b][:], in0=pS[d][:].rearrange("p (h t) -> p h t", h=4),
                                                               in1=mask[:, d, :].unsqueeze(1).to_broadcast([128, 4, 128]), op=ALU.mult),
                     reads=[pS[d].name, mask.name + ".%d" % d], writes=[WT[b].name])
                for h in range(4):
                    o = pN[:, h // 2, (h % 2) * 256:(h % 2) * 256 + DV]
                    P.op("pe", lambda e, h=h, b=b, d=d, o=o: e.matmul(o, lhsT=qkt[b][:, 0, h, :], rhs=Sb[d][:, h, :], start=True, stop=False),
                         reads=[qkt[b].name, Sb[d].name], writes=["pN"], track=False)
                    P.op("pe", lambda e, h=h, b=b, o=o: e.matmul(o, lhsT=WT[b][:, h, :], rhs=va[b][:, h, :], start=False, stop=True),
                         reads=[WT[b].name, va[b].name], writes=["pN"], track=(h == 3))
                pNv = pN[:].rearrange("p a (c x) -> p (a c) x", c=2)
                if aug:
                    P.op("act", lambda e: e.activation(out=dn[:, :, 0:1], in_=pNv[:, :, DVO:DVO + 1], func=AF.Abs), reads=["pN"], writes=[dn.name])
                    P.op("dve", lambda e: e.tensor_single_scalar(out=dn[:, :, 0:1], in_=dn[:, :, 0:1], scalar=1.0, op=ALU.max), reads=[dn.name], writes=[dn.name])
                    P.op("dve", lambda e: e.reciprocal(out=dn[:, :, 1:2], in_=dn[:, :, 0:1]), reads=[dn.name], writes=[dn.name])
                    P.op("dve", lambda e, b=b: e.tensor_tensor(out=ho[b][:], in0=pNv[:, :, 0:DVO], in1=dn[:, :, 1:2].to_broadcast([128, 4, DVO]), op=ALU.mult),
                         reads=["pN", dn.name], writes=[ho[b].name])
                else:
                    P.op("act", lambda e, b=b: e.copy(out=ho[b][:], in_=pNv[:, :, 0:DVO]), reads=["pN"], writes=[ho[b].name])
                P.dma(HO[d][tile], ho[b][:], reads=[ho[b].name], writes=["%s.ho%d.%d" % (key, d, tile)])
                for h in range(4):
                    o = pD[:, h // 2, (h % 2) * 256:(h % 2) * 256 + DV]
                    P.op("pe", lambda e, h=h, b=b, o=o: e.matmul(o, lhsT=kt[b][:, h, :], rhs=va[b][:, h, :], start=True, stop=True),
                         reads=[kt[b].name, va[b].name], writes=["pD"], track=(h == 3))
                pDv = pD[:].rearrange("p a (c x) -> p (a c) x", c=2)
                P.op("dve", lambda e, d=d: e.tensor_tensor(out=tmp[:], in0=pDv[:, :, 0:DV], in1=St[d][:], op=ALU.add), reads=["pD", St[d].name], writes=[tmp.name])
                P.op("dve", lambda e, d=d, b=b: e.tensor_tensor(out=St[d][:], in0=tmp[:], in1=et[b][:].unsqueeze(2).to_broadcast([128, 4, DV]), op=ALU.mult),
                     reads=[tmp.name, et[b].name], writes=[St[d].name])
                P.op("act", lambda e, d=d: e.copy(out=Sb[d][:], in_=St[d][:]), reads=[St[d].name], writes=[Sb[d].name])
    C.P.barrier()


def load_w_bf16(C, sbf, w_d, ncols, key):
    for dc in range(8):
        C.P.dma(sbf[:, dc, :], w_d[dc * 128:(dc + 1) * 128, :], writes=["%s.%d" % (key, dc)], q="pool")


def emit_in_proj(C, es, tile_src, mod_rows, wbf, wkey, ncols, Pj, xt, hl, hT, pbanks, modr):
    nc, P = C.nc, C.P
    src_ap, src_key = tile_src
    P.dma(xt[:], src_ap, reads=[src_key], writes=[xt.name])
    P.op("dve", lambda e: e.tensor_tensor(out=hl[:], in0=xt[:], in1=modr[:, mod_rows[0], :], op=ALU.mult), reads=[xt.name, modr.name + ".%d" % mod_rows[0]], writes=[hl.name])
    P.op("dve", lambda e: e.tensor_tensor(out=hl[:], in0=hl[:], in1=modr[:, mod_rows[1], :], op=ALU.add), reads=[hl.name, modr.name + ".%d" % mod_rows[1]], writes=[hl.name])
    for dc in range(8):
        pb = pbanks[dc // 4]
        P.op("pe", lambda e, dc=dc, pb=pb: e.transpose(out=pb[:, (dc % 4) * 128:(dc % 4 + 1) * 128], in_=hl[:, dc * 128:(dc + 1) * 128], identity=C.ident[:]),
             reads=[hl.name, "c_ident"], writes=[pb.name], track=(dc % 4 == 3))
    for hb in range(2):
        P.op("act", lambda e, hb=hb: e.copy(out=hT[:, hb * 4:(hb + 1) * 4, :], in_=pbanks[hb][:].rearrange("p (c t) -> p c t", c=4)),
             reads=[pbanks[hb].name], writes=[hT.name])
    nch = (ncols + 511) // 512
    for n in range(nch):
        c0, c1 = n * 512, min(ncols, (n + 1) * 512)
        pb = pbanks[2 + n % (len(pbanks) - 2)]
        for dc in range(8):
            P.op("pe", lambda e, dc=dc, pb=pb, c0=c0, c1=c1: e.matmul(pb[:, 0:c1 - c0], lhsT=hT[:, dc, :], rhs=wbf[:, dc, c0:c1], start=(dc == 0), stop=(dc == 7)),
                 reads=[hT.name, "%s.%d" % (wkey, dc)], writes=[pb.name], track=(dc == 7))
        P.op("act", lambda e, pb=pb, c0=c0, c1=c1: e.copy(out=Pj[:, c0:c1], in_=pb[:, 0:c1 - c0]), reads=[pb.name], writes=[Pj.name])


def emit_decay_prep(C, sbs, LFv, LIv, lkey, ncol, cmats, pcum, with_li):
    nc, P = C.nc, C.P
    triu, tril, ones = cmats
    CUM, TOT = sbs
    for d in range(2):
        P.op("pe", lambda e, d=d: e.matmul(pcum[:, d * ncol:(d + 1) * ncol], lhsT=(triu if d == 0 else tril)[:], rhs=LFv[:, d * ncol:(d + 1) * ncol], start=True, stop=True),
             reads=[lkey, "c_tri"], writes=[pcum.name])
    P.op("act", lambda e: e.copy(out=CUM[:], in_=pcum[:, 0:2 * ncol]), reads=[pcum.name], writes=[CUM.name])
    P.op("pe", lambda e: e.matmul(pcum[:, 0:2 * ncol], lhsT=ones[:], rhs=LFv[:, 0:2 * ncol], start=True, stop=True), reads=[lkey, "c_tri"], writes=[pcum.name])
    P.op("act", lambda e: e.activation(out=TOT[:], in_=pcum[:, 0:2 * ncol], func=AF.Exp), reads=[pcum.name], writes=[TOT.name])


def emit_ab_stage_a(C, NL, x_d, xkey, ctx_d, ckey, modd, w_in_d, gate_b_d, rope_d, S):
    nc, P = C.nc, C.P
    NT = 2 + NL
    with ExitStack() as es:
        def sb(name, shape, dt):
            return es.enter_context(nc.sbuf_tensor(C.name(name), shape, dt))

        def ps(name, shape, dt=F32):
            return es.enter_context(nc.psum_tensor(C.name(name), shape, dt))
        NC = 2832
        wbf = sb("w_in", [128, 8, NC], BF16)
        load_w_bf16(C, wbf, w_in_d, NC, wbf.name)
        modr = sb("modr", [128, 4, D], F32)
        for r, (row, k) in enumerate(((0, 1), (0, 0), (1, 1), (1, 0))):
            P.dma(modr[:, r, :], modd[row, k].partition_broadcast(128), reads=["modd"], writes=[modr.name + ".%d" % r])
        gb = sb("gb", [128, 16], F32)
        P.dma(gb[:], gate_b_d.partition_broadcast(128), writes=[gb.name])
        tri = sb("tri", [128, 3, 128], F32)
        P.dma(tri[:], C.consts_d[2:5].rearrange("k p n -> p k n"), writes=["c_tri"])
        cm = (tri[:, 0, :], tri[:, 1, :], tri[:, 2, :])
        cmats = (V(cm[0], "c_tri"), V(cm[1], "c_tri"), V(cm[2], "c_tri"))
        pb = [ps("pb%d" % i, [128, 512]) for i in range(7)]
        pcum = ps("pcum", [128, 512])
        xt = sb("xt", [128, D], F32)
        hl = sb("hl", [128, D], F32)
        hT = sb("hT", [128, 8, 128], BF16)
        Pj = sb("Pj", [128, NC], F32)
        G16 = sb("G16", [128, 16], F32)
        LF = sb("LF", [128, 8], F32)
        LI = sb("LI", [128, 8], F32)
        CUM = sb("CUM", [128, 8], F32)
        TOT = sb("TOT", [128, 8], F32)
        EB = sb("EB", [128, 8], F32)
        EA = sb("EA", [128, 8], F32)
        qk = sb("qk", [128, 2, 2, 4, 128], F32)
        qkT = sb("qkT", [128, 16, 128], BF16)
        ktb = sb("ktb", [128, 8, 128], BF16)
        vab = sb("vab", [128, 4, 129], BF16)
        P.op("dve", lambda e: e.memset(vab[:], 1.0), writes=[vab.name])
        rope = sb("rope", [128, 2, 32], F32)
        qa = sb("qa", [128, 10, 64], F32)
        rt = sb("rt", [128, 4, 10, 32], F32)
        qaT = sb("qaT", [128, 5, 128], BF16)
        vaa = sb("vaa", [128, 2, 65], BF16)
        P.op("dve", lambda e: e.memset(vaa[:], 1.0), writes=[vaa.name])
        for tile in range(NT):
            is_ctx = tile < 2
            if is_ctx:
                src = (ctx_d[tile * 128:(tile + 1) * 128, :], "%s.%d" % (ckey, tile))
            else:
                src = (x_d[(tile - 2) * 128:(tile - 1) * 128, :], "%s.%d" % (xkey, tile - 2))
            emit_in_proj(C, es, src, (2, 3) if is_ctx else (0, 1), wbf, wbf.name, NC, Pj, xt, hl, hT, pb, modr)
            tk = "ab.%d" % tile
            P.op("dve", lambda e: e.tensor_tensor(out=G16[:], in0=Pj[:, 2048:2064], in1=gb[:], op=ALU.add), reads=[Pj.name, gb.name], writes=[G16.name])
            gv = G16[:].rearrange("p (d g h) -> p d g h", d=2, g=2)
            P.op("dve", lambda e: e.tensor_copy(out=LI[:].rearrange("p (d h) -> p d h", d=2), in_=gv[:, :, 0, :]), reads=[G16.name], writes=[LI.name])
            P.op("act", lambda e: e.activation(out=LF[:].rearrange("p (d h) -> p d h", d=2), in_=gv[:, :, 1, :], func=AF.Exp, scale=-1.0), reads=[G16.name], writes=[LF.name])
            P.op("dve", lambda e: e.tensor_single_scalar(out=LF[:], in_=LF[:], scalar=1.0, op=ALU.add), reads=[LF.name], writes=[LF.name])
            P.op("act", lambda e: e.activation(out=LF[:], in_=LF[:], func=AF.Ln), reads=[LF.name], writes=[LF.name])
            P.op("dve", lambda e: e.tensor_single_scalar(out=LF[:], in_=LF[:], scalar=-1.0, op=ALU.mult), reads=[LF.name], writes=[LF.name])
            emit_decay_prep(C, (CUM, TOT), LF[:], LI[:], LF.name, 4, cmats, pcum, True)
            P.dma(S["ET"][tile], TOT[:], reads=[TOT.name], writes=[tk + ".et"])
            P.op("act", lambda e: e.activation(out=EB[:], in_=CUM[:], func=AF.Exp), reads=[CUM.name], writes=[EB.name])
            P.op("dve", lambda e: e.tensor_single_scalar(out=EB[:], in_=EB[:], scalar=128.0 ** -0.5, op=ALU.mult), reads=[EB.name], writes=[EB.name])
            P.op("dve", lambda e: e.tensor_tensor(out=EA[:], in0=LI[:], in1=CUM[:], op=ALU.subtract), reads=[LI.name, CUM.name], writes=[EA.name])
            P.op("act", lambda e: e.activation(out=EA[:], in_=EA[:], func=AF.Exp), reads=[EA.name], writes=[EA.name])
            for w, (E_, c0) in enumerate(((EB, 0), (EA, 512))):
                for d in range(2):
                    P.op("dve" if d == 0 else "pool", lambda e, w=w, d=d, E_=E_, c0=c0: e.tensor_tensor(
                        out=qk[:, w, d], in0=Pj[:, c0:c0 + 512].rearrange("p (h k) -> p h k", h=4),
                        in1=E_[:, d * 4:(d + 1) * 4].unsqueeze(2).to_broadcast([128, 4, 128]), op=ALU.mult),
                        reads=[Pj.name, E_.name], writes=[qk.name + ".%d%d" % (w, d)])
            for s in range(16):
                w, d, h = s // 8, (s // 4) % 2, s % 4
                pp = pb[s // 4]
                P.op("pe", lambda e, s=s, w=w, d=d, h=h, pp=pp: e.transpose(out=pp[:, (s % 4) * 128:(s % 4 + 1) * 128], in_=qk[:, w, d, h, :], identity=C.ident[:]),
                     reads=[qk.name + ".%d%d" % (w, d), "c_ident"], writes=[pp.name], track=(s % 4 == 3))
            for g in range(4):
                P.op("act" if g % 2 == 0 else "dve", lambda e, g=g: (e.copy if g % 2 == 0 else e.tensor_copy)(out=qkT[:, g * 4:(g + 1) * 4, :], in_=pb[g][:].rearrange("p (c t) -> p c t", c=4)),
                     reads=[pb[g].name], writes=[qkT.name + ".%d" % g])
            P.dma(S["QKT"][tile], qkT[:], reads=[qkT.name + ".%d" % g for g in range(4)], writes=[tk + ".qkt"])
            P.op("pool", lambda e: e.tensor_copy(out=ktb[:], in_=qk[:, 1].rearrange("p d h k -> p (d h) k")), reads=[qk.name + ".10", qk.name + ".11"], writes=[ktb.name])
            P.dma(S["KT"][tile], ktb[:], reads=[ktb.name], writes=[tk + ".kt"])
            P.op("pool", lambda e: e.tensor_copy(out=vab[:, :, 0:128], in_=Pj[:, 1024:1536].rearrange("p (h k) -> p h k", h=4)), reads=[Pj.name], writes=[vab.name])
            P.dma(S["VA"][tile], vab[:], reads=[vab.name], writes=[tk + ".va"])
            P.dma(S["OM"][tile], Pj[:, 1536:2048], reads=[Pj.name], writes=[tk + ".om"])
            qsrc = Pj[:, 2064:2704].rearrange("p (h k) -> p h k", h=10)
            qperm = qa[:, 0:8, :].rearrange("p (j a) k -> p a j k", a=2)
            if is_ctx:
                P.op("dve", lambda e: e.tensor_copy(out=qperm, in_=qsrc[:, 0:8, :].rearrange("p (a j) k -> p a j k", a=2)), reads=[Pj.name], writes=[qa.name])
                P.op("dve", lambda e: e.tensor_copy(out=qa[:, 8:10, :], in_=qsrc[:, 8:10, :]), reads=[Pj.name], writes=[qa.name])
            else:
                P.dma(rope[:], rope_d[tile - 2], writes=[rope.name])
                x1 = qsrc.rearrange("p h (i two) -> p h i two", two=2)[:, :, :, 0]
                x2 = qsrc.rearrange("p h (i two) -> p h i two", two=2)[:, :, :, 1]
                cs = rope[:, 0, :].unsqueeze(1).to_broadcast([128, 10, 32])
                sn = rope[:, 1, :].unsqueeze(1).to_broadcast([128, 10, 32])
                qo = qa[:].rearrange("p h (i two) -> p h i two", two=2)
                for j, (xa, tb_) in enumerate(((x1, cs), (x2, sn), (x1, sn), (x2, cs))):
                    P.op("dve" if j % 2 == 0 else "pool", lambda e, j=j, xa=xa, tb_=tb_: e.tensor_tensor(out=rt[:, j], in0=xa, in1=tb_, op=ALU.mult),
                         reads=[Pj.name, rope.name], writes=[rt.name + ".%d" % j])
                qpo = qperm.rearrange("p a j (i two) -> p a j i two", two=2)
                for two, (ra, rb, op_) in enumerate(((0, 1, ALU.subtract), (2, 3, ALU.add))):
                    P.op("dve", lambda e, two=two, ra=ra, rb=rb, op_=op_: e.tensor_tensor(out=qpo[:, :, :, :, two], in0=rt[:, ra, 0:8].rearrange("p (a j) i -> p a j i", a=2),
                                                                                  in1=rt[:, rb, 0:8].rearrange("p (a j) i -> p a j i", a=2), op=op_),
                         reads=[rt.name + ".%d" % ra, rt.name + ".%d" % rb], writes=[qa.name])
                    P.op("dve", lambda e, two=two, ra=ra, rb=rb, op_=op_: e.tensor_tensor(out=qo[:, 8:10, :, two], in0=rt[:, ra, 8:10], in1=rt[:, rb, 8:10], op=op_),
                         reads=[rt.name + ".%d" % ra, rt.name + ".%d" % rb], writes=[qa.name])
            pp = pb[4]
            pq = pb[5]
            for j in range(4):
                P.op("pe", lambda e, j=j, pp=pp: e.transpose(out=pp[:, j * 128:(j + 1) * 128], in_=qa[:].rearrange("p h k -> p (h k)")[:, j * 128:(j + 1) * 128], identity=C.ident[:]),
                     reads=[qa.name, "c_ident"], writes=[pp.name], track=(j == 3))
            P.op("pe", lambda e, pq=pq: e.transpose(out=pq[:, 0:128], in_=qa[:].rearrange("p h k -> p (h k)")[:, 512:640], identity=C.ident[:]), reads=[qa.name, "c_ident"], writes=[pq.name])
            P.op("act", lambda e, pp=pp: e.copy(out=qaT[:, 0:4, :], in_=pp[:].rearrange("p (c t) -> p c t", c=4)), reads=[pp.name], writes=[qaT.name])
            P.op("act", lambda e, pq=pq: e.copy(out=qaT[:, 4, :], in_=pq[:, 0:128]), reads=[pq.name], writes=[qaT.name])
            P.dma(S["QAT"][tile], qaT[:], reads=[qaT.name], writes=[tk + ".qat"])
            P.op("pool", lambda e: e.tensor_copy(out=vaa[:, :, 0:64], in_=Pj[:, 2704:2832].rearrange("p (h k) -> p h k", h=2)), reads=[Pj.name], writes=[vaa.name])
            P.dma(S["VAA"][tile], vaa[:], reads=[vaa.name], writes=[tk + ".vaa"])
    C.P.barrier()


def emit_ab_attn(C, NL, S, sink_d, AO, aokey):
    nc, P = C.nc, C.P
    NT = 2 + NL
    with ExitStack() as es:
        def sb(name, shape, dt):
            return es.enter_context(nc.sbuf_tensor(C.name(name), shape, dt))

        def ps(name, shape, dt=F32):
            return es.enter_context(nc.psum_tensor(C.name(name), shape, dt))
        kT = sb("kT_all", [128, NT, 128], BF16)
        va = sb("va_all", [128, NT, 2, 65], BF16)
        for t in range(NT):
            P.dma(kT[:, t, :], S["QAT"][t][:, 4, :], reads=["ab.%d.qat" % t], writes=[kT.name + ".%d" % t])
            P.dma(va[:, t], S["VAA"][t], reads=["ab.%d.vaa" % t], writes=[va.name + ".%d" % t])
        mk = sb("mk", [128, 2, 128], F32)
        P.dma(mk[:], C.consts_d[2:4].rearrange("k p n -> p k n"), writes=[mk.name])
        mkb = sb("mkb", [128, 2, 128], BF16)
        P.op("dve", lambda e: e.tensor_copy(out=mkb[:], in_=mk[:]), reads=[mk.name], writes=[mkb.name])
        sk = sb("sink", [128, 8], F32)
        P.dma(sk[:], sink_d.partition_broadcast(128), writes=[sk.name])
        P.op("act", lambda e: e.activation(out=sk[:], in_=sk[:], func=AF.Exp), reads=[sk.name], writes=[sk.name])
        qt = [sb("qt%d" % i, [128, 4, 128], BF16) for i in range(2)]
        E = [sb("E%d" % i, [128, 5, 128], BF16) for i in range(2)]
        pE = [ps("pE%d" % i, [128, 1024]) for i in range(2)]
        pO = ps("pO", [128, 2, 512])
        den = sb("den", [128, 8, 2], F32)
        ao = [sb("ao%d" % i, [128, 8, 64], F32) for i in range(2)]
        for qtile in range(NT):
            if qtile < 2:
                blocks = [(0, None), (1, None)]
            else:
                n = qtile - 2
                blocks = [(0, None), (1, None)]
                if n >= 1:
                    blocks.append((qtile - 1, 1))
                blocks.append((qtile, None))
                if n + 1 < NL:
                    blocks.append((qtile + 1, 0))
            nb = len(blocks)
            q = qt[qtile % 2]
            P.dma(q[:], S["QAT"][qtile][:, 0:4, :], reads=["ab.%d.qat" % qtile], writes=[q.name])
            for hq in range(8):
                j, half = hq % 4, hq // 4
                p0, p1 = half * 64, (half + 1) * 64
                pe_ = pE[hq % 2]
                Eb = E[hq % 2]
                for bi, (blk, _) in enumerate(blocks):
                    P.op("pe", lambda e, bi=bi, blk=blk, p0=p0, p1=p1, j=j, pe_=pe_, q=q: e.matmul(pe_[:, bi * 128:(bi + 1) * 128], lhsT=kT[p0:p1, blk, :], rhs=q[p0:p1, j, :], start=True, stop=True),
                         reads=[kT.name + ".%d" % blk, q.name], writes=[pe_.name], track=(bi == nb - 1))
                P.op("act", lambda e, pe_=pe_, Eb=Eb, nb=nb: e.activation(out=Eb[:, 0:nb, :], in_=pe_[:, 0:nb * 128].rearrange("p (b t) -> p b t", b=nb), func=AF.Exp, scale=0.125),
                     reads=[pe_.name], writes=[Eb.name])
                for bi, (blk, m) in enumerate(blocks):
                    if m is not None:
                        P.op("dve" if m == 0 else "pool", lambda e, bi=bi, m=m, Eb=Eb: e.tensor_tensor(out=Eb[:, bi, :], in0=Eb[:, bi, :], in1=mkb[:, m, :], op=ALU.mult),
                             reads=[Eb.name, mkb.name], writes=[Eb.name])
                o = pO[:, hq // 4, (hq % 4) * 65:(hq % 4) * 65 + 65]
                for bi, (blk, _) in enumerate(blocks):
                    P.op("pe", lambda e, bi=bi, blk=blk, half=half, Eb=Eb, o=o: e.matmul(o, lhsT=Eb[:, bi, :], rhs=va[:, blk, half, :], start=(bi == 0), stop=(bi == nb - 1)),
                         reads=[Eb.name, va.name + ".%d" % blk], writes=["pO"], track=(bi == nb - 1))
            pv = pO[:, :, 0:260].rearrange("p a (h x) -> p a h x", h=4)
            a_ = ao[qtile % 2]
            dv = den[:].rearrange("p (a h) x -> p a h x", a=2)
            P.op("dve", lambda e: e.tensor_tensor(out=dv[:, :, :, 0:1], in0=pv[:, :, :, 64:65], in1=sk[:].rearrange("p (a h) -> p a h", a=2).unsqueeze(3), op=ALU.add),
                 reads=["pO", sk.name], writes=[den.name])
            P.op("dve", lambda e: e.reciprocal(out=den[:, :, 1:2], in_=den[:, :, 0:1]), reads=[den.name], writes=[den.name])
            P.op("dve", lambda e, a_=a_: e.tensor_tensor(out=a_[:].rearrange("p (a h) x -> p a h x", a=2), in0=pv[:, :, :, 0:64],
                                                         in1=dv[:, :, :, 1:2].to_broadcast([128, 2, 4, 64]), op=ALU.mult),
                 reads=["pO", den.name], writes=[a_.name])
            P.dma(AO[qtile], a_[:].rearrange("p h x -> p (h x)"), reads=[a_.name], writes=["%s.%d" % (aokey, qtile)])
    C.P.barrier()


def head_norm_rows(C, hm, sq, stt, nheads, dh, hk):
    P = C.P
    P.op("pool", lambda e: e.tensor_tensor(out=sq[:], in0=hm[:], in1=hm[:], op=ALU.mult), reads=[hk], writes=[sq.name])
    P.op("dve", lambda e: e.reduce_sum(out=stt[:, 0:nheads], in_=sq[:], axis=AX.X), reads=[sq.name], writes=[stt.name])
    P.op("dve", lambda e: e.tensor_scalar(out=stt[:, 0:nheads], in0=stt[:, 0:nheads], scalar1=1.0 / dh, scalar2=LN_EPS, op0=ALU.mult, op1=ALU.add), reads=[stt.name], writes=[stt.name])
    P.op("act", lambda e: e.activation(out=stt[:, 0:nheads], in_=stt[:, 0:nheads], func=AF.Sqrt), reads=[stt.name], writes=[stt.name])
    P.op("dve", lambda e: e.reciprocal(out=stt[:, 0:nheads], in_=stt[:, 0:nheads]), reads=[stt.name], writes=[stt.name])
    P.op("dve", lambda e: e.tensor_tensor(out=hm[:], in0=hm[:], in1=stt[:, 0:nheads].unsqueeze(2).to_broadcast([128, nheads, dh]), op=ALU.mult), reads=[hk, stt.name], writes=[hk])


def emit_merge(C, NL, n_ctx_out, kind, S, HO, hokey, AO, aokey, x_d, xkey, ctx_d, ckey, modd, norm_w_d, w_out_d, lnw_d, lnb_d, x1_d, x1key, c1_d, c1key):
    nc, P = C.nc, C.P
    NT = 2 + NL
    with ExitStack() as es:
        def sb(name, shape, dt):
            return es.enter_context(nc.sbuf_tensor(C.name(name), shape, dt))

        def ps(name, shape, dt=F32):
            return es.enter_context(nc.psum_tensor(C.name(name), shape, dt))
        wbf = sb("w_out", [128, 8, D], BF16)
        load_w_bf16(C, wbf, w_out_d, D, wbf.name)
        nw = D // 2 if kind == "ab" else D
        normw = sb("normw", [128, nw], F32)
        P.dma(normw[:], norm_w_d.partition_broadcast(128), writes=[normw.name])
        g1 = sb("g1", [128, 2, D], F32)
        for r in range(2):
            P.dma(g1[:, r, :], modd[r, 2].partition_broadcast(128), reads=["modd"], writes=[g1.name + ".%d" % r])
        lnw = sb("lnw", [128, D], F32)
        lnb = sb("lnb", [128, D], F32)
        P.dma(lnw[:], lnw_d.partition_broadcast(128), writes=[lnw.name])
        P.dma(lnb[:], lnb_d.partition_broadcast(128), writes=[lnb.name])
        h0 = sb("h0", [128, nw], F32)
        h1 = sb("h1", [128, nw], F32)
        sq = sb("sq", [128, nw], F32)
        gt = sb("gt", [128, nw], F32)
        cat = sb("cat", [128, D], F32)
        catT = sb("catT", [128, 8, 128], BF16)
        xt = sb("xt", [128, D], F32)
        ytmp = sb("ytmp", [128, D], F32)
        yo = sb("yo", [128, D], F32)
        stt = sb("stt", [128, 4], F32)
        st = sb("st", [128, 4], F32)
        lt = sb("lt", [128, D], F32)
        pb = [ps("pbm%d" % i, [128, 1024]) for i in range(2)]
        first = 0 if n_ctx_out else 2
        for tile in range(first, NT):
            is_ctx = tile < 2
            P.dma(h0[:], HO[0][tile].rearrange("p h x -> p (h x)"), reads=["%s.ho0.%d" % (hokey, tile)], writes=[h0.name])
            P.dma(h1[:], HO[1][tile].rearrange("p h x -> p (h x)"), reads=["%s.ho1.%d" % (hokey, tile)], writes=[h1.name])
            P.dma(gt[:], S["OM"][tile], reads=["%s.%d.om" % (kind, tile)], writes=[gt.name])
            P.op("dve", lambda e: e.tensor_tensor(out=h0[:], in0=h0[:], in1=h1[:], op=ALU.add), reads=[h0.name, h1.name], writes=[h0.name])
            dh = nw // 4
            hv = V(h0[:].rearrange("p (h x) -> p h x", h=4), h0.name)
            sv = V(sq[:].rearrange("p (h x) -> p h x", h=4), sq.name)
            head_norm_rows(C, hv, sv, stt, 4, dh, h0.name)
            P.op("dve", lambda e: e.tensor_tensor(out=h0[:], in0=h0[:], in1=normw[:], op=ALU.mult), reads=[h0.name, normw.name], writes=[h0.name])
            P.op("act", lambda e: e.activation(out=gt[:], in_=gt[:], func=(AF.Sigmoid if kind == "ab" else AF.Silu)), reads=[gt.name], writes=[gt.name])
            P.op("dve", lambda e: e.tensor_tensor(out=cat[:, 0:nw], in0=h0[:], in1=gt[:], op=ALU.mult), reads=[h0.name, gt.name], writes=[cat.name + ".0"])
            rk = [cat.name + ".0"]
            if kind == "ab":
                P.dma(cat[:, nw:D], AO[tile], reads=["%s.%d" % (aokey, tile)], writes=[cat.name + ".1"])
                rk.append(cat.name + ".1")
            for dc in range(8):
                P.op("pe", lambda e, dc=dc: e.transpose(out=pb[0][:, dc * 128:(dc + 1) * 128], in_=cat[:, dc * 128:(dc + 1) * 128], identity=C.ident[:]),
                     reads=rk + ["c_ident"], writes=[pb[0].name], track=(dc == 7))
            P.op("act", lambda e: e.copy(out=catT[:], in_=pb[0][:].rearrange("p (c t) -> p c t", c=8)), reads=[pb[0].name], writes=[catT.name])
            for hf in range(2):
                for dc in range(8):
                    P.op("pe", lambda e, dc=dc, hf=hf: e.matmul(pb[1][:, hf * 512:(hf + 1) * 512], lhsT=catT[:, dc, :], rhs=wbf[:, dc, hf * 512:(hf + 1) * 512], start=(dc == 0), stop=(dc == 7)),
                         reads=[catT.name, "%s.%d" % (wbf.name, dc)], writes=[pb[1].name], track=(dc == 7 and hf == 1))
            if is_ctx:
                src, skey, dst, dkey = ctx_d[tile * 128:(tile + 1) * 128, :], "%s.%d" % (ckey, tile), c1_d[tile * 128:(tile + 1) * 128, :], "%s.%d" % (c1key, tile)
            else:
                src, skey, dst, dkey = x_d[(tile - 2) * 128:(tile - 1) * 128, :], "%s.%d" % (xkey, tile - 2), x1_d[(tile - 2) * 128:(tile - 1) * 128, :], "%s.%d" % (x1key, tile - 2)
            r = 1 if is_ctx else 0
            P.dma(xt[:], src, reads=[skey], writes=[xt.name])
            P.op("dve", lambda e, r=r: e.tensor_tensor(out=ytmp[:], in0=pb[1][:], in1=g1[:, r, :], op=ALU.mult), reads=[pb[1].name, g1.name + ".%d" % r], writes=[ytmp.name])
            P.op("dve", lambda e: e.scalar_tensor_tensor(out=ytmp[:], in0=xt[:], scalar=ALPHA, in1=ytmp[:], op0=ALU.mult, op1=ALU.add), reads=[xt.name, ytmp.name], writes=[ytmp.name])
            layernorm_rows(P, nc, ytmp, None, lt, st, lnw, lnb, "m", yo)
            P.dma(dst, yo[:], reads=[yo.name], writes=[dkey])
    C.P.barrier()


def make_consts():
    i = np.arange(128)
    return np.stack([np.eye(128), np.tile(i.astype(np.float64), (128, 1)), (i[:, None] <= i[None, :]), (i[:, None] >= i[None, :]), np.ones((128, 128))]).astype(np.float32)


def make_rope(seq):
    rows = seq // 64
    row = np.repeat(np.arange(rows), 64).astype(np.float32)
    col = np.tile(np.arange(64), rows).astype(np.float32)
    inv = (10000.0 ** (-np.arange(16, dtype=np.float32) / 16)).astype(np.float32)
    ang = np.concatenate([row[:, None] * inv, col[:, None] * inv], -1).astype(np.float32)
    r = np.stack([np.cos(ang), np.sin(ang)], 1).astype(np.float32)
    return np.ascontiguousarray(r.reshape(seq // 128, 128, 2, 32))


def ab_scratch(nc, NT, pfx):
    S = {}
    S["QKT"] = nc.dram_tensor(pfx + "QKT", [NT, 128, 16, 128], BF16).ap()
    S["KT"] = nc.dram_tensor(pfx + "KT", [NT, 128, 8, 128], BF16).ap()
    S["VA"] = nc.dram_tensor(pfx + "VA", [NT, 128, 4, 129], BF16).ap()
    S["ET"] = nc.dram_tensor(pfx + "ET", [NT, 128, 8], F32).ap()
    S["OM"] = nc.dram_tensor(pfx + "OM", [NT, 128, 512], F32).ap()
    S["QAT"] = nc.dram_tensor(pfx + "QAT", [NT, 128, 5, 128], BF16).ap()
    S["VAA"] = nc.dram_tensor(pfx + "VAA", [NT, 128, 2, 65], BF16).ap()
    S["HO"] = [nc.dram_tensor(pfx + "HO%d" % d, [NT, 128, 4, 128], F32).ap() for d in range(2)]
    S["AO"] = nc.dram_tensor(pfx + "AO", [NT, 128, 512], F32).ap()
    return S


def emit_layer0_mixer(C, NL, x_d, xkey, ctx_d, ckey, modd, W, x1_d, x1key, c1_d, c1key):
    nc = C.nc
    NT = 2 + NL
    S = ab_scratch(nc, NT, "ab_")
    emit_ab_stage_a(C, NL, x_d, xkey, ctx_d, ckey, modd, W["ab_w_in"], W["ab_gate_b"], W["rope"], S)
    emit_scan(C, NT, 2, S["QKT"], S["KT"], S["VA"], S["ET"], S["HO"], 129, True, C.consts_d[2:4], "abs")
    emit_ab_attn(C, NL, S, W["ab_sink"], S["AO"], "ab.ao")
    emit_merge(C, NL, True, "ab", S, S["HO"], "abs", S["AO"], "ab.ao", x_d, xkey, ctx_d, ckey, modd, W["ab_norm_w"], W["ab_w_out"], W["lnw0"], W["lnb0"], x1_d, x1key, c1_d, c1key)


def emit_c_stage_a(C, NL, x_d, xkey, ctx_d, ckey, modd, w_in_d, gate_up_d, gate_b_d, S):
    nc, P = C.nc, C.P
    NT = 2 + NL
    with ExitStack() as es:
        def sb(name, shape, dt):
            return es.enter_context(nc.sbuf_tensor(C.name(name), shape, dt))

        def ps(name, shape, dt=F32):
            return es.enter_context(nc.psum_tensor(C.name(name), shape, dt))
        NC = 3104
        wbf = sb("w_in", [128, 8, NC], BF16)
        load_w_bf16(C, wbf, w_in_d, NC, wbf.name)
        modr = sb("modr", [128, 4, D], F32)
        for r, (row, k) in enumerate(((0, 1), (0, 0), (1, 1), (1, 0))):
            P.dma(modr[:, r, :], modd[row, k].partition_broadcast(128), reads=["modd"], writes=[modr.name + ".%d" % r])
        gb = sb("gb", [128, 1024], F32)
        P.dma(gb[:], gate_b_d.partition_broadcast(128), writes=[gb.name])
        gup = sb("gup", [16, 2, 512], F32)
        P.dma(gup[:], gate_up_d.rearrange("d r c -> r d c"), writes=[gup.name])
        tri = sb("tri", [128, 3, 128], F32)
        P.dma(tri[:], C.consts_d[2:5].rearrange("k p n -> p k n"), writes=["c_tri"])
        pb = [ps("pb%d" % i, [128, 512]) for i in range(7)]
        pcum = ps("pcum", [128, 512])
        xt = sb("xt", [128, D], F32)
        hl = sb("hl", [128, D], F32)
        hT = sb("hT", [128, 8, 128], BF16)
        Pj = sb("Pj", [128, NC], F32)
        lowT = sb("lowT", [16, 2, 128], F32)
        LA = sb("LA", [128, 1024], F32)
        CUM = sb("CUM", [128, 1024], F32)
        EB = sb("EB", [128, 1024], F32)
        EA = sb("EA", [128, 1024], F32)
        ETt = sb("ETt", [128, 8], F32)
        qk = sb("qk", [128, 2, 2, 4, 128], F32)
        qkT = sb("qkT", [128, 16, 128], BF16)
        ktb = sb("ktb", [128, 8, 128], BF16)
        vab = sb("vab", [128, 4, 256], BF16)
        for tile in range(NT):
            is_ctx = tile < 2
            if is_ctx:
                src = (ctx_d[tile * 128:(tile + 1) * 128, :], "%s.%d" % (ckey, tile))
            else:
                src = (x_d[(tile - 2) * 128:(tile - 1) * 128, :], "%s.%d" % (xkey, tile - 2))
            emit_in_proj(C, es, src, (2, 3) if is_ctx else (0, 1), wbf, wbf.name, NC, Pj, xt, hl, hT, pb, modr)
            tk = "c.%d" % tile
            for d in range(2):
                P.op("pe", lambda e, d=d: e.transpose(out=pcum[0:16, d * 128:(d + 1) * 128], in_=Pj[:, 3072 + 16 * d:3072 + 16 * (d + 1)], identity=C.ident[:]),
                     reads=[Pj.name, "c_ident"], writes=[pcum.name], track=(d == 1))
            P.op("act", lambda e: e.copy(out=lowT[:], in_=pcum[0:16, 0:256].rearrange("p (d t) -> p d t", d=2)), reads=[pcum.name], writes=[lowT.name])
            for d in range(2):
                P.op("pe", lambda e, d=d: e.matmul(pb[d][:, :], lhsT=lowT[:, d, :], rhs=gup[:, d, :], start=True, stop=True), reads=[lowT.name, gup.name], writes=[pb[d].name])
                P.op("dve", lambda e, d=d: e.tensor_tensor(out=LA[:, d * 512:(d + 1) * 512], in0=pb[d][:, :], in1=gb[:, d * 512:(d + 1) * 512], op=ALU.add),
                     reads=[pb[d].name, gb.name], writes=[LA.name])
            P.op("act", lambda e: e.activation(out=LA[:], in_=LA[:], func=AF.Exp, scale=-1.0), reads=[LA.name], writes=[LA.name])
            P.op("dve", lambda e: e.tensor_single_scalar(out=LA[:], in_=LA[:], scalar=1.0, op=ALU.add), reads=[LA.name], writes=[LA.name])
            P.op("act", lambda e: e.activation(out=LA[:], in_=LA[:], func=AF.Ln), reads=[LA.name], writes=[LA.name])
            P.op("dve", lambda e: e.tensor_single_scalar(out=LA[:], in_=LA[:], scalar=-1.0 / 16.0, op=ALU.mult), reads=[LA.name], writes=[LA.name])
            for d in range(2):
                P.op("pe", lambda e, d=d: e.matmul(pb[2 + d][:, :], lhsT=tri[:, d, :], rhs=LA[:, d * 512:(d + 1) * 512], start=True, stop=True), reads=[LA.name, "c_tri"], writes=[pb[2 + d].name])
                P.op("act", lambda e, d=d: e.copy(out=CUM[:, d * 512:(d + 1) * 512], in_=pb[2 + d][:, :]), reads=[pb[2 + d].name], writes=[CUM.name])
            for s in range(8):
                P.op("pe", lambda e, s=s: e.matmul(pcum[:, 256 + s:257 + s], lhsT=LA[:, s * 128:(s + 1) * 128], rhs=tri[:, 2, 0:1], start=True, stop=True),
                     reads=[LA.name, "c_tri"], writes=[pcum.name], track=(s == 7))
            P.op("act", lambda e: e.activation(out=ETt[:], in_=pcum[:, 256:264], func=AF.Exp), reads=[pcum.name], writes=[ETt.name])
            P.dma(S["ET"][tile], ETt[:], reads=[ETt.name], writes=[tk + ".et"])
            P.op("act", lambda e: e.activation(out=EB[:], in_=CUM[:], func=AF.Exp), reads=[CUM.name], writes=[EB.name])
            P.op("act", lambda e: e.activation(out=EA[:], in_=CUM[:], func=AF.Exp, scale=-1.0), reads=[CUM.name], writes=[EA.name])
            P.op("pool", lambda e: e.tensor_single_scalar(out=EB[:], in_=EB[:], scalar=128.0 ** -0.5, op=ALU.mult), reads=[EB.name], writes=[EB.name])
            for w, (E_, c0) in enumerate(((EB, 0), (EA, 512))):
                for d in range(2):
                    P.op("dve" if d == 0 else "pool", lambda e, w=w, d=d, E_=E_, c0=c0: e.tensor_tensor(
                        out=qk[:, w, d].rearrange("p h k -> p (h k)"), in0=Pj[:, c0:c0 + 512], in1=E_[:, d * 512:(d + 1) * 512], op=ALU.mult),
                        reads=[Pj.name, E_.name], writes=[qk.name + ".%d%d" % (w, d)])
            for s in range(16):
                w, d, h = s // 8, (s // 4) % 2, s % 4
                pp = pb[s // 4]
                P.op("pe", lambda e, s=s, w=w, d=d, h=h, pp=pp: e.transpose(out=pp[:, (s % 4) * 128:(s % 4 + 1) * 128], in_=qk[:, w, d, h, :], identity=C.ident[:]),
                     reads=[qk.name + ".%d%d" % (w, d), "c_ident"], writes=[pp.name], track=(s % 4 == 3))
            for g in range(4):
                P.op("act" if g % 2 == 0 else "dve", lambda e, g=g: (e.copy if g % 2 == 0 else e.tensor_copy)(out=qkT[:, g * 4:(g + 1) * 4, :], in_=pb[g][:].rearrange("p (c t) -> p c t", c=4)),
                     reads=[pb[g].name], writes=[qkT.name + ".%d" % g])
            P.dma(S["QKT"][tile], qkT[:], reads=[qkT.name + ".%d" % g for g in range(4)], writes=[tk + ".qkt"])
            P.op("pool", lambda e: e.tensor_copy(out=ktb[:], in_=qk[:, 1].rearrange("p d h k -> p (d h) k")), reads=[qk.name + ".10", qk.name + ".11"], writes=[ktb.name])
            P.dma(S["KT"][tile], ktb[:], reads=[ktb.name], writes=[tk + ".kt"])
            P.op("pool", lambda e: e.tensor_copy(out=vab[:], in_=Pj[:, 1024:2048].rearrange("p (h k) -> p h k", h=4)), reads=[Pj.name], writes=[vab.name])
            P.dma(S["VA"][tile], vab[:], reads=[vab.name], writes=[tk + ".va"])
            P.dma(S["OM"][tile], Pj[:, 2048:3072], reads=[Pj.name], writes=[tk + ".om"])
    C.P.barrier()


def c_scratch(nc, NT, pfx):
    S = {}
    S["QKT"] = nc.dram_tensor(pfx + "QKT", [NT, 128, 16, 128], BF16).ap()
    S["KT"] = nc.dram_tensor(pfx + "KT", [NT, 128, 8, 128], BF16).ap()
    S["VA"] = nc.dram_tensor(pfx + "VA", [NT, 128, 4, 256], BF16).ap()
    S["ET"] = nc.dram_tensor(pfx + "ET", [NT, 128, 8], F32).ap()
    S["OM"] = nc.dram_tensor(pfx + "OM", [NT, 128, 1024], F32).ap()
    S["HO"] = [nc.dram_tensor(pfx + "HO%d" % d, [NT, 128, 4, 256], F32).ap() for d in range(2)]
    return S


def emit_layer1_mixer(C, NL, x_d, xkey, ctx_d, ckey, modd, W, x1_d, x1key):
    nc = C.nc
    NT = 2 + NL
    S = c_scratch(nc, NT, "c_")
    emit_c_stage_a(C, NL, x_d, xkey, ctx_d, ckey, modd, W["gla_w_in"], W["gla_gate_up"], W["gla_gate_b"], S)
    emit_scan(C, NT, 2, S["QKT"], S["KT"], S["VA"], S["ET"], S["HO"], 256, False, C.consts_d[2:4], "cs")
    emit_merge(C, NL, False, "c", S, S["HO"], "cs", None, None, x_d, xkey, ctx_d, ckey, modd, W["gla_norm_w"], W["gla_w_out"], W["lnw0"], W["lnb0"], x1_d, x1key, None, None)


SEQ = 4096
NLAT = SEQ // 128


def build_full(NL=NLAT, peer_groups=None):
    nc = bass.Bass("TRN2", target_bir_lowering=False)
    P = Prog(nc)
    T = NL * 128

    def din(name, shape):
        return nc.dram_tensor(name, shape, F32, kind="ExternalInput").ap()

    def dscr(name, shape, dt=F32):
        return nc.dram_tensor(name, shape, dt).ap()
    consts = din("consts", [5, 128, 128])
    x = din("x", [T, D])
    ctx = din("ctx", [256, D])
    ccT = din("ccT", [128, 8, 2])
    wmod = din("w_mod", [2, D, 6 * D])
    bmod = din("b_mod", [2, 6 * D])
    lnw = din("ln_w", [2, 2, D])
    lnb = din("ln_b", [2, 2, D])
    W0 = dict(ab_w_in=din("ab_w_in", [D, 2832]), ab_gate_b=din("ab_gate_b", [16]), ab_norm_w=din("ab_norm_w", [512]), ab_sink=din("ab_sink", [8]),
              ab_w_out=din("ab_w_out", [D, D]), rope=din("rope", [NL, 128, 2, 32]), lnw0=lnw[0, 0], lnb0=lnb[0, 0])
    W1 = dict(gla_w_in=din("gla_w_in", [D, 3104]), gla_gate_up=din("gla_gate_up", [2, 16, 512]), gla_gate_b=din("gla_gate_b", [1024]), gla_norm_w=din("gla_norm_w", [D]),
              gla_w_out=din("gla_w_out", [D, D]), lnw0=lnw[1, 0], lnb0=lnb[1, 0])
    wq = din("peer_wq", [2, D, 2048])
    keys = din("peer_keys", [2, 16, 128, 128])
    ure = din("peer_ure", [2, 128, 128, 8, 128])
    pv = din("peer_v", [2, 16384, D])
    out = nc.dram_tensor("out", [T, D], F32, kind="ExternalOutput").ap()
    modd = [dscr("modd%d" % l, [2, 6, D]) for l in range(2)]
    x1 = dscr("x1", [T, D]); c1 = dscr("c1", [256, D])
    x2 = dscr("x2", [T, D]); c2 = dscr("c2", [256, D])
    x3 = dscr("x3", [T, D])
    ubf = dscr("ubf", [128, 128, 8, 128], BF16)
    vbf = dscr("vbf", [16384, D], BF16)
    C = Ctx(nc, P, consts)
    emit_peer_prep(C, ure[0], pv[0], ubf, vbf)
    emit_mod(C, ccT, wmod[0], bmod[0], modd[0])
    emit_layer0_mixer(C, NL, x, "x", ctx, "ctx", modd[0], W0, x1, "x1", c1, "c1")
    ng = None if peer_groups is None else peer_groups
    emit_peer(C, T, x1, "x1", x2, "x2", [modd[0][0, 4], modd[0][0, 3], modd[0][0, 5]], lnw[0, 1], lnb[0, 1], wq[0], keys[0], ure[0], pv[0], ubf, vbf, n_groups=ng)
    emit_peer(C, 256, c1, "c1", c2, "c2", [modd[0][1, 4], modd[0][1, 3], modd[0][1, 5]], lnw[0, 1], lnb[0, 1], wq[0], keys[0], ure[0], pv[0], ubf, vbf)
    emit_peer_prep(C, ure[1], pv[1], ubf, vbf)
    emit_mod(C, ccT, wmod[1], bmod[1], modd[1])
    emit_layer1_mixer(C, NL, x2, "x2", c2, "c2", modd[1], W1, x3, "x3")
    emit_peer(C, T, x3, "x3", out, "out", [modd[1][0, 4], modd[1][0, 3], modd[1][0, 5]], lnw[1, 1], lnb[1, 1], wq[1], keys[1], ure[1], pv[1], ubf, vbf, n_groups=ng)
    P.finish()
    return nc


def make_feeds(inputs, NL=NLAT):
    T = NL * 128
    f32 = lambda a: np.ascontiguousarray(np.asarray(a, dtype=np.float32))
    consts = make_consts()
    rope = make_rope(T)
    pu = np.asarray(inputs["peer_u"], dtype=np.float32)
    ure = np.ascontiguousarray(pu.reshape(2, 128, 128, 8, 128).transpose(0, 1, 4, 3, 2))
    shared = dict(consts=consts, rope=rope, w_mod=f32(inputs["w_mod"]), b_mod=f32(inputs["b_mod"]), ln_w=f32(inputs["ln_w"]), ln_b=f32(inputs["ln_b"]),
                  ab_w_in=f32(inputs["ab_w_in"][0]), ab_gate_b=f32(np.asarray(inputs["ab_gate_b"][0]).reshape(16)), ab_norm_w=f32(inputs["ab_norm_w"][0]),
                  ab_sink=f32(inputs["ab_sink"][0]), ab_w_out=f32(inputs["ab_w_out"][0]), gla_w_in=f32(inputs["gla_w_in"][0]), gla_gate_up=f32(inputs["gla_gate_up"][0]),
                  gla_gate_b=f32(np.asarray(inputs["gla_gate_b"][0]).reshape(1024)), gla_norm_w=f32(inputs["gla_norm_w"][0]), gla_w_out=f32(inputs["gla_w_out"][0]),
                  peer_wq=f32(inputs["peer_wq"]), peer_keys=f32(np.asarray(inputs["peer_keys"]).reshape(2, 16, 128, 128)), peer_ure=ure, peer_v=f32(inputs["peer_v"]))
    feeds = []
    xs = np.asarray(inputs["x"], dtype=np.float32)
    cs = np.asarray(inputs["c"], dtype=np.float32)
    cx = np.asarray(inputs["ctx"], dtype=np.float32)
    cctx = np.asarray(inputs["c_ctx"], dtype=np.float32)
    for b in range(xs.shape[0]):
        cc = np.stack([cs[b], cctx], -1)
        ccT = np.ascontiguousarray(cc.reshape(8, 128, 2).transpose(1, 0, 2))
        d = dict(shared)
        d.update(x=np.ascontiguousarray(xs[b, :T]), ctx=np.ascontiguousarray(cx[b]), ccT=ccT)
        feeds.append(d)
    return feeds


_NC_CACHE = {}


def kernel(**inputs):
    if "full" not in _NC_CACHE:
        _NC_CACHE["full"] = build_full()
    nc = _NC_CACHE["full"]
    feeds = make_feeds(inputs)
    res = run_bass_kernel_spmd(nc, feeds, core_ids=list(range(len(feeds))))
    return np.stack([r["out"] for r in res.results], 0).astype(np.float32)
```

```python
from contextlib import ExitStack
import numpy as np
import concourse.bass as bass
import concourse.mybir as mybir
from concourse.bass_utils import run_bass_kernel_spmd

F32 = mybir.dt.float32
BF16 = mybir.dt.bfloat16
U32 = mybir.dt.uint32
AF = mybir.ActivationFunctionType
ALU = mybir.AluOpType
AX = mybir.AxisListType

D = 1024
NKEY = 128
PH = 8
PK = 16


class Prog:
    def __init__(self, nc, n_dma_slots=12):
        self.nc = nc
        self.eng = {"pe": nc.tensor, "act": nc.scalar, "dve": nc.vector, "pool": nc.gpsimd, "sp": nc.sync}
        self.csem = {e: nc.alloc_semaphore("c_" + e) for e in ("pe", "act", "dve", "pool")}
        self.cnt = {e: 0 for e in self.csem}
        self.dslots = {}
        for q, n in (("sp", n_dma_slots), ("pool", 6), ("act", 4)):
            self.dslots[q] = [[nc.alloc_semaphore("d_%s_%d" % (q, i)), 0] for i in range(n)]
        self.dnext = {q: 0 for q in self.dslots}
        self.seen = {e: {} for e in self.eng}
        self.lastw = {}
        self.lastr = {}
        self.multi = {}
        self.n_ops = 0

    def _wait(self, e, tok):
        if tok is None:
            return
        sem, val, src = tok
        key = sem.num if hasattr(sem, "num") else id(sem)
        if self.seen[e].get(key, 0) >= val:
            return
        self.eng[e].wait_ge(sem, val)
        self.seen[e][key] = val

    def _deps(self, e, reads, writes):
        for b in reads:
            for t in self.multi.get(b, ()):
                self._wait(e, t)
            t = self.lastw.get(b)
            if t is not None and not (t[2] == e and e == "pe"):
                self._wait(e, t)
        for b in writes:
            t = self.lastw.get(b)
            if t is not None and t[2] != e:
                self._wait(e, t)
            for t in self.lastr.get(b, ()):
                if t[2] != e:
                    self._wait(e, t)

    def _record(self, tok, reads, writes):
        for b in reads:
            self.lastr.setdefault(b, []).append(tok)
            if len(self.lastr[b]) > 6:
                best = {}
                for t in self.lastr[b]:
                    k = (t[2], t[0].num if hasattr(t[0], "num") else id(t[0]))
                    if k not in best or best[k][1] < t[1]:
                        best[k] = t
                self.lastr[b] = list(best.values())
        for b in writes:
            self.lastw[b] = tok
            self.lastr[b] = []

    def op(self, e, fn, reads=(), writes=(), track=True):
        self._deps(e, reads, writes)
        inst = fn(self.eng[e])
        self.n_ops += 1
        if track:
            self.cnt[e] += 1
            inst.then_inc(self.csem[e], 1)
            tok = (self.csem[e], self.cnt[e], e)
        else:
            tok = (self.csem[e], self.cnt[e] + 1, e)
        self._record(tok, reads, writes)
        return tok

    def dma(self, out, in_, reads=(), writes=(), q="sp", **kw):
        slots = self.dslots[q]
        i = self.dnext[q]
        self.dnext[q] = (i + 1) % len(slots)
        sem, uses = slots[i]
        if uses > 0:
            self._wait(q, (sem, 16 * uses, "dma"))
        self._deps(q, reads, writes)
        self.eng[q].dma_start(out=out, in_=in_, **kw).then_inc(sem, 16)
        self.n_ops += 1
        slots[i][1] = uses + 1
        tok = (sem, 16 * (uses + 1), "dma")
        self._record(tok, reads, writes)
        return tok

    def barrier(self):
        toks = [(self.csem[e], self.cnt[e], e) for e in self.csem if self.cnt[e] > 0]
        for q, slots in self.dslots.items():
            toks += [(sem, 16 * uses, "dma") for sem, uses in slots if uses > 0]
        for e in self.eng:
            for t in toks:
                self._wait(e, t)
        self.lastw = {k: None for k in self.lastw}
        self.lastr = {}
        self.multi = {}

    def finish(self):
        for q, slots in self.dslots.items():
            for sem, uses in slots:
                if uses > 0:
                    self._wait("sp", (sem, 16 * uses, "dma"))


def bc(ap, shape):
    return ap.to_broadcast(shape)


ALPHA = 4.0 ** 0.25
LN_EPS = 1e-5


class Ctx:
    def __init__(self, nc, P, consts_d):
        self.nc = nc
        self.P = P
        self.uid = 0
        self.consts_d = consts_d
        self.ident = nc.alloc_sbuf_tensor("c_ident", [128, 128], F32)
        self.iota = nc.alloc_sbuf_tensor("c_iota", [128, 128], F32)
        P.dma(self.ident[:], consts_d[0], writes=["c_ident"])
        P.dma(self.iota[:], consts_d[1], writes=["c_iota"])

    def name(self, s):
        self.uid += 1
        return "%s_%d" % (s, self.uid)


class V:
    def __init__(self, ap, name):
        self.ap = ap
        self.name = name

    def __getitem__(self, k):
        return self.ap if k == slice(None) else self.ap[k]


def layernorm_rows(P, nc, y, yn, tmp, st, lnw, lnb, tag, out):
    yk, tk, sk = y.name, tmp.name, st.name
    P.op("dve", lambda e: e.reduce_sum(out=st[:, 0:1], in_=y[:], axis=AX.X), reads=[yk], writes=[sk])
    P.op("dve", lambda e: e.tensor_single_scalar(out=st[:, 1:2], in_=st[:, 0:1], scalar=-1.0 / D, op=ALU.mult), reads=[sk], writes=[sk])
    P.op("dve", lambda e: e.tensor_scalar(out=y[:], in0=y[:], scalar1=st[:, 1:2], scalar2=None, op0=ALU.add), reads=[yk, sk], writes=[yk])
    P.op("act", lambda e: e.activation(out=tmp[:], in_=y[:], func=AF.Square, accum_out=st[:, 2:3]), reads=[yk], writes=[tk, sk])
    P.op("dve", lambda e: e.tensor_scalar(out=st[:, 3:4], in0=st[:, 2:3], scalar1=1.0 / D, scalar2=LN_EPS, op0=ALU.mult, op1=ALU.add), reads=[sk], writes=[sk])
    P.op("act", lambda e: e.activation(out=st[:, 3:4], in_=st[:, 3:4], func=AF.Sqrt), reads=[sk], writes=[sk])
    P.op("dve", lambda e: e.reciprocal(out=st[:, 3:4], in_=st[:, 3:4]), reads=[sk], writes=[sk])
    P.op("dve", lambda e: e.tensor_scalar(out=y[:], in0=y[:], scalar1=st[:, 3:4], scalar2=None, op0=ALU.mult), reads=[yk, sk], writes=[yk])
    P.op("dve", lambda e: e.tensor_tensor(out=y[:], in0=y[:], in1=lnw[:], op=ALU.mult), reads=[yk, lnw.name], writes=[yk])
    P.op("dve", lambda e: e.tensor_tensor(out=out[:], in0=y[:], in1=lnb[:], op=ALU.add), reads=[yk, lnb.name], writes=[out.name])


def top16x2(P, nc, srcs, srck, scrs, vals, idxs, outk):
    n = srcs[0].shape[-1]
    ks = [[k + ".c%d" % q for k in outk] for q in range(2)]
    for q in range(2):
        P.op("dve", lambda e, q=q: e.max(out=vals[q][:, 0:8], in_=srcs[q]), reads=[srck], writes=ks[q])
    yield
    for q in range(2):
        P.op("dve", lambda e, q=q: e.max_index(out=idxs[q][:, 0:8], in_max=vals[q][:, 0:8], in_values=srcs[q]), reads=[srck] + ks[q], writes=ks[q])
    yield
    for q in range(2):
        P.op("dve", lambda e, q=q: e.match_replace(out=scrs[q][:, 0:n], in_to_replace=vals[q][:, 0:8], in_values=srcs[q], imm_value=-1e30), reads=[srck] + ks[q], writes=[scrs[q].name])
    yield
    for q in range(2):
        P.op("dve", lambda e, q=q: e.max(out=vals[q][:, 8:16], in_=scrs[q][:, 0:n]), reads=[scrs[q].name], writes=ks[q])
    yield
    for q in range(2):
        P.op("dve", lambda e, q=q: e.max_index(out=idxs[q][:, 8:16], in_max=vals[q][:, 8:16], in_values=scrs[q][:, 0:n]), reads=[scrs[q].name] + ks[q], writes=outk + ks[q])
    yield


def top16(P, nc, src_ap, srck, scratch, vals_ap, idx_ap, outk):
    sk = scratch.name
    n = src_ap.shape[-1]
    P.op("dve", lambda e: e.max(out=vals_ap[:, 0:8], in_=src_ap), reads=[srck], writes=outk)
    P.op("dve", lambda e: e.max_index(out=idx_ap[:, 0:8], in_max=vals_ap[:, 0:8], in_values=src_ap), reads=[srck] + outk, writes=outk)
    P.op("dve", lambda e: e.match_replace(out=scratch[:, 0:n], in_to_replace=vals_ap[:, 0:8], in_values=src_ap, imm_value=-1e30), reads=[srck] + outk, writes=[sk])
    P.op("dve", lambda e: e.max(out=vals_ap[:, 8:16], in_=scratch[:, 0:n]), reads=[sk], writes=outk)
    P.op("dve", lambda e: e.max_index(out=idx_ap[:, 8:16], in_max=vals_ap[:, 8:16], in_values=scratch[:, 0:n]), reads=[sk] + outk, writes=outk)


def emit_peer(C, T, x_in, xin_key, x_out, xout_key, mod_d, lnw_d, lnb_d, wq_d, keys_d, ure_d, v_d, ubf_d, vbf_d, n_groups=None):
    nc, P = C.nc, C.P
    TG = 256
    NG = T // TG if n_groups is None else n_groups
    with ExitStack() as es:
        def sb(name, shape, dt):
            return es.enter_context(nc.sbuf_tensor(C.name(name), shape, dt))

        def ps(name, shape, dt=F32):
            return es.enter_context(nc.psum_tensor(C.name(name), shape, dt))

        wq = sb("wq", [128, 8, 2048], BF16)
        for dc in range(8):
            P.dma(wq[:, dc, :], wq_d[dc * 128:(dc + 1) * 128, :], writes=["%s.%d" % (wq.name, dc)], q="pool")
        keysT = sb("keysT", [128, 16, 128], BF16)
        S = sb("S", [128, 2048], F32)
        ktmp = V(S[:].rearrange("p (h k) -> p h k", h=16), S.name)
        P.dma(ktmp[:], keys_d.rearrange("h n k -> n h k"), writes=[ktmp.name])
        modr = sb("modr", [128, 3, D], F32)
        for r in range(3):
            P.dma(modr[:, r, :], mod_d[r].partition_broadcast(128), writes=["%s.%d" % (modr.name, r)])
        lnw = sb("lnw", [128, D], F32)
        lnb = sb("lnb", [128, D], F32)
        P.dma(lnw[:], lnw_d.partition_broadcast(128), writes=[lnw.name])
        P.dma(lnb[:], lnb_d.partition_broadcast(128), writes=[lnb.name])

        pbig = [ps("pbig%d" % i, [128, 1024]) for i in range(2)]
        psm = [ps("psm%d" % i, [128, 512]) for i in range(4)]
        for hp in range(16):
            pt = psm[hp % 4]
            P.op("pe", lambda e, pt=pt, hp=hp: e.transpose(out=pt[:, 0:128], in_=ktmp[:, hp, :], identity=C.ident[:]),
                 reads=[ktmp.name, "c_ident"], writes=[pt.name])
            P.op("act", lambda e, pt=pt, hp=hp: e.copy(out=keysT[:, hp, :], in_=pt[:, 0:128]), reads=[pt.name], writes=[keysT.name])

        xf = sb("xf", [128, D], F32)
        xe = [sb("xe%d" % i, [128, D], F32) for i in range(2)]
        hm = sb("hm", [128, D], F32)
        hT = [sb("hT%d" % b, [128, 8, TG], BF16) for b in range(2)]
        qT = sb("qT", [128, 16, TG], BF16)
        scr2 = [sb("scr%d" % i, [128, 256], F32) for i in range(2)]
        stop = sb("stop", [128, 16, 16], F32)
        itopu = sb("itopu", [128, 16, 16], U32)
        itopf = sb("itopf", [128, 16, 16], F32)
        cand = sb("cand", [128, 8, 256], F32)
        eq = V(cand[:].rearrange("p h (a b) -> p h a b", a=16), cand.name)
        best = sb("best", [128, 8, 16], F32)
        posu = sb("posu", [128, 8, 16], U32)
        abu = sb("abu", [128, 2, 8, 16], U32)
        abf = sb("abf", [128, 2, 8, 16], F32)
        IJG = sb("IJG", [128, 3, 128], F32)
        IJGT = [sb("IJGT%d" % b, [128, 3, TG], BF16) for b in range(2)]
        GKF = [sb("GKF%d" % b, [128, TG], BF16) for b in range(2)]
        iotab = sb("iotab", [128, 128], BF16)
        P.op("dve", lambda e: e.tensor_copy(out=iotab[:], in_=C.iota[:]), reads=["c_iota"], writes=[iotab.name])
        sm = sb("sm", [128, 8, 4], F32)
        NB = 8
        oig = [sb("oig%d" % i, [128, NB, 128], BF16) for i in range(2)]
        oj = [sb("oj%d" % i, [128, NB, 128], BF16) for i in range(2)]
        Gall = sb("Gall", [128, TG, 128], BF16)
        UB = 1
        NUB = 3
        ubuf = [sb("ubuf%d" % i, [128, UB, 8, 128], BF16) for i in range(NUB)]
        vbuf = [sb("vbuf%d" % i, [128, UB, D], BF16) for i in range(NUB)]
        Ag = [sb("Ag%d" % i, [128, TG], BF16) for i in range(2)]
        GA = [sb("GA%d" % i, [128, TG], BF16) for i in range(2)]
        ybuf = hm
        ytmp = V(S[:, 0:D], S.name)
        yout = V(S[:, D:2 * D], S.name)
        st = sb("st", [128, 4], F32)

        def front(g):
            bsel = g % 2
            t0 = g * TG
            hTb, IJb = hT[bsel], IJGT[bsel]
            for tt in range(2):
                xb = xf
                xk = xb.name
                P.dma(xb[:], x_in[t0 + tt * 128: t0 + (tt + 1) * 128, :], reads=["%s.%d" % (xin_key, (t0 + tt * 128) // 128)], writes=[xk])
                P.op("dve", lambda e, xb=xb: e.tensor_tensor(out=hm[:], in0=xb[:], in1=modr[:, 0, :], op=ALU.mult), reads=[xk, modr.name + ".0"], writes=[hm.name])
                P.op("dve", lambda e: e.tensor_tensor(out=hm[:], in0=hm[:], in1=modr[:, 1, :], op=ALU.add), reads=[hm.name, modr.name + ".1"], writes=[hm.name])
                for dc in range(8):
                    pb = psm[dc // 4]
                    P.op("pe", lambda e, dc=dc, pb=pb: e.transpose(out=pb[:, (dc % 4) * 128:(dc % 4 + 1) * 128], in_=hm[:, dc * 128:(dc + 1) * 128], identity=C.ident[:]),
                         reads=[hm.name, "c_ident"], writes=[pb.name], track=(dc % 4 == 3))
                for hb in range(2):
                    P.op("act", lambda e, tt=tt, hb=hb, hTb=hTb: e.copy(out=hTb[:, hb * 4:(hb + 1) * 4, tt * 128:(tt + 1) * 128], in_=psm[hb][:].rearrange("p (c t) -> p c t", c=4)),
                         reads=[psm[hb].name], writes=[hTb.name])
                yield
            for hp in range(16):
                pt = psm[hp % 2]
                for dc in range(8):
                    P.op("pe", lambda e, hp=hp, dc=dc, pt=pt, hTb=hTb: e.matmul(pt[:, 0:TG], lhsT=wq[:, dc, hp * 128:(hp + 1) * 128], rhs=hTb[:, dc, :], start=(dc == 0), stop=(dc == 7)),
                         reads=["%s.%d" % (wq.name, dc), hTb.name], writes=[pt.name], track=(dc == 7))
                P.op("act", lambda e, hp=hp, pt=pt: e.copy(out=qT[:, hp, :], in_=pt[:, 0:TG]), reads=[pt.name], writes=[qT.name])
                yield
            for tt in range(2):
                for grp in range(4):
                    pb = psm[grp % 2]
                    for k in range(4):
                        hp = grp * 4 + k
                        P.op("pe", lambda e, hp=hp, pb=pb, k=k, tt=tt: e.matmul(pb[:, k * 128:(k + 1) * 128], lhsT=qT[:, hp, tt * 128:(tt + 1) * 128], rhs=keysT[:, hp, :], start=True, stop=True),
                             reads=[qT.name, keysT.name], writes=[pb.name], track=(k == 3))
                    P.op("act", lambda e, grp=grp, pb=pb: e.copy(out=S[:, grp * 512:(grp + 1) * 512], in_=pb[:]), reads=[pb.name], writes=[S.name])
                    yield
                for hp in range(0, 16, 2):
                    yield from top16x2(P, nc, [S[:, (hp + q) * 128:(hp + q + 1) * 128] for q in range(2)], S.name, scr2, [stop[:, hp + q, :] for q in range(2)],
                                       [itopu[:, hp + q, :] for q in range(2)], [stop.name, itopu.name])
                P.op("dve", lambda e: e.tensor_copy(out=itopf[:], in_=itopu[:]), reads=[itopu.name], writes=[itopf.name])
                sv = stop[:].rearrange("p (h two) k -> p h two k", two=2)
                P.op("dve", lambda e: e.tensor_tensor(out=cand[:].rearrange("p h (a b) -> p h a b", a=16),
                                                      in0=sv[:, :, 0, :].unsqueeze(3).to_broadcast([128, 8, 16, 16]),
                                                      in1=sv[:, :, 1, :].unsqueeze(2).to_broadcast([128, 8, 16, 16]), op=ALU.add),
                     reads=[stop.name], writes=[cand.name])
                yield
                for h in range(0, 8, 2):
                    yield from top16x2(P, nc, [cand[:, h + q, :] for q in range(2)], cand.name, scr2, [best[:, h + q, :] for q in range(2)],
                                       [posu[:, h + q, :] for q in range(2)], [best.name, posu.name])
                P.op("dve", lambda e: e.tensor_single_scalar(out=abu[:, 0], in_=posu[:], scalar=4, op=ALU.logical_shift_right), reads=[posu.name], writes=[abu.name])
                P.op("dve", lambda e: e.tensor_single_scalar(out=abu[:, 1], in_=posu[:], scalar=15, op=ALU.bitwise_and), reads=[posu.name], writes=[abu.name])
                P.op("dve", lambda e: e.tensor_copy(out=abf[:], in_=abu[:]), reads=[abu.name], writes=[abf.name])
                yield
                iv = itopf[:].rearrange("p (h two) k -> p h two k", two=2)
                for w in range(2):
                    P.op("dve", lambda e, w=w: e.tensor_tensor(out=eq[:], in0=abf[:, w].unsqueeze(3).to_broadcast([128, 8, 16, 16]),
                                                               in1=C.iota[:, 0:16].unsqueeze(1).unsqueeze(1).to_broadcast([128, 8, 16, 16]), op=ALU.is_equal),
                         reads=[abf.name, "c_iota"], writes=[eq.name])
                    P.op("dve", lambda e, w=w: e.tensor_tensor(out=eq[:], in0=eq[:], in1=iv[:, :, w, :].unsqueeze(2).to_broadcast([128, 8, 16, 16]), op=ALU.mult),
                         reads=[eq.name, itopf.name], writes=[eq.name])
                    P.op("dve", lambda e, w=w: e.reduce_sum(out=IJG[:, w, :].rearrange("p (h k) -> p h k", h=8), in_=eq[:], axis=AX.X),
                         reads=[eq.name], writes=[IJG.name])
                    yield
                gk = IJG[:, 2, :].rearrange("p (h k) -> p h k", h=8)
                P.op("dve", lambda e: e.tensor_tensor(out=gk, in0=best[:], in1=best[:, :, 0:1].to_broadcast([128, 8, 16]), op=ALU.subtract),
                     reads=[best.name], writes=[IJG.name])
                P.op("act", lambda e: e.activation(out=gk, in_=gk, func=AF.Exp), reads=[IJG.name], writes=[IJG.name])
                P.op("dve", lambda e: e.reduce_sum(out=sm[:, :, 0:1], in_=gk, axis=AX.X), reads=[IJG.name], writes=[sm.name])
                P.op("dve", lambda e: e.reciprocal(out=sm[:, :, 1:2], in_=sm[:, :, 0:1]), reads=[sm.name], writes=[sm.name])
                P.op("dve", lambda e: e.tensor_tensor(out=gk, in0=gk, in1=sm[:, :, 1:2].to_broadcast([128, 8, 16]), op=ALU.mult),
                     reads=[IJG.name, sm.name], writes=[IJG.name])
                yield
                for w in range(3):
                    pt = psm[w % 2]
                    P.op("pe", lambda e, w=w, pt=pt: e.transpose(out=pt[:, 0:128], in_=IJG[:, w, :], identity=C.ident[:]), reads=[IJG.name, "c_ident"], writes=[pt.name])
                    if w < 2:
                        P.op("act", lambda e, w=w, pt=pt, tt=tt, IJb=IJb: e.copy(out=IJb[:, w, tt * 128:(tt + 1) * 128], in_=pt[:, 0:128]), reads=[pt.name], writes=[IJb.name])
                    else:
                        P.op("act", lambda e, pt=pt, tt=tt, bsel=bsel: e.copy(out=GKF[bsel][:, tt * 128:(tt + 1) * 128], in_=pt[:, 0:128]), reads=[pt.name], writes=[GKF[bsel].name])
                yield

        def gbuild(g):
            IJb = IJGT[g % 2]
            GKf = GKF[g % 2]
            for tb in range(TG // NB):
                s = tb % 2
                t0_ = tb * NB
                iob = iotab[:].unsqueeze(1).to_broadcast([128, NB, 128])
                P.op("dve", lambda e, s=s, t0_=t0_, iob=iob: e.tensor_tensor(out=oig[s][:], in0=iob, in1=IJb[:, 0, t0_:t0_ + NB].unsqueeze(2).to_broadcast([128, NB, 128]), op=ALU.is_equal),
                     reads=[iotab.name, IJb.name], writes=[oig[s].name])
                P.op("dve", lambda e, s=s, t0_=t0_, iob=iob: e.tensor_tensor(out=oj[s][:], in0=iob, in1=IJb[:, 1, t0_:t0_ + NB].unsqueeze(2).to_broadcast([128, NB, 128]), op=ALU.is_equal),
                     reads=[iotab.name, IJb.name], writes=[oj[s].name])
                P.op("dve", lambda e, s=s, t0_=t0_: e.tensor_tensor(out=oig[s][:], in0=oig[s][:], in1=GKf[:, t0_:t0_ + NB].unsqueeze(2).to_broadcast([128, NB, 128]), op=ALU.mult),
                     reads=[oig[s].name, GKf.name], writes=[oig[s].name])
                for qd in range(NB // 4):
                    pt = psm[(tb * (NB // 4) + qd) % 4]
                    for k in range(4):
                        kk = qd * 4 + k
                        P.op("pe", lambda e, s=s, k=k, kk=kk, pt=pt: e.matmul(pt[:, k * 128:(k + 1) * 128], lhsT=oj[s][:, kk, :], rhs=oig[s][:, kk, :], start=True, stop=True),
                             reads=[oj[s].name, oig[s].name], writes=[pt.name], track=(k == 3))
                    P.op("act", lambda e, t0_=t0_, qd=qd, pt=pt: e.copy(out=Gall[:, t0_ + qd * 4:t0_ + qd * 4 + 4, :], in_=pt[:].rearrange("p (k i) -> p k i", k=4)),
                         reads=[pt.name], writes=[Gall.name])

        def load_uv(blk):
            s = blk % NUB
            P.dma(ubuf[s][:], ubf_d[blk * UB:(blk + 1) * UB].rearrange("i p c e -> p i c e"), reads=["ubf%d" % (blk * UB // 8)], writes=[ubuf[s].name])
            P.dma(vbuf[s][:], vbf_d[blk * UB * 128:(blk + 1) * UB * 128, :].rearrange("(i e) d -> e i d", i=UB), reads=["vbf%d" % (blk * UB // 8)], writes=[vbuf[s].name])

        def mainloop(g, gen):
            hTb = hT[g % 2]

            def a_mm(i):
                s, ii = (i // UB) % NUB, i % UB
                pa = psm[2 + i % 2]
                for dc in range(8):
                    P.op("pe", lambda e, s=s, ii=ii, dc=dc, pa=pa: e.matmul(pa[:, 0:TG], lhsT=ubuf[s][:, ii, dc, :], rhs=hTb[:, dc, :], start=(dc == 0), stop=(dc == 7)),
                         reads=[ubuf[s].name, hTb.name], writes=[pa.name], track=(dc == 7))
                P.op("act", lambda e, i=i, pa=pa: e.activation(out=Ag[i % 2][:], in_=pa[:, 0:TG], func=AF.Gelu), reads=[pa.name], writes=[Ag[i % 2].name])
                P.op("dve", lambda e, i=i: e.tensor_tensor(out=GA[i % 2][:], in0=Ag[i % 2][:], in1=Gall[:, :, i], op=ALU.mult),
                     reads=[Ag[i % 2].name, Gall.name], writes=[GA[i % 2].name])

            def v_mm(i):
                s, ii = (i // UB) % NUB, i % UB
                for tt in range(2):
                    for hf in range(2):
                        P.op("pe", lambda e, s=s, ii=ii, tt=tt, hf=hf, i=i: e.matmul(pbig[tt][:, hf * 512:(hf + 1) * 512], lhsT=GA[i % 2][:, tt * 128:(tt + 1) * 128],
                                                                                   rhs=vbuf[s][:, ii, hf * 512:(hf + 1) * 512], start=(i == 0), stop=(i == 127)),
                             reads=[GA[i % 2].name, vbuf[s].name], writes=[pbig[tt].name], track=(tt == 1 and hf == 1))

            for b_ in range(NUB):
                load_uv(b_)
            a_mm(0)
            for i in range(128):
                if i + 1 < 128:
                    a_mm(i + 1)
                v_mm(i)
                if i % UB == UB - 1 and i // UB + NUB < 128 // UB:
                    load_uv(i // UB + NUB)
                if i == 96:
                    for tt in range(2):
                        P.dma(xe[tt][:], x_in[g * TG + tt * 128: g * TG + (tt + 1) * 128, :], reads=["%s.%d" % (xin_key, (g * TG + tt * 128) // 128)], writes=[xe[tt].name])
                if gen is not None and i >= 2:
                    next(gen, None)
                    next(gen, None)
            if gen is not None:
                for _ in gen:
                    pass

        def epilogue(g):
            t0 = g * TG
            for tt in range(2):
                xb = xe[tt]
                P.op("dve", lambda e, tt=tt: e.tensor_tensor(out=ytmp[:], in0=pbig[tt][:], in1=modr[:, 2, :], op=ALU.mult), reads=[pbig[tt].name, modr.name + ".2"], writes=[ytmp.name])
                P.op("dve", lambda e, xb=xb: e.scalar_tensor_tensor(out=ybuf[:], in0=xb[:], scalar=ALPHA, in1=ytmp[:], op0=ALU.mult, op1=ALU.add),
                     reads=[xb.name, ytmp.name], writes=[ybuf.name])
                layernorm_rows(P, nc, ybuf, None, ytmp, st, lnw, lnb, "pe", yout)
                P.dma(x_out[t0 + tt * 128: t0 + (tt + 1) * 128, :], yout[:], reads=[yout.name], writes=["%s.%d" % (xout_key, (t0 + tt * 128) // 128)])

        for _ in front(0):
            pass
        for g in range(NG):
            gbuild(g)
            mainloop(g, front(g + 1) if g + 1 < NG else None)
            epilogue(g)
    C.P.barrier()


def emit_peer_prep(C, ure_d, v_d, ubf_d, vbf_d):
    P = C.P
    for b in range(16):
        P.dma(ubf_d[b * 8:(b + 1) * 8].rearrange("i p c e -> (i p) (c e)"), ure_d[b * 8:(b + 1) * 8].rearrange("i p c e -> (i p) (c e)"),
              reads=["ure"], writes=["ubf%d" % b], q="pool")
        P.dma(vbf_d[b * 1024:(b + 1) * 1024, :], v_d[b * 1024:(b + 1) * 1024, :], reads=["vsrc"], writes=["vbf%d" % b], q="pool")


def emit_mod(C, ccT_d, wmod_d, bmod_d, modd):
    nc, P = C.nc, C.P
    with ExitStack() as es:
        def sb(name, shape, dt):
            return es.enter_context(nc.sbuf_tensor(C.name(name), shape, dt))
        cc = sb("cc", [128, 8, 2], F32)
        P.dma(cc[:], ccT_d, writes=[cc.name])
        P.op("act", lambda e: e.activation(out=cc[:], in_=cc[:], func=AF.Silu), reads=[cc.name], writes=[cc.name])
        wb = [sb("wmod%d" % i, [128, 8, 512], F32) for i in range(2)]
        bm = sb("bm", [2, 6 * D], F32)
        P.dma(bm[:], bmod_d.partition_broadcast(2), writes=[bm.name])
        mo = sb("mo", [2, 6 * D], F32)
        pm = [es.enter_context(nc.psum_tensor(C.name("pmod%d" % i), [128, 512], F32)) for i in range(2)]
        for n in range(12):
            w = wb[n % 2]
            P.dma(w[:], wmod_d[:, n * 512:(n + 1) * 512].rearrange("(c p) n -> p c n", p=128), writes=[w.name])
            pp = pm[n % 2]
            for dc in range(8):
                P.op("pe", lambda e, dc=dc, w=w, pp=pp: e.matmul(pp[0:2, :], lhsT=cc[:, dc, :], rhs=w[:, dc, :], start=(dc == 0), stop=(dc == 7)),
                     reads=[cc.name, w.name], writes=[pp.name], track=(dc == 7))
            P.op("dve", lambda e, n=n, pp=pp: e.tensor_tensor(out=mo[:, n * 512:(n + 1) * 512], in0=pp[0:2, :], in1=bm[:, n * 512:(n + 1) * 512], op=ALU.add),
                 reads=[pp.name, bm.name], writes=[mo.name])
        for k in (1, 4):
            P.op("dve", lambda e, k=k: e.tensor_single_scalar(out=mo[:, k * D:(k + 1) * D], in_=mo[:, k * D:(k + 1) * D], scalar=1.0, op=ALU.add),
                 reads=[mo.name], writes=[mo.name])
        P.dma(modd.rearrange("r k d -> r (k d)"), mo[:], reads=[mo.name], writes=["modd"])
    C.P.barrier()


def emit_scan(C, NT, n_ctx, QKT, KT, VA, ET, HO, DV, aug, masks_d, key):
    nc, P = C.nc, C.P
    DVO = DV - 1 if aug else DV
    with ExitStack() as es:
        def sb(name, shape, dt):
            return es.enter_context(nc.sbuf_tensor(C.name(name), shape, dt))

        def ps(name, shape, dt=F32):
            return es.enter_context(nc.psum_tensor(C.name(name), shape, dt))
        mask = sb("mask", [128, 2, 128], F32)
        P.dma(mask[:, 0, :], masks_d[0], writes=[mask.name + ".0"])
        P.dma(mask[:, 1, :], masks_d[1], writes=[mask.name + ".1"])
        St = [sb("St%d" % d, [128, 4, DV], F32) for d in range(2)]
        Sb = [sb("Sb%d" % d, [128, 4, DV], BF16) for d in range(2)]
        for d in range(2):
            P.op("dve", lambda e, d=d: e.memset(St[d][:], 0.0), writes=[St[d].name])
            P.op("pool", lambda e, d=d: e.memset(Sb[d][:], 0.0), writes=[Sb[d].name])
        qkt = [sb("qkt%d" % i, [128, 2, 4, 128], BF16) for i in range(2)]
        kt = [sb("kt%d" % i, [128, 4, 128], BF16) for i in range(2)]
        va = [sb("va%d" % i, [128, 4, DV], BF16) for i in range(2)]
        et = [sb("et%d" % i, [128, 4], F32) for i in range(2)]
        WT = [sb("WT%d" % i, [128, 4, 128], BF16) for i in range(2)]
        tmp = sb("tmp", [128, 4, DV], F32)
        ho = [sb("ho%d" % i, [128, 4, DVO], F32) for i in range(2)]
        dn = sb("dn", [128, 4, 2], F32)
        pS = [ps("pS%d" % d, [128, 512]) for d in range(2)]
        NB = 2 if DV <= 256 else 4
        pN = ps("pN", [128, 2, 512])
        pD = ps("pD", [128, 2, 512])
        lat = list(range(n_ctx, NT))
        order = [list(range(n_ctx)) + lat, list(range(n_ctx))[::-1] + lat[::-1]]
        for n in range(NT):
            for d in range(2):
                tile = order[d][n]
                b = d
                tk = "%s.%d" % (key, tile)
                qv = QKT[tile].rearrange("p (w dd h) t -> p w dd h t", w=2, dd=2)
                P.dma(qkt[b][:], qv[:, :, d, :, :], reads=[tk], writes=[qkt[b].name])
                P.dma(kt[b][:], KT[tile][:, d * 4:(d + 1) * 4, :], reads=[tk], writes=[kt[b].name])
                P.dma(va[b][:], VA[tile], reads=[tk], writes=[va[b].name])
                P.dma(et[b][:], ET[tile][:, d * 4:(d + 1) * 4], reads=[tk], writes=[et[b].name])
                for h in range(4):
                    P.op("pe", lambda e, h=h, b=b, d=d: e.matmul(pS[d][:, h * 128:(h + 1) * 128], lhsT=qkt[b][:, 1, h, :], rhs=qkt[b][:, 0, h, :], start=True, stop=True),
                         reads=[qkt[b].name], writes=[pS[d].name], track=(h == 3))
                P.op("dve", lambda e, b=b, d=d: e.tensor_tensor(out=WT[b][:], in0=pS[d][:].rearrange("p (h t) -> p h t", h=4),
                                                               in1=mask[:, d, :].unsqueeze(1).to_broadcast([128, 4, 128]), op=ALU.mult),
                     reads=[pS[d].name, mask.name + ".%d" % d], writes=[WT[b].name])
                for h in range(4):
                    o = pN[:, h // 2, (h % 2) * 256:(h % 2) * 256 + DV]
                    P.op("pe", lambda e, h=h, b=b, d=d, o=o: e.matmul(o, lhsT=qkt[b][:, 0, h, :], rhs=Sb[d][:, h, :], start=True, stop=False),
                         reads=[qkt[b].name, Sb[d].name], writes=["pN"], track=False)
                    P.op("pe", lambda e, h=h, b=b, o=o: e.matmul(o, lhsT=WT[b][:, h, :], rhs=va[b][:, h, :], start=False, stop=True),
                         reads=[WT[b].name, va[b].name], writes=["pN"], track=(h == 3))
                pNv = pN[:].rearrange("p a (c x) -> p (a c) x", c=2)
                if aug:
                    P.op("act", lambda e: e.activation(out=dn[:, :, 0:1], in_=pNv[:, :, DVO:DVO + 1], func=AF.Abs), reads=["pN"], writes=[dn.name])
                    P.op("dve", lambda e: e.tensor_single_scalar(out=dn[:, :, 0:1], in_=dn[:, :, 0:1], scalar=1.0, op=ALU.max), reads=[dn.name], writes=[dn.name])
                    P.op("dve", lambda e: e.reciprocal(out=dn[:, :, 1:2], in_=dn[:, :, 0:1]), reads=[dn.name], writes=[dn.name])
                    P.op("dve", lambda e, b=b: e.tensor_tensor(out=ho[b][:], in0=pNv[:, :, 0:DVO], in1=dn[:, :, 1:2].to_broadcast([128, 4, DVO]), op=ALU.mult),
                         reads=["pN", dn.name], writes=[ho[b].name])
                else:
                    P.op("act", lambda e, b=b: e.copy(out=ho[b][:], in_=pNv[:, :, 0:DVO]), reads=["pN"], writes=[ho[b].name])
                P.dma(HO[d][tile], ho[b][:], reads=[ho[b].name], writes=["%s.ho%d.%d" % (key, d, tile)])
                for h in range(4):
                    o = pD[:, h // 2, (h % 2) * 256:(h % 2) * 256 + DV]
                    P.op("pe", lambda e, h=h, b=b, o=o: e.matmul(o, lhsT=kt[b][:, h, :], rhs=va[b][:, h, :], start=True, stop=True),
                         reads=[kt[b].name, va[b].name], writes=["pD"], track=(h == 3))
                pDv = pD[:].rearrange("p a (c x) -> p (a c) x", c=2)
                P.op("dve", lambda e, d=d: e.tensor_tensor(out=tmp[:], in0=pDv[:, :, 0:DV], in1=St[d][:], op=ALU.add), reads=["pD", St[d].name], writes=[tmp.name])
                P.op("dve", lambda e, d=d, b=b: e.tensor_tensor(out=St[d][:], in0=tmp[:], in1=et[b][:].unsqueeze(2).to_broadcast([128, 4, DV]), op=ALU.mult),
                     reads=[tmp.name, et[b].name], writes=[St[d].name])
                P.op("act", lambda e, d=d: e.copy(out=Sb[d][:], in_=St[d][:]), reads=[St[d].name], writes=[Sb[d].name])
    C.P.barrier()


def load_w_bf16(C, sbf, w_d, ncols, key):
    for dc in range(8):
        C.P.dma(sbf[:, dc, :], w_d[dc * 128:(dc + 1) * 128, :], writes=["%s.%d" % (key, dc)], q="pool")


def emit_in_proj(C, es, tile_src, mod_rows, wbf, wkey, ncols, Pj, xt, hl, hT, pbanks, modr):
    nc, P = C.nc, C.P
    src_ap, src_key = tile_src
    P.dma(xt[:], src_ap, reads=[src_key], writes=[xt.name])
    P.op("dve", lambda e: e.tensor_tensor(out=hl[:], in0=xt[:], in1=modr[:, mod_rows[0], :], op=ALU.mult), reads=[xt.name, modr.name + ".%d" % mod_rows[0]], writes=[hl.name])
    P.op("dve", lambda e: e.tensor_tensor(out=hl[:], in0=hl[:], in1=modr[:, mod_rows[1], :], op=ALU.add), reads=[hl.name, modr.name + ".%d" % mod_rows[1]], writes=[hl.name])
    for dc in range(8):
        pb = pbanks[dc // 4]
        P.op("pe", lambda e, dc=dc, pb=pb: e.transpose(out=pb[:, (dc % 4) * 128:(dc % 4 + 1) * 128], in_=hl[:, dc * 128:(dc + 1) * 128], identity=C.ident[:]),
             reads=[hl.name, "c_ident"], writes=[pb.name], track=(dc % 4 == 3))
    for hb in range(2):
        P.op("act", lambda e, hb=hb: e.copy(out=hT[:, hb * 4:(hb + 1) * 4, :], in_=pbanks[hb][:].rearrange("p (c t) -> p c t", c=4)),
             reads=[pbanks[hb].name], writes=[hT.name])
    nch = (ncols + 511) // 512
    for n in range(nch):
        c0, c1 = n * 512, min(ncols, (n + 1) * 512)
        pb = pbanks[2 + n % (len(pbanks) - 2)]
        for dc in range(8):
            P.op("pe", lambda e, dc=dc, pb=pb, c0=c0, c1=c1: e.matmul(pb[:, 0:c1 - c0], lhsT=hT[:, dc, :], rhs=wbf[:, dc, c0:c1], start=(dc == 0), stop=(dc == 7)),
                 reads=[hT.name, "%s.%d" % (wkey, dc)], writes=[pb.name], track=(dc == 7))
        P.op("act", lambda e, pb=pb, c0=c0, c1=c1: e.copy(out=Pj[:, c0:c1], in_=pb[:, 0:c1 - c0]), reads=[pb.name], writes=[Pj.name])


def emit_decay_prep(C, sbs, LFv, LIv, lkey, ncol, cmats, pcum, with_li):
    nc, P = C.nc, C.P
    triu, tril, ones = cmats
    CUM, TOT = sbs
    for d in range(2):
        P.op("pe", lambda e, d=d: e.matmul(pcum[:, d * ncol:(d + 1) * ncol], lhsT=(triu if d == 0 else tril)[:], rhs=LFv[:, d * ncol:(d + 1) * ncol], start=True, stop=True),
             reads=[lkey, "c_tri"], writes=[pcum.name])
    P.op("act", lambda e: e.copy(out=CUM[:], in_=pcum[:, 0:2 * ncol]), reads=[pcum.name], writes=[CUM.name])
    P.op("pe", lambda e: e.matmul(pcum[:, 0:2 * ncol], lhsT=ones[:], rhs=LFv[:, 0:2 * ncol], start=True, stop=True), reads=[lkey, "c_tri"], writes=[pcum.name])
    P.op("act", lambda e: e.activation(out=TOT[:], in_=pcum[:, 0:2 * ncol], func=AF.Exp), reads=[pcum.name], writes=[TOT.name])


def emit_ab_stage_a(C, NL, x_d, xkey, ctx_d, ckey, modd, w_in_d, gate_b_d, rope_d, S):
    nc, P = C.nc, C.P
    NT = 2 + NL
    with ExitStack() as es:
        def sb(name, shape, dt):
            return es.enter_context(nc.sbuf_tensor(C.name(name), shape, dt))

        def ps(name, shape, dt=F32):
            return es.enter_context(nc.psum_tensor(C.name(name), shape, dt))
        NC = 2832
        wbf = sb("w_in", [128, 8, NC], BF16)
        load_w_bf16(C, wbf, w_in_d, NC, wbf.name)
        modr = sb("modr", [128, 4, D], F32)
        for r, (row, k) in enumerate(((0, 1), (0, 0), (1, 1), (1, 0))):
            P.dma(modr[:, r, :], modd[row, k].partition_broadcast(128), reads=["modd"], writes=[modr.name + ".%d" % r])
        gb = sb("gb", [128, 16], F32)
        P.dma(gb[:], gate_b_d.partition_broadcast(128), writes=[gb.name])
        tri = sb("tri", [128, 3, 128], F32)
        P.dma(tri[:], C.consts_d[2:5].rearrange("k p n -> p k n"), writes=["c_tri"])
        cm = (tri[:, 0, :], tri[:, 1, :], tri[:, 2, :])
        cmats = (V(cm[0], "c_tri"), V(cm[1], "c_tri"), V(cm[2], "c_tri"))
        pb = [ps("pb%d" % i, [128, 512]) for i in range(7)]
        pcum = ps("pcum", [128, 512])
        xt = sb("xt", [128, D], F32)
        hl = sb("hl", [128, D], F32)
        hT = sb("hT", [128, 8, 128], BF16)
        Pj = sb("Pj", [128, NC], F32)
        G16 = sb("G16", [128, 16], F32)
        LF = sb("LF", [128, 8], F32)
        LI = sb("LI", [128, 8], F32)
        CUM = sb("CUM", [128, 8], F32)
        TOT = sb("TOT", [128, 8], F32)
        EB = sb("EB", [128, 8], F32)
        EA = sb("EA", [128, 8], F32)
        qk = sb("qk", [128, 2, 2, 4, 128], F32)
        qkT = sb("qkT", [128, 16, 128], BF16)
        ktb = sb("ktb", [128, 8, 128], BF16)
        vab = sb("vab", [128, 4, 129], BF16)
        P.op("dve", lambda e: e.memset(vab[:], 1.0), writes=[vab.name])
        rope = sb("rope", [128, 2, 32], F32)
        qa = sb("qa", [128, 10, 64], F32)
        rt = sb("rt", [128, 4, 10, 32], F32)
        qaT = sb("qaT", [128, 5, 128], BF16)
        vaa = sb("vaa", [128, 2, 65], BF16)
        P.op("dve", lambda e: e.memset(vaa[:], 1.0), writes=[vaa.name])
        for tile in range(NT):
            is_ctx = tile < 2
            if is_ctx:
                src = (ctx_d[tile * 128:(tile + 1) * 128, :], "%s.%d" % (ckey, tile))
            else:
                src = (x_d[(tile - 2) * 128:(tile - 1) * 128, :], "%s.%d" % (xkey, tile - 2))
            emit_in_proj(C, es, src, (2, 3) if is_ctx else (0, 1), wbf, wbf.name, NC, Pj, xt, hl, hT, pb, modr)
            tk = "ab.%d" % tile
            P.op("dve", lambda e: e.tensor_tensor(out=G16[:], in0=Pj[:, 2048:2064], in1=gb[:], op=ALU.add), reads=[Pj.name, gb.name], writes=[G16.name])
            gv = G16[:].rearrange("p (d g h) -> p d g h", d=2, g=2)
            P.op("dve", lambda e: e.tensor_copy(out=LI[:].rearrange("p (d h) -> p d h", d=2), in_=gv[:, :, 0, :]), reads=[G16.name], writes=[LI.name])
            P.op("act", lambda e: e.activation(out=LF[:].rearrange("p (d h) -> p d h", d=2), in_=gv[:, :, 1, :], func=AF.Exp, scale=-1.0), reads=[G16.name], writes=[LF.name])
            P.op("dve", lambda e: e.tensor_single_scalar(out=LF[:], in_=LF[:], scalar=1.0, op=ALU.add), reads=[LF.name], writes=[LF.name])
            P.op("act", lambda e: e.activation(out=LF[:], in_=LF[:], func=AF.Ln), reads=[LF.name], writes=[LF.name])
            P.op("dve", lambda e: e.tensor_single_scalar(out=LF[:], in_=LF[:], scalar=-1.0, op=ALU.mult), reads=[LF.name], writes=[LF.name])
            emit_decay_prep(C, (CUM, TOT), LF[:], LI[:], LF.name, 4, cmats, pcum, True)
            P.dma(S["ET"][tile], TOT[:], reads=[TOT.name], writes=[tk + ".et"])
            P.op("act", lambda e: e.activation(out=EB[:], in_=CUM[:], func=AF.Exp), reads=[CUM.name], writes=[EB.name])
            P.op("dve", lambda e: e.tensor_single_scalar(out=EB[:], in_=EB[:], scalar=128.0 ** -0.5, op=ALU.mult), reads=[EB.name], writes=[EB.name])
            P.op("dve", lambda e: e.tensor_tensor(out=EA[:], in0=LI[:], in1=CUM[:], op=ALU.subtract), reads=[LI.name, CUM.name], writes=[EA.name])
            P.op("act", lambda e: e.activation(out=EA[:], in_=EA[:], func=AF.Exp), reads=[EA.name], writes=[EA.name])
            for w, (E_, c0) in enumerate(((EB, 0), (EA, 512))):
                for d in range(2):
                    P.op("dve" if d == 0 else "pool", lambda e, w=w, d=d, E_=E_, c0=c0: e.tensor_tensor(
                        out=qk[:, w, d], in0=Pj[:, c0:c0 + 512].rearrange("p (h k) -> p h k", h=4),
                        in1=E_[:, d * 4:(d + 1) * 4].unsqueeze(2).to_broadcast([128, 4, 128]), op=ALU.mult),
                        reads=[Pj.name, E_.name], writes=[qk.name + ".%d%d" % (w, d)])
            for s in range(16):
                w, d, h = s // 8, (s // 4) % 2, s % 4
                pp = pb[s // 4]
                P.op("pe", lambda e, s=s, w=w, d=d, h=h, pp=pp: e.transpose(out=pp[:, (s % 4) * 128:(s % 4 + 1) * 128], in_=qk[:, w, d, h, :], identity=C.ident[:]),
                     reads=[qk.name + ".%d%d" % (w, d), "c_ident"], writes=[pp.name], track=(s % 4 == 3))
            for g in range(4):
                P.op("act" if g % 2 == 0 else "dve", lambda e, g=g: (e.copy if g % 2 == 0 else e.tensor_copy)(out=qkT[:, g * 4:(g + 1) * 4, :], in_=pb[g][:].rearrange("p (c t) -> p c t", c=4)),
                     reads=[pb[g].name], writes=[qkT.name + ".%d" % g])
            P.dma(S["QKT"][tile], qkT[:], reads=[qkT.name + ".%d" % g for g in range(4)], writes=[tk + ".qkt"])
            P.op("pool", lambda e: e.tensor_copy(out=ktb[:], in_=qk[:, 1].rearrange("p d h k -> p (d h) k")), reads=[qk.name + ".10", qk.name + ".11"], writes=[ktb.name])
            P.dma(S["KT"][tile], ktb[:], reads=[ktb.name], writes=[tk + ".kt"])
            P.op("pool", lambda e: e.tensor_copy(out=vab[:, :, 0:128], in_=Pj[:, 1024:1536].rearrange("p (h k) -> p h k", h=4)), reads=[Pj.name], writes=[vab.name])
            P.dma(S["VA"][tile], vab[:], reads=[vab.name], writes=[tk + ".va"])
            P.dma(S["OM"][tile], Pj[:, 1536:2048], reads=[Pj.name], writes=[tk + ".om"])
            qsrc = Pj[:, 2064:2704].rearrange("p (h k) -> p h k", h=10)
            qperm = qa[:, 0:8, :].rearrange("p (j a) k -> p a j k", a=2)
            if is_ctx:
                P.op("dve", lambda e: e.tensor_copy(out=qperm, in_=qsrc[:, 0:8, :].rearrange("p (a j) k -> p a j k", a=2)), reads=[Pj.name], writes=[qa.name])
                P.op("dve", lambda e: e.tensor_copy(out=qa[:, 8:10, :], in_=qsrc[:, 8:10, :]), reads=[Pj.name], writes=[qa.name])
            else:
                P.dma(rope[:], rope_d[tile - 2], writes=[rope.name])
                x1 = qsrc.rearrange("p h (i two) -> p h i two", two=2)[:, :, :, 0]
                x2 = qsrc.rearrange("p h (i two) -> p h i two", two=2)[:, :, :, 1]
                cs = rope[:, 0, :].unsqueeze(1).to_broadcast([128, 10, 32])
                sn = rope[:, 1, :].unsqueeze(1).to_broadcast([128, 10, 32])
                qo = qa[:].rearrange("p h (i two) -> p h i two", two=2)
                for j, (xa, tb_) in enumerate(((x1, cs), (x2, sn), (x1, sn), (x2, cs))):
                    P.op("dve" if j % 2 == 0 else "pool", lambda e, j=j, xa=xa, tb_=tb_: e.tensor_tensor(out=rt[:, j], in0=xa, in1=tb_, op=ALU.mult),
                         reads=[Pj.name, rope.name], writes=[rt.name + ".%d" % j])
                qpo = qperm.rearrange("p a j (i two) -> p a j i two", two=2)
                for two, (ra, rb, op_) in enumerate(((0, 1, ALU.subtract), (2, 3, ALU.add))):
                    P.op("dve", lambda e, two=two, ra=ra, rb=rb, op_=op_: e.tensor_tensor(out=qpo[:, :, :, :, two], in0=rt[:, ra, 0:8].rearrange("p (a j) i -> p a j i", a=2),
                                                                                  in1=rt[:, rb, 0:8].rearrange("p (a j) i -> p a j i", a=2), op=op_),
                         reads=[rt.name + ".%d" % ra, rt.name + ".%d" % rb], writes=[qa.name])
                    P.op("dve", lambda e, two=two, ra=ra, rb=rb, op_=op_: e.tensor_tensor(out=qo[:, 8:10, :, two], in0=rt[:, ra, 8:10], in1=rt[:, rb, 8:10], op=op_),
                         reads=[rt.name + ".%d" % ra, rt.name + ".%d" % rb], writes=[qa.name])
            pp = pb[4]
            pq = pb[5]
            for j in range(4):
                P.op("pe", lambda e, j=j, pp=pp: e.transpose(out=pp[:, j * 128:(j + 1) * 128], in_=qa[:].rearrange("p h k -> p (h k)")[:, j * 128:(j + 1) * 128], identity=C.ident[:]),
                     reads=[qa.name, "c_ident"], writes=[pp.name], track=(j == 3))
            P.op("pe", lambda e, pq=pq: e.transpose(out=pq[:, 0:128], in_=qa[:].rearrange("p h k -> p (h k)")[:, 512:640], identity=C.ident[:]), reads=[qa.name, "c_ident"], writes=[pq.name])
            P.op("act", lambda e, pp=pp: e.copy(out=qaT[:, 0:4, :], in_=pp[:].rearrange("p (c t) -> p c t", c=4)), reads=[pp.name], writes=[qaT.name])
            P.op("act", lambda e, pq=pq: e.copy(out=qaT[:, 4, :], in_=pq[:, 0:128]), reads=[pq.name], writes=[qaT.name])
            P.dma(S["QAT"][tile], qaT[:], reads=[qaT.name], writes=[tk + ".qat"])
            P.op("pool", lambda e: e.tensor_copy(out=vaa[:, :, 0:64], in_=Pj[:, 2704:2832].rearrange("p (h k) -> p h k", h=2)), reads=[Pj.name], writes=[vaa.name])
            P.dma(S["VAA"][tile], vaa[:], reads=[vaa.name], writes=[tk + ".vaa"])
    C.P.barrier()


def emit_ab_attn(C, NL, S, sink_d, AO, aokey):
    nc, P = C.nc, C.P
    NT = 2 + NL
    with ExitStack() as es:
        def sb(name, shape, dt):
            return es.enter_context(nc.sbuf_tensor(C.name(name), shape, dt))

        def ps(name, shape, dt=F32):
            return es.enter_context(nc.psum_tensor(C.name(name), shape, dt))
        kT = sb("kT_all", [128, NT, 128], BF16)
        va = sb("va_all", [128, NT, 2, 65], BF16)
        for t in range(NT):
            P.dma(kT[:, t, :], S["QAT"][t][:, 4, :], reads=["ab.%d.qat" % t], writes=[kT.name + ".%d" % t])
            P.dma(va[:, t], S["VAA"][t], reads=["ab.%d.vaa" % t], writes=[va.name + ".%d" % t])
        mk = sb("mk", [128, 2, 128], F32)
        P.dma(mk[:], C.consts_d[2:4].rearrange("k p n -> p k n"), writes=[mk.name])
        mkb = sb("mkb", [128, 2, 128], BF16)
        P.op("dve", lambda e: e.tensor_copy(out=mkb[:], in_=mk[:]), reads=[mk.name], writes=[mkb.name])
        sk = sb("sink", [128, 8], F32)
        P.dma(sk[:], sink_d.partition_broadcast(128), writes=[sk.name])
        P.op("act", lambda e: e.activation(out=sk[:], in_=sk[:], func=AF.Exp), reads=[sk.name], writes=[sk.name])
        qt = [sb("qt%d" % i, [128, 4, 128], BF16) for i in range(2)]
        E = [sb("E%d" % i, [128, 5, 128], BF16) for i in range(2)]
        pE = [ps("pE%d" % i, [128, 1024]) for i in range(2)]
        pO = ps("pO", [128, 2, 512])
        den = sb("den", [128, 8, 2], F32)
        ao = [sb("ao%d" % i, [128, 8, 64], F32) for i in range(2)]
        for qtile in range(NT):
            if qtile < 2:
                blocks = [(0, None), (1, None)]
            else:
                n = qtile - 2
                blocks = [(0, None), (1, None)]
                if n >= 1:
                    blocks.append((qtile - 1, 1))
                blocks.append((qtile, None))
                if n + 1 < NL:
                    blocks.append((qtile + 1, 0))
            nb = len(blocks)
            q = qt[qtile % 2]
            P.dma(q[:], S["QAT"][qtile][:, 0:4, :], reads=["ab.%d.qat" % qtile], writes=[q.name])
            for hq in range(8):
                j, half = hq % 4, hq // 4
                p0, p1 = half * 64, (half + 1) * 64
                pe_ = pE[hq % 2]
                Eb = E[hq % 2]
                for bi, (blk, _) in enumerate(blocks):
                    P.op("pe", lambda e, bi=bi, blk=blk, p0=p0, p1=p1, j=j, pe_=pe_, q=q: e.matmul(pe_[:, bi * 128:(bi + 1) * 128], lhsT=kT[p0:p1, blk, :], rhs=q[p0:p1, j, :], start=True, stop=True),
                         reads=[kT.name + ".%d" % blk, q.name], writes=[pe_.name], track=(bi == nb - 1))
                P.op("act", lambda e, pe_=pe_, Eb=Eb, nb=nb: e.activation(out=Eb[:, 0:nb, :], in_=pe_[:, 0:nb * 128].rearrange("p (b t) -> p b t", b=nb), func=AF.Exp, scale=0.125),
                     reads=[pe_.name], writes=[Eb.name])
                for bi, (blk, m) in enumerate(blocks):
                    if m is not None:
                        P.op("dve" if m == 0 else "pool", lambda e, bi=bi, m=m, Eb=Eb: e.tensor_tensor(out=Eb[:, bi, :], in0=Eb[:, bi, :], in1=mkb[:, m, :], op=ALU.mult),
                             reads=[Eb.name, mkb.name], writes=[Eb.name])
                o = pO[:, hq // 4, (hq % 4) * 65:(hq % 4) * 65 + 65]
                for bi, (blk, _) in enumerate(blocks):
                    P.op("pe", lambda e, bi=bi, blk=blk, half=half, Eb=Eb, o=o: e.matmul(o, lhsT=Eb[:, bi, :], rhs=va[:, blk, half, :], start=(bi == 0), stop=(bi == nb - 1)),
                         reads=[Eb.name, va.name + ".%d" % blk], writes=["pO"], track=(bi == nb - 1))
            pv = pO[:, :, 0:260].rearrange("p a (h x) -> p a h x", h=4)
            a_ = ao[qtile % 2]
            dv = den[:].rearrange("p (a h) x -> p a h x", a=2)
            P.op("dve", lambda e: e.tensor_tensor(out=dv[:, :, :, 0:1], in0=pv[:, :, :, 64:65], in1=sk[:].rearrange("p (a h) -> p a h", a=2).unsqueeze(3), op=ALU.add),
                 reads=["pO", sk.name], writes=[den.name])
            P.op("dve", lambda e: e.reciprocal(out=den[:, :, 1:2], in_=den[:, :, 0:1]), reads=[den.name], writes=[den.name])
            P.op("dve", lambda e, a_=a_: e.tensor_tensor(out=a_[:].rearrange("p (a h) x -> p a h x", a=2), in0=pv[:, :, :, 0:64],
                                                         in1=dv[:, :, :, 1:2].to_broadcast([128, 2, 4, 64]), op=ALU.mult),
                 reads=["pO", den.name], writes=[a_.name])
            P.dma(AO[qtile], a_[:].rearrange("p h x -> p (h x)"), reads=[a_.name], writes=["%s.%d" % (aokey, qtile)])
    C.P.barrier()


def head_norm_rows(C, hm, sq, stt, nheads, dh, hk):
    P = C.P
    P.op("pool", lambda e: e.tensor_tensor(out=sq[:], in0=hm[:], in1=hm[:], op=ALU.mult), reads=[hk], writes=[sq.name])
    P.op("dve", lambda e: e.reduce_sum(out=stt[:, 0:nheads], in_=sq[:], axis=AX.X), reads=[sq.name], writes=[stt.name])
    P.op("dve", lambda e: e.tensor_scalar(out=stt[:, 0:nheads], in0=stt[:, 0:nheads], scalar1=1.0 / dh, scalar2=LN_EPS, op0=ALU.mult, op1=ALU.add), reads=[stt.name], writes=[stt.name])
    P.op("act", lambda e: e.activation(out=stt[:, 0:nheads], in_=stt[:, 0:nheads], func=AF.Sqrt), reads=[stt.name], writes=[stt.name])
    P.op("dve", lambda e: e.reciprocal(out=stt[:, 0:nheads], in_=stt[:, 0:nheads]), reads=[stt.name], writes=[stt.name])
    P.op("dve", lambda e: e.tensor_tensor(out=hm[:], in0=hm[:], in1=stt[:, 0:nheads].unsqueeze(2).to_broadcast([128, nheads, dh]), op=ALU.mult), reads=[hk, stt.name], writes=[hk])


def emit_merge(C, NL, n_ctx_out, kind, S, HO, hokey, AO, aokey, x_d, xkey, ctx_d, ckey, modd, norm_w_d, w_out_d, lnw_d, lnb_d, x1_d, x1key, c1_d, c1key):
    nc, P = C.nc, C.P
    NT = 2 + NL
    with ExitStack() as es:
        def sb(name, shape, dt):
            return es.enter_context(nc.sbuf_tensor(C.name(name), shape, dt))

        def ps(name, shape, dt=F32):
            return es.enter_context(nc.psum_tensor(C.name(name), shape, dt))
        wbf = sb("w_out", [128, 8, D], BF16)
        load_w_bf16(C, wbf, w_out_d, D, wbf.name)
        nw = D // 2 if kind == "ab" else D
        normw = sb("normw", [128, nw], F32)
        P.dma(normw[:], norm_w_d.partition_broadcast(128), writes=[normw.name])
        g1 = sb("g1", [128, 2, D], F32)
        for r in range(2):
            P.dma(g1[:, r, :], modd[r, 2].partition_broadcast(128), reads=["modd"], writes=[g1.name + ".%d" % r])
        lnw = sb("lnw", [128, D], F32)
        lnb = sb("lnb", [128, D], F32)
        P.dma(lnw[:], lnw_d.partition_broadcast(128), writes=[lnw.name])
        P.dma(lnb[:], lnb_d.partition_broadcast(128), writes=[lnb.name])
        h0 = sb("h0", [128, nw], F32)
        h1 = sb("h1", [128, nw], F32)
        sq = sb("sq", [128, nw], F32)
        gt = sb("gt", [128, nw], F32)
        cat = sb("cat", [128, D], F32)
        catT = sb("catT", [128, 8, 128], BF16)
        xt = sb("xt", [128, D], F32)
        ytmp = sb("ytmp", [128, D], F32)
        yo = sb("yo", [128, D], F32)
        stt = sb("stt", [128, 4], F32)
        st = sb("st", [128, 4], F32)
        lt = sb("lt", [128, D], F32)
        pb = [ps("pbm%d" % i, [128, 1024]) for i in range(2)]
        first = 0 if n_ctx_out else 2
        for tile in range(first, NT):
            is_ctx = tile < 2
            P.dma(h0[:], HO[0][tile].rearrange("p h x -> p (h x)"), reads=["%s.ho0.%d" % (hokey, tile)], writes=[h0.name])
            P.dma(h1[:], HO[1][tile].rearrange("p h x -> p (h x)"), reads=["%s.ho1.%d" % (hokey, tile)], writes=[h1.name])
            P.dma(gt[:], S["OM"][tile], reads=["%s.%d.om" % (kind, tile)], writes=[gt.name])
            P.op("dve", lambda e: e.tensor_tensor(out=h0[:], in0=h0[:], in1=h1[:], op=ALU.add), reads=[h0.name, h1.name], writes=[h0.name])
            dh = nw // 4
            hv = V(h0[:].rearrange("p (h x) -> p h x", h=4), h0.name)
            sv = V(sq[:].rearrange("p (h x) -> p h x", h=4), sq.name)
            head_norm_rows(C, hv, sv, stt, 4, dh, h0.name)
            P.op("dve", lambda e: e.tensor_tensor(out=h0[:], in0=h0[:], in1=normw[:], op=ALU.mult), reads=[h0.name, normw.name], writes=[h0.name])
            P.op("act", lambda e: e.activation(out=gt[:], in_=gt[:], func=(AF.Sigmoid if kind == "ab" else AF.Silu)), reads=[gt.name], writes=[gt.name])
            P.op("dve", lambda e: e.tensor_tensor(out=cat[:, 0:nw], in0=h0[:], in1=gt[:], op=ALU.mult), reads=[h0.name, gt.name], writes=[cat.name + ".0"])
            rk = [cat.name + ".0"]
            if kind == "ab":
                P.dma(cat[:, nw:D], AO[tile], reads=["%s.%d" % (aokey, tile)], writes=[cat.name + ".1"])
                rk.append(cat.name + ".1")
            for dc in range(8):
                P.op("pe", lambda e, dc=dc: e.transpose(out=pb[0][:, dc * 128:(dc + 1) * 128], in_=cat[:, dc * 128:(dc + 1) * 128], identity=C.ident[:]),
                     reads=rk + ["c_ident"], writes=[pb[0].name], track=(dc == 7))
            P.op("act", lambda e: e.copy(out=catT[:], in_=pb[0][:].rearrange("p (c t) -> p c t", c=8)), reads=[pb[0].name], writes=[catT.name])
            for hf in range(2):
                for dc in range(8):
                    P.op("pe", lambda e, dc=dc, hf=hf: e.matmul(pb[1][:, hf * 512:(hf + 1) * 512], lhsT=catT[:, dc, :], rhs=wbf[:, dc, hf * 512:(hf + 1) * 512], start=(dc == 0), stop=(dc == 7)),
                         reads=[catT.name, "%s.%d" % (wbf.name, dc)], writes=[pb[1].name], track=(dc == 7 and hf == 1))
            if is_ctx:
                src, skey, dst, dkey = ctx_d[tile * 128:(tile + 1) * 128, :], "%s.%d" % (ckey, tile), c1_d[tile * 128:(tile + 1) * 128, :], "%s.%d" % (c1key, tile)
            else:
                src, skey, dst, dkey = x_d[(tile - 2) * 128:(tile - 1) * 128, :], "%s.%d" % (xkey, tile - 2), x1_d[(tile - 2) * 128:(tile - 1) * 128, :], "%s.%d" % (x1key, tile - 2)
            r = 1 if is_ctx else 0
            P.dma(xt[:], src, reads=[skey], writes=[xt.name])
            P.op("dve", lambda e, r=r: e.tensor_tensor(out=ytmp[:], in0=pb[1][:], in1=g1[:, r, :], op=ALU.mult), reads=[pb[1].name, g1.name + ".%d" % r], writes=[ytmp.name])
            P.op("dve", lambda e: e.scalar_tensor_tensor(out=ytmp[:], in0=xt[:], scalar=ALPHA, in1=ytmp[:], op0=ALU.mult, op1=ALU.add), reads=[xt.name, ytmp.name], writes=[ytmp.name])
            layernorm_rows(P, nc, ytmp, None, lt, st, lnw, lnb, "m", yo)
            P.dma(dst, yo[:], reads=[yo.name], writes=[dkey])
    C.P.barrier()


def make_consts():
    i = np.arange(128)
    return np.stack([np.eye(128), np.tile(i.astype(np.float64), (128, 1)), (i[:, None] <= i[None, :]), (i[:, None] >= i[None, :]), np.ones((128, 128))]).astype(np.float32)


def make_rope(seq):
    rows = seq // 64
    row = np.repeat(np.arange(rows), 64).astype(np.float32)
    col = np.tile(np.arange(64), rows).astype(np.float32)
    inv = (10000.0 ** (-np.arange(16, dtype=np.float32) / 16)).astype(np.float32)
    ang = np.concatenate([row[:, None] * inv, col[:, None] * inv], -1).astype(np.float32)
    r = np.stack([np.cos(ang), np.sin(ang)], 1).astype(np.float32)
    return np.ascontiguousarray(r.reshape(seq // 128, 128, 2, 32))


def ab_scratch(nc, NT, pfx):
    S = {}
    S["QKT"] = nc.dram_tensor(pfx + "QKT", [NT, 128, 16, 128], BF16).ap()
    S["KT"] = nc.dram_tensor(pfx + "KT", [NT, 128, 8, 128], BF16).ap()
    S["VA"] = nc.dram_tensor(pfx + "VA", [NT, 128, 4, 129], BF16).ap()
    S["ET"] = nc.dram_tensor(pfx + "ET", [NT, 128, 8], F32).ap()
    S["OM"] = nc.dram_tensor(pfx + "OM", [NT, 128, 512], F32).ap()
    S["QAT"] = nc.dram_tensor(pfx + "QAT", [NT, 128, 5, 128], BF16).ap()
    S["VAA"] = nc.dram_tensor(pfx + "VAA", [NT, 128, 2, 65], BF16).ap()
    S["HO"] = [nc.dram_tensor(pfx + "HO%d" % d, [NT, 128, 4, 128], F32).ap() for d in range(2)]
    S["AO"] = nc.dram_tensor(pfx + "AO", [NT, 128, 512], F32).ap()
    return S


def emit_layer0_mixer(C, NL, x_d, xkey, ctx_d, ckey, modd, W, x1_d, x1key, c1_d, c1key):
    nc = C.nc
    NT = 2 + NL
    S = ab_scratch(nc, NT, "ab_")
    emit_ab_stage_a(C, NL, x_d, xkey, ctx_d, ckey, modd, W["ab_w_in"], W["ab_gate_b"], W["rope"], S)
    emit_scan(C, NT, 2, S["QKT"], S["KT"], S["VA"], S["ET"], S["HO"], 129, True, C.consts_d[2:4], "abs")
    emit_ab_attn(C, NL, S, W["ab_sink"], S["AO"], "ab.ao")
    emit_merge(C, NL, True, "ab", S, S["HO"], "abs", S["AO"], "ab.ao", x_d, xkey, ctx_d, ckey, modd, W["ab_norm_w"], W["ab_w_out"], W["lnw0"], W["lnb0"], x1_d, x1key, c1_d, c1key)


def emit_c_stage_a(C, NL, x_d, xkey, ctx_d, ckey, modd, w_in_d, gate_up_d, gate_b_d, S):
    nc, P = C.nc, C.P
    NT = 2 + NL
    with ExitStack() as es:
        def sb(name, shape, dt):
            return es.enter_context(nc.sbuf_tensor(C.name(name), shape, dt))

        def ps(name, shape, dt=F32):
            return es.enter_context(nc.psum_tensor(C.name(name), shape, dt))
        NC = 3104
        wbf = sb("w_in", [128, 8, NC], BF16)
        load_w_bf16(C, wbf, w_in_d, NC, wbf.name)
        modr = sb("modr", [128, 4, D], F32)
        for r, (row, k) in enumerate(((0, 1), (0, 0), (1, 1), (1, 0))):
            P.dma(modr[:, r, :], modd[row, k].partition_broadcast(128), reads=["modd"], writes=[modr.name + ".%d" % r])
        gb = sb("gb", [128, 1024], F32)
        P.dma(gb[:], gate_b_d.partition_broadcast(128), writes=[gb.name])
        gup = sb("gup", [16, 2, 512], F32)
        P.dma(gup[:], gate_up_d.rearrange("d r c -> r d c"), writes=[gup.name])
        tri = sb("tri", [128, 3, 128], F32)
        P.dma(tri[:], C.consts_d[2:5].rearrange("k p n -> p k n"), writes=["c_tri"])
        pb = [ps("pb%d" % i, [128, 512]) for i in range(7)]
        pcum = ps("pcum", [128, 512])
        xt = sb("xt", [128, D], F32)
        hl = sb("hl", [128, D], F32)
        hT = sb("hT", [128, 8, 128], BF16)
        Pj = sb("Pj", [128, NC], F32)
        lowT = sb("lowT", [16, 2, 128], F32)
        LA = sb("LA", [128, 1024], F32)
        CUM = sb("CUM", [128, 1024], F32)
        EB = sb("EB", [128, 1024], F32)
        EA = sb("EA", [128, 1024], F32)
        ETt = sb("ETt", [128, 8], F32)
        qk = sb("qk", [128, 2, 2, 4, 128], F32)
        qkT = sb("qkT", [128, 16, 128], BF16)
        ktb = sb("ktb", [128, 8, 128], BF16)
        vab = sb("vab", [128, 4, 256], BF16)
        for tile in range(NT):
            is_ctx = tile < 2
            if is_ctx:
                src = (ctx_d[tile * 128:(tile + 1) * 128, :], "%s.%d" % (ckey, tile))
            else:
                src = (x_d[(tile - 2) * 128:(tile - 1) * 128, :], "%s.%d" % (xkey, tile - 2))
            emit_in_proj(C, es, src, (2, 3) if is_ctx else (0, 1), wbf, wbf.name, NC, Pj, xt, hl, hT, pb, modr)
            tk = "c.%d" % tile
            for d in range(2):
                P.op("pe", lambda e, d=d: e.transpose(out=pcum[0:16, d * 128:(d + 1) * 128], in_=Pj[:, 3072 + 16 * d:3072 + 16 * (d + 1)], identity=C.ident[:]),
                     reads=[Pj.name, "c_ident"], writes=[pcum.name], track=(d == 1))
            P.op("act", lambda e: e.copy(out=lowT[:], in_=pcum[0:16, 0:256].rearrange("p (d t) -> p d t", d=2)), reads=[pcum.name], writes=[lowT.name])
            for d in range(2):
                P.op("pe", lambda e, d=d: e.matmul(pb[d][:, :], lhsT=lowT[:, d, :], rhs=gup[:, d, :], start=True, stop=True), reads=[lowT.name, gup.name], writes=[pb[d].name])
                P.op("dve", lambda e, d=d: e.tensor_tensor(out=LA[:, d * 512:(d + 1) * 512], in0=pb[d][:, :], in1=gb[:, d * 512:(d + 1) * 512], op=ALU.add),
                     reads=[pb[d].name, gb.name], writes=[LA.name])
            P.op("act", lambda e: e.activation(out=LA[:], in_=LA[:], func=AF.Exp, scale=-1.0), reads=[LA.name], writes=[LA.name])
            P.op("dve", lambda e: e.tensor_single_scalar(out=LA[:], in_=LA[:], scalar=1.0, op=ALU.add), reads=[LA.name], writes=[LA.name])
            P.op("act", lambda e: e.activation(out=LA[:], in_=LA[:], func=AF.Ln), reads=[LA.name], writes=[LA.name])
            P.op("dve", lambda e: e.tensor_single_scalar(out=LA[:], in_=LA[:], scalar=-1.0 / 16.0, op=ALU.mult), reads=[LA.name], writes=[LA.name])
            for d in range(2):
                P.op("pe", lambda e, d=d: e.matmul(pb[2 + d][:, :], lhsT=tri[:, d, :], rhs=LA[:, d * 512:(d + 1) * 512], start=True, stop=True), reads=[LA.name, "c_tri"], writes=[pb[2 + d].name])
                P.op("act", lambda e, d=d: e.copy(out=CUM[:, d * 512:(d + 1) * 512], in_=pb[2 + d][:, :]), reads=[pb[2 + d].name], writes=[CUM.name])
            for s in range(8):
                P.op("pe", lambda e, s=s: e.matmul(pcum[:, 256 + s:257 + s], lhsT=LA[:, s * 128:(s + 1) * 128], rhs=tri[:, 2, 0:1], start=True, stop=True),
                     reads=[LA.name, "c_tri"], writes=[pcum.name], track=(s == 7))
            P.op("act", lambda e: e.activation(out=ETt[:], in_=pcum[:, 256:264], func=AF.Exp), reads=[pcum.name], writes=[ETt.name])
            P.dma(S["ET"][tile], ETt[:], reads=[ETt.name], writes=[tk + ".et"])
            P.op("act", lambda e: e.activation(out=EB[:], in_=CUM[:], func=AF.Exp), reads=[CUM.name], writes=[EB.name])
            P.op("act", lambda e: e.activation(out=EA[:], in_=CUM[:], func=AF.Exp, scale=-1.0), reads=[CUM.name], writes=[EA.name])
            P.op("pool", lambda e: e.tensor_single_scalar(out=EB[:], in_=EB[:], scalar=128.0 ** -0.5, op=ALU.mult), reads=[EB.name], writes=[EB.name])
            for w, (E_, c0) in enumerate(((EB, 0), (EA, 512))):
                for d in range(2):
                    P.op("dve" if d == 0 else "pool", lambda e, w=w, d=d, E_=E_, c0=c0: e.tensor_tensor(
                        out=qk[:, w, d].rearrange("p h k -> p (h k)"), in0=Pj[:, c0:c0 + 512], in1=E_[:, d * 512:(d + 1) * 512], op=ALU.mult),
                        reads=[Pj.name, E_.name], writes=[qk.name + ".%d%d" % (w, d)])
            for s in range(16):
                w, d, h = s // 8, (s // 4) % 2, s % 4
                pp = pb[s // 4]
                P.op("pe", lambda e, s=s, w=w, d=d, h=h, pp=pp: e.transpose(out=pp[:, (s % 4) * 128:(s % 4 + 1) * 128], in_=qk[:, w, d, h, :], identity=C.ident[:]),
                     reads=[qk.name + ".%d%d" % (w, d), "c_ident"], writes=[pp.name], track=(s % 4 == 3))
            for g in range(4):
                P.op("act" if g % 2 == 0 else "dve", lambda e, g=g: (e.copy if g % 2 == 0 else e.tensor_copy)(out=qkT[:, g * 4:(g + 1) * 4, :], in_=pb[g][:].rearrange("p (c t) -> p c t", c=4)),
                     reads=[pb[g].name], writes=[qkT.name + ".%d" % g])
            P.dma(S["QKT"][tile], qkT[:], reads=[qkT.name + ".%d" % g for g in range(4)], writes=[tk + ".qkt"])
            P.op("pool", lambda e: e.tensor_copy(out=ktb[:], in_=qk[:, 1].rearrange("p d h k -> p (d h) k")), reads=[qk.name + ".10", qk.name + ".11"], writes=[ktb.name])
            P.dma(S["KT"][tile], ktb[:], reads=[ktb.name], writes=[tk + ".kt"])
            P.op("pool", lambda e: e.tensor_copy(out=vab[:], in_=Pj[:, 1024:2048].rearrange("p (h k) -> p h k", h=4)), reads=[Pj.name], writes=[vab.name])
            P.dma(S["VA"][tile], vab[:], reads=[vab.name], writes=[tk + ".va"])
            P.dma(S["OM"][tile], Pj[:, 2048:3072], reads=[Pj.name], writes=[tk + ".om"])
    C.P.barrier()


def c_scratch(nc, NT, pfx):
    S = {}
    S["QKT"] = nc.dram_tensor(pfx + "QKT", [NT, 128, 16, 128], BF16).ap()
    S["KT"] = nc.dram_tensor(pfx + "KT", [NT, 128, 8, 128], BF16).ap()
    S["VA"] = nc.dram_tensor(pfx + "VA", [NT, 128, 4, 256], BF16).ap()
    S["ET"] = nc.dram_tensor(pfx + "ET", [NT, 128, 8], F32).ap()
    S["OM"] = nc.dram_tensor(pfx + "OM", [NT, 128, 1024], F32).ap()
    S["HO"] = [nc.dram_tensor(pfx + "HO%d" % d, [NT, 128, 4, 256], F32).ap() for d in range(2)]
    return S


def emit_layer1_mixer(C, NL, x_d, xkey, ctx_d, ckey, modd, W, x1_d, x1key):
    nc = C.nc
    NT = 2 + NL
    S = c_scratch(nc, NT, "c_")
    emit_c_stage_a(C, NL, x_d, xkey, ctx_d, ckey, modd, W["gla_w_in"], W["gla_gate_up"], W["gla_gate_b"], S)
    emit_scan(C, NT, 2, S["QKT"], S["KT"], S["VA"], S["ET"], S["HO"], 256, False, C.consts_d[2:4], "cs")
    emit_merge(C, NL, False, "c", S, S["HO"], "cs", None, None, x_d, xkey, ctx_d, ckey, modd, W["gla_norm_w"], W["gla_w_out"], W["lnw0"], W["lnb0"], x1_d, x1key, None, None)


SEQ = 4096
NLAT = SEQ // 128


def build_full(NL=NLAT, peer_groups=None):
    nc = bass.Bass("TRN2", target_bir_lowering=False)
    P = Prog(nc)
    T = NL * 128

    def din(name, shape):
        return nc.dram_tensor(name, shape, F32, kind="ExternalInput").ap()

    def dscr(name, shape, dt=F32):
        return nc.dram_tensor(name, shape, dt).ap()
    consts = din("consts", [5, 128, 128])
    x = din("x", [T, D])
    ctx = din("ctx", [256, D])
    ccT = din("ccT", [128, 8, 2])
    wmod = din("w_mod", [2, D, 6 * D])
    bmod = din("b_mod", [2, 6 * D])
    lnw = din("ln_w", [2, 2, D])
    lnb = din("ln_b", [2, 2, D])
    W0 = dict(ab_w_in=din("ab_w_in", [D, 2832]), ab_gate_b=din("ab_gate_b", [16]), ab_norm_w=din("ab_norm_w", [512]), ab_sink=din("ab_sink", [8]),
              ab_w_out=din("ab_w_out", [D, D]), rope=din("rope", [NL, 128, 2, 32]), lnw0=lnw[0, 0], lnb0=lnb[0, 0])
    W1 = dict(gla_w_in=din("gla_w_in", [D, 3104]), gla_gate_up=din("gla_gate_up", [2, 16, 512]), gla_gate_b=din("gla_gate_b", [1024]), gla_norm_w=din("gla_norm_w", [D]),
              gla_w_out=din("gla_w_out", [D, D]), lnw0=lnw[1, 0], lnb0=lnb[1, 0])
    wq = din("peer_wq", [2, D, 2048])
    keys = din("peer_keys", [2, 16, 128, 128])
    ure = din("peer_ure", [2, 128, 128, 8, 128])
    pv = din("peer_v", [2, 16384, D])
    out = nc.dram_tensor("out", [T, D], F32, kind="ExternalOutput").ap()
    modd = [dscr("modd%d" % l, [2, 6, D]) for l in range(2)]
    x1 = dscr("x1", [T, D]); c1 = dscr("c1", [256, D])
    x2 = dscr("x2", [T, D]); c2 = dscr("c2", [256, D])
    x3 = dscr("x3", [T, D])
    ubf = dscr("ubf", [128, 128, 8, 128], BF16)
    vbf = dscr("vbf", [16384, D], BF16)
    C = Ctx(nc, P, consts)
    emit_peer_prep(C, ure[0], pv[0], ubf, vbf)
    emit_mod(C, ccT, wmod[0], bmod[0], modd[0])
    emit_layer0_mixer(C, NL, x, "x", ctx, "ctx", modd[0], W0, x1, "x1", c1, "c1")
    ng = None if peer_groups is None else peer_groups
    emit_peer(C, T, x1, "x1", x2, "x2", [modd[0][0, 4], modd[0][0, 3], modd[0][0, 5]], lnw[0, 1], lnb[0, 1], wq[0], keys[0], ure[0], pv[0], ubf, vbf, n_groups=ng)
    emit_peer(C, 256, c1, "c1", c2, "c2", [modd[0][1, 4], modd[0][1, 3], modd[0][1, 5]], lnw[0, 1], lnb[0, 1], wq[0], keys[0], ure[0], pv[0], ubf, vbf)
    emit_peer_prep(C, ure[1], pv[1], ubf, vbf)
    emit_mod(C, ccT, wmod[1], bmod[1], modd[1])
    emit_layer1_mixer(C, NL, x2, "x2", c2, "c2", modd[1], W1, x3, "x3")
    emit_peer(C, T, x3, "x3", out, "out", [modd[1][0, 4], modd[1][0, 3], modd[1][0, 5]], lnw[1, 1], lnb[1, 1], wq[1], keys[1], ure[1], pv[1], ubf, vbf, n_groups=ng)
    P.finish()
    return nc


def make_feeds(inputs, NL=NLAT):
    T = NL * 128
    f32 = lambda a: np.ascontiguousarray(np.asarray(a, dtype=np.float32))
    consts = make_consts()
    rope = make_rope(T)
    pu = np.asarray(inputs["peer_u"], dtype=np.float32)
    ure = np.ascontiguousarray(pu.reshape(2, 128, 128, 8, 128).transpose(0, 1, 4, 3, 2))
    shared = dict(consts=consts, rope=rope, w_mod=f32(inputs["w_mod"]), b_mod=f32(inputs["b_mod"]), ln_w=f32(inputs["ln_w"]), ln_b=f32(inputs["ln_b"]),
                  ab_w_in=f32(inputs["ab_w_in"][0]), ab_gate_b=f32(np.asarray(inputs["ab_gate_b"][0]).reshape(16)), ab_norm_w=f32(inputs["ab_norm_w"][0]),
                  ab_sink=f32(inputs["ab_sink"][0]), ab_w_out=f32(inputs["ab_w_out"][0]), gla_w_in=f32(inputs["gla_w_in"][0]), gla_gate_up=f32(inputs["gla_gate_up"][0]),
                  gla_gate_b=f32(np.asarray(inputs["gla_gate_b"][0]).reshape(1024)), gla_norm_w=f32(inputs["gla_norm_w"][0]), gla_w_out=f32(inputs["gla_w_out"][0]),
                  peer_wq=f32(inputs["peer_wq"]), peer_keys=f32(np.asarray(inputs["peer_keys"]).reshape(2, 16, 128, 128)), peer_ure=ure, peer_v=f32(inputs["peer_v"]))
    feeds = []
    xs = np.asarray(inputs["x"], dtype=np.float32)
    cs = np.asarray(inputs["c"], dtype=np.float32)
    cx = np.asarray(inputs["ctx"], dtype=np.float32)
    cctx = np.asarray(inputs["c_ctx"], dtype=np.float32)
    for b in range(xs.shape[0]):
        cc = np.stack([cs[b], cctx], -1)
        ccT = np.ascontiguousarray(cc.reshape(8, 128, 2).transpose(1, 0, 2))
        d = dict(shared)
        d.update(x=np.ascontiguousarray(xs[b, :T]), ctx=np.ascontiguousarray(cx[b]), ccT=ccT)
        feeds.append(d)
    return feeds


_NC_CACHE = {}


def kernel(**inputs):
    if "full" not in _NC_CACHE:
        _NC_CACHE["full"] = build_full()
    nc = _NC_CACHE["full"]
    feeds = make_feeds(inputs)
    res = run_bass_kernel_spmd(nc, feeds, core_ids=list(range(len(feeds))))
    return np.stack([r["out"] for r in res.results], 0).astype(np.float32)
```

```python
from contextlib import ExitStack
import numpy as np
import concourse.bass as bass
import concourse.mybir as mybir
from concourse.bass_utils import run_bass_kernel_spmd

F32 = mybir.dt.float32
BF16 = mybir.dt.bfloat16
U32 = mybir.dt.uint32
AF = mybir.ActivationFunctionType
ALU = mybir.AluOpType
AX = mybir.AxisListType

D = 1024
NKEY = 128
PH = 8
PK = 16


class Prog:
    def __init__(self, nc, n_dma_slots=12):
        self.nc = nc
        self.eng = {"pe": nc.tensor, "act": nc.scalar, "dve": nc.vector, "pool": nc.gpsimd, "sp": nc.sync}
        self.csem = {e: nc.alloc_semaphore("c_" + e) for e in ("pe", "act", "dve", "pool")}
        self.cnt = {e: 0 for e in self.csem}
        self.dslots = {}
        for q, n in (("sp", n_dma_slots), ("pool", 6), ("act", 4)):
            self.dslots[q] = [[nc.alloc_semaphore("d_%s_%d" % (q, i)), 0] for i in range(n)]
        self.dnext = {q: 0 for q in self.dslots}
        self.seen = {e: {} for e in self.eng}
        self.lastw = {}
        self.lastr = {}
        self.multi = {}
        self.n_ops = 0

    def _wait(self, e, tok):
        if tok is None:
            return
        sem, val, src = tok
        key = sem.num if hasattr(sem, "num") else id(sem)
        if self.seen[e].get(key, 0) >= val:
            return
        self.eng[e].wait_ge(sem, val)
        self.seen[e][key] = val

    def _deps(self, e, reads, writes):
        for b in reads:
            for t in self.multi.get(b, ()):
                self._wait(e, t)
            t = self.lastw.get(b)
            if t is not None and not (t[2] == e and e == "pe"):
                self._wait(e, t)
        for b in writes:
            t = self.lastw.get(b)
            if t is not None and t[2] != e:
                self._wait(e, t)
            for t in self.lastr.get(b, ()):
                if t[2] != e:
                    self._wait(e, t)

    def _record(self, tok, reads, writes):
        for b in reads:
            self.lastr.setdefault(b, []).append(tok)
            if len(self.lastr[b]) > 6:
                best = {}
                for t in self.lastr[b]:
                    k = (t[2], t[0].num if hasattr(t[0], "num") else id(t[0]))
                    if k not in best or best[k][1] < t[1]:
                        best[k] = t
                self.lastr[b] = list(best.values())
        for b in writes:
            self.lastw[b] = tok
            self.lastr[b] = []

    def op(self, e, fn, reads=(), writes=(), track=True):
        self._deps(e, reads, writes)
        inst = fn(self.eng[e])
        self.n_ops += 1
        if track:
            self.cnt[e] += 1
            inst.then_inc(self.csem[e], 1)
            tok = (self.csem[e], self.cnt[e], e)
        else:
            tok = (self.csem[e], self.cnt[e] + 1, e)
        self._record(tok, reads, writes)
        return tok

    def dma(self, out, in_, reads=(), writes=(), q="sp", **kw):
        slots = self.dslots[q]
        i = self.dnext[q]
        self.dnext[q] = (i + 1) % len(slots)
        sem, uses = slots[i]
        if uses > 0:
            self._wait(q, (sem, 16 * uses, "dma"))
        self._deps(q, reads, writes)
        self.eng[q].dma_start(out=out, in_=in_, **kw).then_inc(sem, 16)
        self.n_ops += 1
        slots[i][1] = uses + 1
        tok = (sem, 16 * (uses + 1), "dma")
        self._record(tok, reads, writes)
        return tok

    def barrier(self):
        toks = [(self.csem[e], self.cnt[e], e) for e in self.csem if self.cnt[e] > 0]
        for q, slots in self.dslots.items():
            toks += [(sem, 16 * uses, "dma") for sem, uses in slots if uses > 0]
        for e in self.eng:
            for t in toks:
                self._wait(e, t)
        self.lastw = {k: None for k in self.lastw}
        self.lastr = {}
        self.multi = {}

    def finish(self):
        for q, slots in self.dslots.items():
            for sem, uses in slots:
                if uses > 0:
                    self._wait("sp", (sem, 16 * uses, "dma"))


def bc(ap, shape):
    return ap.to_broadcast(shape)


ALPHA = 4.0 ** 0.25
LN_EPS = 1e-5


class Ctx:
    def __init__(self, nc, P, consts_d):
        self.nc = nc
        self.P = P
        self.uid = 0
        self.consts_d = consts_d
        self.ident = nc.alloc_sbuf_tensor("c_ident", [128, 128], F32)
        self.iota = nc.alloc_sbuf_tensor("c_iota", [128, 128], F32)
        P.dma(self.ident[:], consts_d[0], writes=["c_ident"])
        P.dma(self.iota[:], consts_d[1], writes=["c_iota"])

    def name(self, s):
        self.uid += 1
        return "%s_%d" % (s, self.uid)


class V:
    def __init__(self, ap, name):
        self.ap = ap
        self.name = name

    def __getitem__(self, k):
        return self.ap if k == slice(None) else self.ap[k]


def layernorm_rows(P, nc, y, yn, tmp, st, lnw, lnb, tag, out):
    yk, tk, sk = y.name, tmp.name, st.name
    P.op("dve", lambda e: e.reduce_sum(out=st[:, 0:1], in_=y[:], axis=AX.X), reads=[yk], writes=[sk])
    P.op("dve", lambda e: e.tensor_single_scalar(out=st[:, 1:2], in_=st[:, 0:1], scalar=-1.0 / D, op=ALU.mult), reads=[sk], writes=[sk])
    P.op("dve", lambda e: e.tensor_scalar(out=y[:], in0=y[:], scalar1=st[:, 1:2], scalar2=None, op0=ALU.add), reads=[yk, sk], writes=[yk])
    P.op("act", lambda e: e.activation(out=tmp[:], in_=y[:], func=AF.Square, accum_out=st[:, 2:3]), reads=[yk], writes=[tk, sk])
    P.op("dve", lambda e: e.tensor_scalar(out=st[:, 3:4], in0=st[:, 2:3], scalar1=1.0 / D, scalar2=LN_EPS, op0=ALU.mult, op1=ALU.add), reads=[sk], writes=[sk])
    P.op("act", lambda e: e.activation(out=st[:, 3:4], in_=st[:, 3:4], func=AF.Sqrt), reads=[sk], writes=[sk])
    P.op("dve", lambda e: e.reciprocal(out=st[:, 3:4], in_=st[:, 3:4]), reads=[sk], writes=[sk])
    P.op("dve", lambda e: e.tensor_scalar(out=y[:], in0=y[:], scalar1=st[:, 3:4], scalar2=None, op0=ALU.mult), reads=[yk, sk], writes=[yk])
    P.op("dve", lambda e: e.tensor_tensor(out=y[:], in0=y[:], in1=lnw[:], op=ALU.mult), reads=[yk, lnw.name], writes=[yk])
    P.op("dve", lambda e: e.tensor_tensor(out=out[:], in0=y[:], in1=lnb[:], op=ALU.add), reads=[yk, lnb.name], writes=[out.name])


def top16x2(P, nc, srcs, srck, scrs, vals, idxs, outk):
    n = srcs[0].shape[-1]
    ks = [[k + ".c%d" % q for k in outk] for q in range(2)]
    for q in range(2):
        P.op("dve", lambda e, q=q: e.max(out=vals[q][:, 0:8], in_=srcs[q]), reads=[srck], writes=ks[q])
    yield
    for q in range(2):
        P.op("dve", lambda e, q=q: e.max_index(out=idxs[q][:, 0:8], in_max=vals[q][:, 0:8], in_values=srcs[q]), reads=[srck] + ks[q], writes=ks[q])
    yield
    for q in range(2):
        P.op("dve", lambda e, q=q: e.match_replace(out=scrs[q][:, 0:n], in_to_replace=vals[q][:, 0:8], in_values=srcs[q], imm_value=-1e30), reads=[srck] + ks[q], writes=[scrs[q].name])
    yield
    for q in range(2):
        P.op("dve", lambda e, q=q: e.max(out=vals[q][:, 8:16], in_=scrs[q][:, 0:n]), reads=[scrs[q].name], writes=ks[q])
    yield
    for q in range(2):
        P.op("dve", lambda e, q=q: e.max_index(out=idxs[q][:, 8:16], in_max=vals[q][:, 8:16], in_values=scrs[q][:, 0:n]), reads=[scrs[q].name] + ks[q], writes=outk + ks[q])
    yield


def top16(P, nc, src_ap, srck, scratch, vals_ap, idx_ap, outk):
    sk = scratch.name
    n = src_ap.shape[-1]
    P.op("dve", lambda e: e.max(out=vals_ap[:, 0:8], in_=src_ap), reads=[srck], writes=outk)
    P.op("dve", lambda e: e.max_index(out=idx_ap[:, 0:8], in_max=vals_ap[:, 0:8], in_values=src_ap), reads=[srck] + outk, writes=outk)
    P.op("dve", lambda e: e.match_replace(out=scratch[:, 0:n], in_to_replace=vals_ap[:, 0:8], in_values=src_ap, imm_value=-1e30), reads=[srck] + outk, writes=[sk])
    P.op("dve", lambda e: e.max(out=vals_ap[:, 8:16], in_=scratch[:, 0:n]), reads=[sk], writes=outk)
    P.op("dve", lambda e: e.max_index(out=idx_ap[:, 8:16], in_max=vals_ap[:, 8:16], in_values=scratch[:, 0:n]), reads=[sk] + outk, writes=outk)


def emit_peer(C, T, x_in, xin_key, x_out, xout_key, mod_d, lnw_d, lnb_d, wq_d, keys_d, ure_d, v_d, ubf_d, vbf_d, n_groups=None):
    nc, P = C.nc, C.P
    TG = 256
    NG = T // TG if n_groups is None else n_groups
    with ExitStack() as es:
        def sb(name, shape, dt):
            return es.enter_context(nc.sbuf_tensor(C.name(name), shape, dt))

        def ps(name, shape, dt=F32):
            return es.enter_context(nc.psum_tensor(C.name(name), shape, dt))

        wq = sb("wq", [128, 8, 2048], BF16)
        for dc in range(8):
            P.dma(wq[:, dc, :], wq_d[dc * 128:(dc + 1) * 128, :], writes=["%s.%d" % (wq.name, dc)], q="pool")
        keysT = sb("keysT", [128, 16, 128], BF16)
        S = sb("S", [128, 2048], F32)
        ktmp = V(S[:].rearrange("p (h k) -> p h k", h=16), S.name)
        P.dma(ktmp[:], keys_d.rearrange("h n k -> n h k"), writes=[ktmp.name])
        modr = sb("modr", [128, 3, D], F32)
        for r in range(3):
            P.dma(modr[:, r, :], mod_d[r].partition_broadcast(128), writes=["%s.%d" % (modr.name, r)])
        lnw = sb("lnw", [128, D], F32)
        lnb = sb("lnb", [128, D], F32)
        P.dma(lnw[:], lnw_d.partition_broadcast(128), writes=[lnw.name])
        P.dma(lnb[:], lnb_d.partition_broadcast(128), writes=[lnb.name])

        pbig = [ps("pbig%d" % i, [128, 1024]) for i in range(2)]
        psm = [ps("psm%d" % i, [128, 512]) for i in range(4)]
        for hp in range(16):
            pt = psm[hp % 4]
            P.op("pe", lambda e, pt=pt, hp=hp: e.transpose(out=pt[:, 0:128], in_=ktmp[:, hp, :], identity=C.ident[:]),
                 reads=[ktmp.name, "c_ident"], writes=[pt.name])
            P.op("act", lambda e, pt=pt, hp=hp: e.copy(out=keysT[:, hp, :], in_=pt[:, 0:128]), reads=[pt.name], writes=[keysT.name])

        xf = sb("xf", [128, D], F32)
        xe = [sb("xe%d" % i, [128, D], F32) for i in range(2)]
        hm = sb("hm", [128, D], F32)
        hT = [sb("hT%d" % b, [128, 8, TG], BF16) for b in range(2)]
        qT = sb("qT", [128, 16, TG], BF16)
        scr2 = [sb("scr%d" % i, [128, 256], F32) for i in range(2)]
        stop = sb("stop", [128, 16, 16], F32)
        itopu = sb("itopu", [128, 16, 16], U32)
        itopf = sb("itopf", [128, 16, 16], F32)
        cand = sb("cand", [128, 8, 256], F32)
        eq = V(cand[:].rearrange("p h (a b) -> p h a b", a=16), cand.name)
        best = sb("best", [128, 8, 16], F32)
        posu = sb("posu", [128, 8, 16], U32)
        abu = sb("abu", [128, 2, 8, 16], U32)
        abf = sb("abf", [128, 2, 8, 16], F32)
        IJG = sb("IJG", [128, 3, 128], F32)
        IJGT = [sb("IJGT%d" % b, [128, 3, TG], BF16) for b in range(2)]
        GKF = [sb("GKF%d" % b, [128, TG], BF16) for b in range(2)]
        iotab = sb("iotab", [128, 128], BF16)
        P.op("dve", lambda e: e.tensor_copy(out=iotab[:], in_=C.iota[:]), reads=["c_iota"], writes=[iotab.name])
        sm = sb("sm", [128, 8, 4], F32)
        NB = 8
        oig = [sb("oig%d" % i, [128, NB, 128], BF16) for i in range(2)]
        oj = [sb("oj%d" % i, [128, NB, 128], BF16) for i in range(2)]
        Gall = sb("Gall", [128, TG, 128], BF16)
        UB = 1
        NUB = 3
        ubuf = [sb("ubuf%d" % i, [128, UB, 8, 128], BF16) for i in range(NUB)]
        vbuf = [sb("vbuf%d" % i, [128, UB, D], BF16) for i in range(NUB)]
        Ag = [sb("Ag%d" % i, [128, TG], BF16) for i in range(2)]
        GA = [sb("GA%d" % i, [128, TG], BF16) for i in range(2)]
        ybuf = hm
        ytmp = V(S[:, 0:D], S.name)
        yout = V(S[:, D:2 * D], S.name)
        st = sb("st", [128, 4], F32)

        def front(g):
            bsel = g % 2
            t0 = g * TG
            hTb, IJb = hT[bsel], IJGT[bsel]
            for tt in range(2):
                xb = xf
                xk = xb.name
                P.dma(xb[:], x_in[t0 + tt * 128: t0 + (tt + 1) * 128, :], reads=["%s.%d" % (xin_key, (t0 + tt * 128) // 128)], writes=[xk])
                P.op("dve", lambda e, xb=xb: e.tensor_tensor(out=hm[:], in0=xb[:], in1=modr[:, 0, :], op=ALU.mult), reads=[xk, modr.name + ".0"], writes=[hm.name])
                P.op("dve", lambda e: e.tensor_tensor(out=hm[:], in0=hm[:], in1=modr[:, 1, :], op=ALU.add), reads=[hm.name, modr.name + ".1"], writes=[hm.name])
                for dc in range(8):
                    pb = psm[dc // 4]
                    P.op("pe", lambda e, dc=dc, pb=pb: e.transpose(out=pb[:, (dc % 4) * 128:(dc % 4 + 1) * 128], in_=hm[:, dc * 128:(dc + 1) * 128], identity=C.ident[:]),
                         reads=[hm.name, "c_ident"], writes=[pb.name], track=(dc % 4 == 3))
                for hb in range(2):
                    P.op("act", lambda e, tt=tt, hb=hb, hTb=hTb: e.copy(out=hTb[:, hb * 4:(hb + 1) * 4, tt * 128:(tt + 1) * 128], in_=psm[hb][:].rearrange("p (c t) -> p c t", c=4)),
                         reads=[psm[hb].name], writes=[hTb.name])
                yield
            for hp in range(16):
                pt = psm[hp % 2]
                for dc in range(8):
                    P.op("pe", lambda e, hp=hp, dc=dc, pt=pt, hTb=hTb: e.matmul(pt[:, 0:TG], lhsT=wq[:, dc, hp * 128:(hp + 1) * 128], rhs=hTb[:, dc, :], start=(dc == 0), stop=(dc == 7)),
                         reads=["%s.%d" % (wq.name, dc), hTb.name], writes=[pt.name], track=(dc == 7))
                P.op("act", lambda e, hp=hp, pt=pt: e.copy(out=qT[:, hp, :], in_=pt[:, 0:TG]), reads=[pt.name], writes=[qT.name])
                yield
            for tt in range(2):
                for grp in range(4):
                    pb = psm[grp % 2]
                    for k in range(4):
                        hp = grp * 4 + k
                        P.op("pe", lambda e, hp=hp, pb=pb, k=k, tt=tt: e.matmul(pb[:, k * 128:(k + 1) * 128], lhsT=qT[:, hp, tt * 128:(tt + 1) * 128], rhs=keysT[:, hp, :], start=True, stop=True),
                             reads=[qT.name, keysT.name], writes=[pb.name], track=(k == 3))
                    P.op("act", lambda e, grp=grp, pb=pb: e.copy(out=S[:, grp * 512:(grp + 1) * 512], in_=pb[:]), reads=[pb.name], writes=[S.name])
                    yield
                for hp in range(0, 16, 2):
                    yield from top16x2(P, nc, [S[:, (hp + q) * 128:(hp + q + 1) * 128] for q in range(2)], S.name, scr2, [stop[:, hp + q, :] for q in range(2)],
                                       [itopu[:, hp + q, :] for q in range(2)], [stop.name, itopu.name])
                P.op("dve", lambda e: e.tensor_copy(out=itopf[:], in_=itopu[:]), reads=[itopu.name], writes=[itopf.name])
                sv = stop[:].rearrange("p (h two) k -> p h two k", two=2)
                P.op("dve", lambda e: e.tensor_tensor(out=cand[:].rearrange("p h (a b) -> p h a b", a=16),
                                                      in0=sv[:, :, 0, :].unsqueeze(3).to_broadcast([128, 8, 16, 16]),
                                                      in1=sv[:, :, 1, :].unsqueeze(2).to_broadcast([128, 8, 16, 16]), op=ALU.add),
                     reads=[stop.name], writes=[cand.name])
                yield
                for h in range(0, 8, 2):
                    yield from top16x2(P, nc, [cand[:, h + q, :] for q in range(2)], cand.name, scr2, [best[:, h + q, :] for q in range(2)],
                                       [posu[:, h + q, :] for q in range(2)], [best.name, posu.name])
                P.op("dve", lambda e: e.tensor_single_scalar(out=abu[:, 0], in_=posu[:], scalar=4, op=ALU.logical_shift_right), reads=[posu.name], writes=[abu.name])
                P.op("dve", lambda e: e.tensor_single_scalar(out=abu[:, 1], in_=posu[:], scalar=15, op=ALU.bitwise_and), reads=[posu.name], writes=[abu.name])
                P.op("dve", lambda e: e.tensor_copy(out=abf[:], in_=abu[:]), reads=[abu.name], writes=[abf.name])
                yield
                iv = itopf[:].rearrange("p (h two) k -> p h two k", two=2)
                for w in range(2):
                    P.op("dve", lambda e, w=w: e.tensor_tensor(out=eq[:], in0=abf[:, w].unsqueeze(3).to_broadcast([128, 8, 16, 16]),
                                                               in1=C.iota[:, 0:16].unsqueeze(1).unsqueeze(1).to_broadcast([128, 8, 16, 16]), op=ALU.is_equal),
                         reads=[abf.name, "c_iota"], writes=[eq.name])
                    P.op("dve", lambda e, w=w: e.tensor_tensor(out=eq[:], in0=eq[:], in1=iv[:, :, w, :].unsqueeze(2).to_broadcast([128, 8, 16, 16]), op=ALU.mult),
                         reads=[eq.name, itopf.name], writes=[eq.name])
                    P.op("dve", lambda e, w=w: e.reduce_sum(out=IJG[:, w, :].rearrange("p (h k) -> p h k", h=8), in_=eq[:], axis=AX.X),
                         reads=[eq.name], writes=[IJG.name])
                    yield
                gk = IJG[:, 2, :].rearrange("p (h k) -> p h k", h=8)
                P.op("dve", lambda e: e.tensor_tensor(out=gk, in0=best[:], in1=best[:, :, 0:1].to_broadcast([128, 8, 16]), op=ALU.subtract),
                     reads=[best.name], writes=[IJG.name])
                P.op("act", lambda e: e.activation(out=gk, in_=gk, func=AF.Exp), reads=[IJG.name], writes=[IJG.name])
                P.op("dve", lambda e: e.reduce_sum(out=sm[:, :, 0:1], in_=gk, axis=AX.X), reads=[IJG.name], writes=[sm.name])
                P.op("dve", lambda e: e.reciprocal(out=sm[:, :, 1:2], in_=sm[:, :, 0:1]), reads=[sm.name], writes=[sm.name])
                P.op("dve", lambda e: e.tensor_tensor(out=gk, in0=gk, in1=sm[:, :, 1:2].to_broadcast([128, 8, 16]), op=ALU.mult),
                     reads=[IJG.name, sm.name], writes=[IJG.name])
                yield
                for w in range(3):
                    pt = psm[w % 2]
                    P.op("pe", lambda e, w=w, pt=pt: e.transpose(out=pt[:, 0:128], in_=IJG[:, w, :], identity=C.ident[:]), reads=[IJG.name, "c_ident"], writes=[pt.name])
                    if w < 2:
                        P.op("act", lambda e, w=w, pt=pt, tt=tt, IJb=IJb: e.copy(out=IJb[:, w, tt * 128:(tt + 1) * 128], in_=pt[:, 0:128]), reads=[pt.name], writes=[IJb.name])
                    else:
                        P.op("act", lambda e, pt=pt, tt=tt, bsel=bsel: e.copy(out=GKF[bsel][:, tt * 128:(tt + 1) * 128], in_=pt[:, 0:128]), reads=[pt.name], writes=[GKF[bsel].name])
                yield

        def gbuild(g):
            IJb = IJGT[g % 2]
            GKf = GKF[g % 2]
            for tb in range(TG // NB):
                s = tb % 2
                t0_ = tb * NB
                iob = iotab[:].unsqueeze(1).to_broadcast([128, NB, 128])
                P.op("dve", lambda e, s=s, t0_=t0_, iob=iob: e.tensor_tensor(out=oig[s][:], in0=iob, in1=IJb[:, 0, t0_:t0_ + NB].unsqueeze(2).to_broadcast([128, NB, 128]), op=ALU.is_equal),
                     reads=[iotab.name, IJb.name], writes=[oig[s].name])
                P.op("dve", lambda e, s=s, t0_=t0_, iob=iob: e.tensor_tensor(out=oj[s][:], in0=iob, in1=IJb[:, 1, t0_:t0_ + NB].unsqueeze(2).to_broadcast([128, NB, 128]), op=ALU.is_equal),
                     reads=[iotab.name, IJb.name], writes=[oj[s].name])
                P.op("dve", lambda e, s=s, t0_=t0_: e.tensor_tensor(out=oig[s][:], in0=oig[s][:], in1=GKf[:, t0_:t0_ + NB].unsqueeze(2).to_broadcast([128, NB, 128]), op=ALU.mult),
                     reads=[oig[s].name, GKf.name], writes=[oig[s].name])
                for qd in range(NB // 4):
                    pt = psm[(tb * (NB // 4) + qd) % 4]
                    for k in range(4):
                        kk = qd * 4 + k
                        P.op("pe", lambda e, s=s, k=k, kk=kk, pt=pt: e.matmul(pt[:, k * 128:(k + 1) * 128], lhsT=oj[s][:, kk, :], rhs=oig[s][:, kk, :], start=True, stop=True),
                             reads=[oj[s].name, oig[s].name], writes=[pt.name], track=(k == 3))
                    P.op("act", lambda e, t0_=t0_, qd=qd, pt=pt: e.copy(out=Gall[:, t0_ + qd * 4:t0_ + qd * 4 + 4, :], in_=pt[:].rearrange("p (k i) -> p k i", k=4)),
                         reads=[pt.name], writes=[Gall.name])

        def load_uv(blk):
            s = blk % NUB
            P.dma(ubuf[s][:], ubf_d[blk * UB:(blk + 1) * UB].rearrange("i p c e -> p i c e"), reads=["ubf%d" % (blk * UB // 8)], writes=[ubuf[s].name])
            P.dma(vbuf[s][:], vbf_d[blk * UB * 128:(blk + 1) * UB * 128, :].rearrange("(i e) d -> e i d", i=UB), reads=["vbf%d" % (blk * UB // 8)], writes=[vbuf[s].name])

        def mainloop(g, gen):
            hTb = hT[g % 2]

            def a_mm(i):
                s, ii = (i // UB) % NUB, i % UB
                pa = psm[2 + i % 2]
                for dc in range(8):
                    P.op("pe", lambda e, s=s, ii=ii, dc=dc, pa=pa: e.matmul(pa[:, 0:TG], lhsT=ubuf[s][:, ii, dc, :], rhs=hTb[:, dc, :], start=(dc == 0), stop=(dc == 7)),
                         reads=[ubuf[s].name, hTb.name], writes=[pa.name], track=(dc == 7))
                P.op("act", lambda e, i=i, pa=pa: e.activation(out=Ag[i % 2][:], in_=pa[:, 0:TG], func=AF.Gelu), reads=[pa.name], writes=[Ag[i % 2].name])
                P.op("dve", lambda e, i=i: e.tensor_tensor(out=GA[i % 2][:], in0=Ag[i % 2][:], in1=Gall[:, :, i], op=ALU.mult),
                     reads=[Ag[i % 2].name, Gall.name], writes=[GA[i % 2].name])

            def v_mm(i):
                s, ii = (i // UB) % NUB, i % UB
                for tt in range(2):
                    for hf in range(2):
                        P.op("pe", lambda e, s=s, ii=ii, tt=tt, hf=hf, i=i: e.matmul(pbig[tt][:, hf * 512:(hf + 1) * 512], lhsT=GA[i % 2][:, tt * 128:(tt + 1) * 128],
                                                                                   rhs=vbuf[s][:, ii, hf * 512:(hf + 1) * 512], start=(i == 0), stop=(i == 127)),
                             reads=[GA[i % 2].name, vbuf[s].name], writes=[pbig[tt].name], track=(tt == 1 and hf == 1))

            for b_ in range(NUB):
                load_uv(b_)
            a_mm(0)
            for i in range(128):
                if i + 1 < 128:
                    a_mm(i + 1)
                v_mm(i)
                if i % UB == UB - 1 and i // UB + NUB < 128 // UB:
                    load_uv(i // UB + NUB)
                if i == 96:
                    for tt in range(2):
                        P.dma(xe[tt][:], x_in[g * TG + tt * 128: g * TG + (tt + 1) * 128, :], reads=["%s.%d" % (xin_key, (g * TG + tt * 128) // 128)], writes=[xe[tt].name])
                if gen is not None and i >= 2:
                    next(gen, None)
                    next(gen, None)
            if gen is not None:
                for _ in gen:
                    pass

        def epilogue(g):
            t0 = g * TG
            for tt in range(2):
                xb = xe[tt]
                P.op("dve", lambda e, tt=tt: e.tensor_tensor(out=ytmp[:], in0=pbig[tt][:], in1=modr[:, 2, :], op=ALU.mult), reads=[pbig[tt].name, modr.name + ".2"], writes=[ytmp.name])
                P.op("dve", lambda e, xb=xb: e.scalar_tensor_tensor(out=ybuf[:], in0=xb[:], scalar=ALPHA, in1=ytmp[:], op0=ALU.mult, op1=ALU.add),
                     reads=[xb.name, ytmp.name], writes=[ybuf.name])
                layernorm_rows(P, nc, ybuf, None, ytmp, st, lnw, lnb, "pe", yout)
                P.dma(x_out[t0 + tt * 128: t0 + (tt + 1) * 128, :], yout[:], reads=[yout.name], writes=["%s.%d" % (xout_key, (t0 + tt * 128) // 128)])

        for _ in front(0):
            pass
        for g in range(NG):
            gbuild(g)
            mainloop(g, front(g + 1) if g + 1 < NG else None)
            epilogue(g)
    C.P.barrier()


def emit_peer_prep(C, ure_d, v_d, ubf_d, vbf_d):
    P = C.P
    for b in range(16):
        P.dma(ubf_d[b * 8:(b + 1) * 8].rearrange("i p c e -> (i p) (c e)"), ure_d[b * 8:(b + 1) * 8].rearrange("i p c e -> (i p) (c e)"),
              reads=["ure"], writes=["ubf%d" % b], q="pool")
        P.dma(vbf_d[b * 1024:(b + 1) * 1024, :], v_d[b * 1024:(b + 1) * 1024, :], reads=["vsrc"], writes=["vbf%d" % b], q="pool")


def emit_mod(C, ccT_d, wmod_d, bmod_d, modd):
    nc, P = C.nc, C.P
    with ExitStack() as es:
        def sb(name, shape, dt):
            return es.enter_context(nc.sbuf_tensor(C.name(name), shape, dt))
        cc = sb("cc", [128, 8, 2], F32)
        P.dma(cc[:], ccT_d, writes=[cc.name])
        P.op("act", lambda e: e.activation(out=cc[:], in_=cc[:], func=AF.Silu), reads=[cc.name], writes=[cc.name])
        wb = [sb("wmod%d" % i, [128, 8, 512], F32) for i in range(2)]
        bm = sb("bm", [2, 6 * D], F32)
        P.dma(bm[:], bmod_d.partition_broadcast(2), writes=[bm.name])
        mo = sb("mo", [2, 6 * D], F32)
        pm = [es.enter_context(nc.psum_tensor(C.name("pmod%d" % i), [128, 512], F32)) for i in range(2)]
        for n in range(12):
            w = wb[n % 2]
            P.dma(w[:], wmod_d[:, n * 512:(n + 1) * 512].rearrange("(c p) n -> p c n", p=128), writes=[w.name])
            pp = pm[n % 2]
            for dc in range(8):
                P.op("pe", lambda e, dc=dc, w=w, pp=pp: e.matmul(pp[0:2, :], lhsT=cc[:, dc, :], rhs=w[:, dc, :], start=(dc == 0), stop=(dc == 7)),
                     reads=[cc.name, w.name], writes=[pp.name], track=(dc == 7))
            P.op("dve", lambda e, n=n, pp=pp: e.tensor_tensor(out=mo[:, n * 512:(n + 1) * 512], in0=pp[0:2, :], in1=bm[:, n * 512:(n + 1) * 512], op=ALU.add),
                 reads=[pp.name, bm.name], writes=[mo.name])
        for k in (1, 4):
            P.op("dve", lambda e, k=k: e.tensor_single_scalar(out=mo[:, k * D:(k + 1) * D], in_=mo[:, k * D:(k + 1) * D], scalar=1.0, op=ALU.add),
                 reads=[mo.name], writes=[mo.name])
        P.dma(modd.rearrange("r k d -> r (k d)"), mo[:], reads=[mo.name], writes=["modd"])
    C.P.barrier()


def emit_scan(C, NT, n_ctx, QKT, KT, VA, ET, HO, DV, aug, masks_d, key):
    nc, P = C.nc, C.P
    DVO = DV - 1 if aug else DV
    with ExitStack() as es:
        def sb(name, shape, dt):
            return es.enter_context(nc.sbuf_tensor(C.name(name), shape, dt))

        def ps(name, shape, dt=F32):
            return es.enter_context(nc.psum_tensor(C.name(name), shape, dt))
        mask = sb("mask", [128, 2, 128], F32)
        P.dma(mask[:, 0, :], masks_d[0], writes=[mask.name + ".0"])
        P.dma(mask[:, 1, :], masks_d[1], writes=[mask.name + ".1"])
        St = [sb("St%d" % d, [128, 4, DV], F32) for d in range(2)]
        Sb = [sb("Sb%d" % d, [128, 4, DV], BF16) for d in range(2)]
        for d in range(2):
            P.op("dve", lambda e, d=d: e.memset(St[d][:], 0.0), writes=[St[d].name])
            P.op("pool", lambda e, d=d: e.memset(Sb[d][:], 0.0), writes=[Sb[d].name])
        qkt = [sb("qkt%d" % i, [128, 2, 4, 128], BF16) for i in range(2)]
        kt = [sb("kt%d" % i, [128, 4, 128], BF16) for i in range(2)]
        va = [sb("va%d" % i, [128, 4, DV], BF16) for i in range(2)]
        et = [sb("et%d" % i, [128, 4], F32) for i in range(2)]
        WT = [sb("WT%d" % i, [128, 4, 128], BF16) for i in range(2)]
        tmp = sb("tmp", [128, 4, DV], F32)
        ho = [sb("ho%d" % i, [128, 4, DVO], F32) for i in range(2)]
        dn = sb("dn", [128, 4, 2], F32)
        pS = [ps("pS%d" % d, [128, 512]) for d in range(2)]
        NB = 2 if DV <= 256 else 4
        pN = ps("pN", [128, 2, 512])
        pD = ps("pD", [128, 2, 512])
        lat = list(range(n_ctx, NT))
        order = [list(range(n_ctx)) + lat, list(range(n_ctx))[::-1] + lat[::-1]]
        for n in range(NT):
            for d in range(2):
                tile = order[d][n]
                b = d
                tk = "%s.%d" % (key, tile)
                qv = QKT[tile].rearrange("p (w dd h) t -> p w dd h t", w=2, dd=2)
                P.dma(qkt[b][:], qv[:, :, d, :, :], reads=[tk], writes=[qkt[b].name])
                P.dma(kt[b][:], KT[tile][:, d * 4:(d + 1) * 4, :], reads=[tk], writes=[kt[b].name])
                P.dma(va[b][:], VA[tile], reads=[tk], writes=[va[b].name])
                P.dma(et[b][:], ET[tile][:, d * 4:(d + 1) * 4], reads=[tk], writes=[et[b].name])
                for h in range(4):
                    P.op("pe", lambda e, h=h, b=b, d=d: e.matmul(pS[d][:, h * 128:(h + 1) * 128], lhsT=qkt[b][:, 1, h, :], rhs=qkt[b][:, 0, h, :], start=True, stop=True),
                         reads=[qkt[b].name], writes=[pS[d].name], track=(h == 3))
                P.op("dve", lambda e, b=b, d=d: e.tensor_tensor(out=WT[b][:], in0=pS[d][:].rearrange("p (h t) -> p h t", h=4),
                                                               in1=mask[:, d, :].unsqueeze(1).to_broadcast([128, 4, 128]), op=ALU.mult),
                     reads=[pS[d].name, mask.name + ".%d" % d], writes=[WT[b].name])
                for h in range(4):
                    o = pN[:, h // 2, (h % 2) * 256:(h % 2) * 256 + DV]
                    P.op("pe", lambda e, h=h, b=b, d=d, o=o: e.matmul(o, lhsT=qkt[b][:, 0, h, :], rhs=Sb[d][:, h, :], start=True, stop=False),
                         reads=[qkt[b].name, Sb[d].name], writes=["pN"], track=False)
                    P.op("pe", lambda e, h=h, b=b, o=o: e.matmul(o, lhsT=WT[b][:, h, :], rhs=va[b][:, h, :], start=False, stop=True),
                         reads=[WT[b].name, va[b].name], writes=["pN"], track=(h == 3))
                pNv = pN[:].rearrange("p a (c x) -> p (a c) x", c=2)
                if aug:
                    P.op("act", lambda e: e.activation(out=dn[:, :, 0:1], in_=pNv[:, :, DVO:DVO + 1], func=AF.Abs), reads=["pN"], writes=[dn.name])
                    P.op("dve", lambda e: e.tensor_single_scalar(out=dn[:, :, 0:1], in_=dn[:, :, 0:1], scalar=1.0, op=ALU.max), reads=[dn.name], writes=[dn.name])
                    P.op("dve", lambda e: e.reciprocal(out=dn[:, :, 1:2], in_=dn[:, :, 0:1]), reads=[dn.name], writes=[dn.name])
                    P.op("dve", lambda e, b=b: e.tensor_tensor(out=ho[b][:], in0=pNv[:, :, 0:DVO], in1=dn[:, :, 1:2].to_broadcast([128, 4, DVO]), op=ALU.mult),
                         reads=["pN", dn.name], writes=[ho[b].name])
                else:
                    P.op("act", lambda e, b=b: e.copy(out=ho[b][:], in_=pNv[:, :, 0:DVO]), reads=["pN"], writes=[ho[b].name])
                P.dma(HO[d][tile], ho[b][:], reads=[ho[b].name], writes=["%s.ho%d.%d" % (key, d, tile)], q="pool")
                for h in range(4):
                    o = pD[:, h // 2, (h % 2) * 256:(h % 2) * 256 + DV]
                    P.op("pe", lambda e, h=h, b=b, o=o: e.matmul(o, lhsT=kt[b][:, h, :], rhs=va[b][:, h, :], start=True, stop=True),
                         reads=[kt[b].name, va[b].name], writes=["pD"], track=(h == 3))
                pDv = pD[:].rearrange("p a (c x) -> p (a c) x", c=2)
                P.op("dve", lambda e, d=d: e.tensor_tensor(out=tmp[:], in0=pDv[:, :, 0:DV], in1=St[d][:], op=ALU.add), reads=["pD", St[d].name], writes=[tmp.name])
                P.op("dve", lambda e, d=d, b=b: e.tensor_tensor(out=St[d][:], in0=tmp[:], in1=et[b][:].unsqueeze(2).to_broadcast([128, 4, DV]), op=ALU.mult),
                     reads=[tmp.name, et[b].name], writes=[St[d].name])
                P.op("act", lambda e, d=d: e.copy(out=Sb[d][:], in_=St[d][:]), reads=[St[d].name], writes=[Sb[d].name])
    C.P.barrier()


def load_w_bf16(C, sbf, w_d, ncols, key):
    for dc in range(8):
        C.P.dma(sbf[:, dc, :], w_d[dc * 128:(dc + 1) * 128, :], writes=["%s.%d" % (key, dc)], q="pool")


def emit_in_proj(C, es, tile_src, mod_rows, wbf, wkey, ncols, Pj, xt, hl, hT, pbanks, modr):
    nc, P = C.nc, C.P
    src_ap, src_key = tile_src
    P.dma(xt[:], src_ap, reads=[src_key], writes=[xt.name])
    P.op("dve", lambda e: e.tensor_tensor(out=hl[:], in0=xt[:], in1=modr[:, mod_rows[0], :], op=ALU.mult), reads=[xt.name, modr.name + ".%d" % mod_rows[0]], writes=[hl.name])
    P.op("dve", lambda e: e.tensor_tensor(out=hl[:], in0=hl[:], in1=modr[:, mod_rows[1], :], op=ALU.add), reads=[hl.name, modr.name + ".%d" % mod_rows[1]], writes=[hl.name])
    yield
    for dc in range(8):
        pb = pbanks[dc // 4]
        P.op("pe", lambda e, dc=dc, pb=pb: e.transpose(out=pb[:, (dc % 4) * 128:(dc % 4 + 1) * 128], in_=hl[:, dc * 128:(dc + 1) * 128], identity=C.ident[:]),
             reads=[hl.name, "c_ident"], writes=[pb.name], track=(dc % 4 == 3))
    yield
    for hb in range(2):
        P.op("act", lambda e, hb=hb: e.copy(out=hT[:, hb * 4:(hb + 1) * 4, :], in_=pbanks[hb][:].rearrange("p (c t) -> p c t", c=4)),
             reads=[pbanks[hb].name], writes=[hT.name])
    yield
    nch = (ncols + 511) // 512
    for n in range(nch):
        c0, c1 = n * 512, min(ncols, (n + 1) * 512)
        pb = pbanks[2 + n % (len(pbanks) - 2)]
        for dc in range(8):
            P.op("pe", lambda e, dc=dc, pb=pb, c0=c0, c1=c1: e.matmul(pb[:, 0:c1 - c0], lhsT=hT[:, dc, :], rhs=wbf[:, dc, c0:c1], start=(dc == 0), stop=(dc == 7)),
                 reads=[hT.name, "%s.%d" % (wkey, dc)], writes=[pb.name], track=(dc == 7))
        P.op("act", lambda e, pb=pb, c0=c0, c1=c1: e.copy(out=Pj[:, c0:c1], in_=pb[:, 0:c1 - c0]), reads=[pb.name], writes=[Pj.name])
        if n % 2 == 1:
            yield


def emit_decay_prep(C, sbs, LFv, LIv, lkey, ncol, cmats, pcum, with_li):
    nc, P = C.nc, C.P
    triu, tril, ones = cmats
    CUM, TOT = sbs
    for d in range(2):
        P.op("pe", lambda e, d=d: e.matmul(pcum[:, d * ncol:(d + 1) * ncol], lhsT=(triu if d == 0 else tril)[:], rhs=LFv[:, d * ncol:(d + 1) * ncol], start=True, stop=True),
             reads=[lkey, "c_tri"], writes=[pcum.name])
    P.op("act", lambda e: e.copy(out=CUM[:], in_=pcum[:, 0:2 * ncol]), reads=[pcum.name], writes=[CUM.name])
    P.op("pe", lambda e: e.matmul(pcum[:, 0:2 * ncol], lhsT=ones[:], rhs=LFv[:, 0:2 * ncol], start=True, stop=True), reads=[lkey, "c_tri"], writes=[pcum.name])
    P.op("act", lambda e: e.activation(out=TOT[:], in_=pcum[:, 0:2 * ncol], func=AF.Exp), reads=[pcum.name], writes=[TOT.name])


def emit_ab_stage_a(C, NL, x_d, xkey, ctx_d, ckey, modd, w_in_d, gate_b_d, rope_d, S):
    nc, P = C.nc, C.P
    NT = 2 + NL
    with ExitStack() as es:
        def sb(name, shape, dt):
            return es.enter_context(nc.sbuf_tensor(C.name(name), shape, dt))

        def ps(name, shape, dt=F32):
            return es.enter_context(nc.psum_tensor(C.name(name), shape, dt))
        NC = 2832
        wbf = sb("w_in", [128, 8, NC], BF16)
        load_w_bf16(C, wbf, w_in_d, NC, wbf.name)
        modr = sb("modr", [128, 4, D], F32)
        for r, (row, k) in enumerate(((0, 1), (0, 0), (1, 1), (1, 0))):
            P.dma(modr[:, r, :], modd[row, k].partition_broadcast(128), reads=["modd"], writes=[modr.name + ".%d" % r])
        gb = sb("gb", [128, 16], F32)
        P.dma(gb[:], gate_b_d.partition_broadcast(128), writes=[gb.name])
        tri = sb("tri", [128, 3, 128], F32)
        P.dma(tri[:], C.consts_d[2:5].rearrange("k p n -> p k n"), writes=["c_tri"])
        cm = (tri[:, 0, :], tri[:, 1, :], tri[:, 2, :])
        cmats = (V(cm[0], "c_tri"), V(cm[1], "c_tri"), V(cm[2], "c_tri"))
        pb = [ps("pb%d" % i, [128, 512]) for i in range(7)]
        pcum = ps("pcum", [128, 512])
        xt_ = [sb("xt%d" % i_, [128, D], F32) for i_ in range(2)]
        hl_ = [sb("hl%d" % i_, [128, D], F32) for i_ in range(2)]
        hT_ = [sb("hT%d" % i_, [128, 8, 128], BF16) for i_ in range(2)]
        Pj_ = [sb("Pj%d" % i_, [128, NC], F32) for i_ in range(2)]
        G16_ = [sb("G16%d" % i_, [128, 16], F32) for i_ in range(2)]
        LF_ = [sb("LF%d" % i_, [128, 8], F32) for i_ in range(2)]
        LI_ = [sb("LI%d" % i_, [128, 8], F32) for i_ in range(2)]
        CUM_ = [sb("CUM%d" % i_, [128, 8], F32) for i_ in range(2)]
        TOT_ = [sb("TOT%d" % i_, [128, 8], F32) for i_ in range(2)]
        EB_ = [sb("EB%d" % i_, [128, 8], F32) for i_ in range(2)]
        EA_ = [sb("EA%d" % i_, [128, 8], F32) for i_ in range(2)]
        qk_ = [sb("qk%d" % i_, [128, 2, 2, 4, 128], F32) for i_ in range(2)]
        qkT_ = [sb("qkT%d" % i_, [128, 16, 128], BF16) for i_ in range(2)]
        ktb_ = [sb("ktb%d" % i_, [128, 8, 128], BF16) for i_ in range(2)]
        vab_ = [sb("vab%d" % i_, [128, 4, 129], BF16) for i_ in range(2)]
        for vab in vab_:
            P.op("dve", lambda e, vab=vab: e.memset(vab[:], 1.0), writes=[vab.name])
        rope_ = [sb("rope%d" % i_, [128, 2, 32], F32) for i_ in range(2)]
        qa_ = [sb("qa%d" % i_, [128, 10, 64], F32) for i_ in range(2)]
        rt_ = [sb("rt%d" % i_, [128, 4, 10, 32], F32) for i_ in range(2)]
        qaT_ = [sb("qaT%d" % i_, [128, 5, 128], BF16) for i_ in range(2)]
        vaa_ = [sb("vaa%d" % i_, [128, 2, 65], BF16) for i_ in range(2)]
        for vaa in vaa_:
            P.op("dve", lambda e, vaa=vaa: e.memset(vaa[:], 1.0), writes=[vaa.name])
        def tile_gen(tile):
            xt = xt_[tile % 2]
            hl = hl_[tile % 2]
            hT = hT_[tile % 2]
            Pj = Pj_[tile % 2]
            G16 = G16_[tile % 2]
            LF = LF_[tile % 2]
            LI = LI_[tile % 2]
            CUM = CUM_[tile % 2]
            TOT = TOT_[tile % 2]
            EB = EB_[tile % 2]
            EA = EA_[tile % 2]
            qk = qk_[tile % 2]
            qkT = qkT_[tile % 2]
            ktb = ktb_[tile % 2]
            vab = vab_[tile % 2]
            rope = rope_[tile % 2]
            qa = qa_[tile % 2]
            rt = rt_[tile % 2]
            qaT = qaT_[tile % 2]
            vaa = vaa_[tile % 2]
            is_ctx = tile < 2
            if is_ctx:
                src = (ctx_d[tile * 128:(tile + 1) * 128, :], "%s.%d" % (ckey, tile))
            else:
                src = (x_d[(tile - 2) * 128:(tile - 1) * 128, :], "%s.%d" % (xkey, tile - 2))
            yield from emit_in_proj(C, es, src, (2, 3) if is_ctx else (0, 1), wbf, wbf.name, NC, Pj, xt, hl, hT, pb, modr)
            tk = "ab.%d" % tile
            P.op("dve", lambda e: e.tensor_tensor(out=G16[:], in0=Pj[:, 2048:2064], in1=gb[:], op=ALU.add), reads=[Pj.name, gb.name], writes=[G16.name])
            gv = G16[:].rearrange("p (d g h) -> p d g h", d=2, g=2)
            P.op("dve", lambda e: e.tensor_copy(out=LI[:].rearrange("p (d h) -> p d h", d=2), in_=gv[:, :, 0, :]), reads=[G16.name], writes=[LI.name])
            P.op("act", lambda e: e.activation(out=LF[:].rearrange("p (d h) -> p d h", d=2), in_=gv[:, :, 1, :], func=AF.Exp, scale=-1.0), reads=[G16.name], writes=[LF.name])
            P.op("dve", lambda e: e.tensor_single_scalar(out=LF[:], in_=LF[:], scalar=1.0, op=ALU.add), reads=[LF.name], writes=[LF.name])
            P.op("act", lambda e: e.activation(out=LF[:], in_=LF[:], func=AF.Ln), reads=[LF.name], writes=[LF.name])
            P.op("dve", lambda e: e.tensor_single_scalar(out=LF[:], in_=LF[:], scalar=-1.0, op=ALU.mult), reads=[LF.name], writes=[LF.name])
            emit_decay_prep(C, (CUM, TOT), LF[:], LI[:], LF.name, 4, cmats, pcum, True)
            yield
            P.dma(S["ET"][tile], TOT[:], reads=[TOT.name], writes=[tk + ".et"], q="pool")
            yield
            P.op("act", lambda e: e.activation(out=EB[:], in_=CUM[:], func=AF.Exp), reads=[CUM.name], writes=[EB.name])
            P.op("dve", lambda e: e.tensor_single_scalar(out=EB[:], in_=EB[:], scalar=128.0 ** -0.5, op=ALU.mult), reads=[EB.name], writes=[EB.name])
            P.op("dve", lambda e: e.tensor_tensor(out=EA[:], in0=LI[:], in1=CUM[:], op=ALU.subtract), reads=[LI.name, CUM.name], writes=[EA.name])
            P.op("act", lambda e: e.activation(out=EA[:], in_=EA[:], func=AF.Exp), reads=[EA.name], writes=[EA.name])
            for w, (E_, c0) in enumerate(((EB, 0), (EA, 512))):
                for d in range(2):
                    P.op("dve", lambda e, w=w, d=d, E_=E_, c0=c0: e.tensor_tensor(
                        out=qk[:, w, d], in0=Pj[:, c0:c0 + 512].rearrange("p (h k) -> p h k", h=4),
                        in1=E_[:, d * 4:(d + 1) * 4].unsqueeze(2).to_broadcast([128, 4, 128]), op=ALU.mult),
                        reads=[Pj.name, E_.name], writes=[qk.name + ".%d%d" % (w, d)])
            for s in range(16):
                w, d, h = s // 8, (s // 4) % 2, s % 4
                pp = pb[s // 4]
                P.op("pe", lambda e, s=s, w=w, d=d, h=h, pp=pp: e.transpose(out=pp[:, (s % 4) * 128:(s % 4 + 1) * 128], in_=qk[:, w, d, h, :], identity=C.ident[:]),
                     reads=[qk.name + ".%d%d" % (w, d), "c_ident"], writes=[pp.name], track=(s % 4 == 3))
            for g in range(4):
                P.op("act" if g % 2 == 0 else "dve", lambda e, g=g: (e.copy if g % 2 == 0 else e.tensor_copy)(out=qkT[:, g * 4:(g + 1) * 4, :], in_=pb[g][:].rearrange("p (c t) -> p c t", c=4)),
                     reads=[pb[g].name], writes=[qkT.name + ".%d" % g])
            P.dma(S["QKT"][tile], qkT[:], reads=[qkT.name + ".%d" % g for g in range(4)], writes=[tk + ".qkt"], q="pool")
            yield
            P.op("act", lambda e: e.copy(out=ktb[:], in_=qk[:, 1].rearrange("p d h k -> p (d h) k")), reads=[qk.name + ".10", qk.name + ".11"], writes=[ktb.name])
            P.dma(S["KT"][tile], ktb[:], reads=[ktb.name], writes=[tk + ".kt"], q="pool")
            yield
            P.op("act", lambda e: e.copy(out=vab[:, :, 0:128], in_=Pj[:, 1024:1536].rearrange("p (h k) -> p h k", h=4)), reads=[Pj.name], writes=[vab.name])
            P.dma(S["VA"][tile], vab[:], reads=[vab.name], writes=[tk + ".va"], q="pool")
            yield
            P.dma(S["OM"][tile], Pj[:, 1536:2048], reads=[Pj.name], writes=[tk + ".om"], q="pool")
            yield
            qsrc = Pj[:, 2064:2704].rearrange("p (h k) -> p h k", h=10)
            qperm = qa[:, 0:8, :].rearrange("p (j a) k -> p a j k", a=2)
            if is_ctx:
                P.op("dve", lambda e: e.tensor_copy(out=qperm, in_=qsrc[:, 0:8, :].rearrange("p (a j) k -> p a j k", a=2)), reads=[Pj.name], writes=[qa.name])
                P.op("dve", lambda e: e.tensor_copy(out=qa[:, 8:10, :], in_=qsrc[:, 8:10, :]), reads=[Pj.name], writes=[qa.name])
            else:
                P.dma(rope[:], rope_d[tile - 2], writes=[rope.name])
                x1 = qsrc.rearrange("p h (i two) -> p h i two", two=2)[:, :, :, 0]
                x2 = qsrc.rearrange("p h (i two) -> p h i two", two=2)[:, :, :, 1]
                cs = rope[:, 0, :].unsqueeze(1).to_broadcast([128, 10, 32])
                sn = rope[:, 1, :].unsqueeze(1).to_broadcast([128, 10, 32])
                qo = qa[:].rearrange("p h (i two) -> p h i two", two=2)
                for j, (xa, tb_) in enumerate(((x1, cs), (x2, sn), (x1, sn), (x2, cs))):
                    P.op("dve", lambda e, j=j, xa=xa, tb_=tb_: e.tensor_tensor(out=rt[:, j], in0=xa, in1=tb_, op=ALU.mult),
                         reads=[Pj.name, rope.name], writes=[rt.name + ".%d" % j])
                qpo = qperm.rearrange("p a j (i two) -> p a j i two", two=2)
                for two, (ra, rb, op_) in enumerate(((0, 1, ALU.subtract), (2, 3, ALU.add))):
                    P.op("dve", lambda e, two=two, ra=ra, rb=rb, op_=op_: e.tensor_tensor(out=qpo[:, :, :, :, two], in0=rt[:, ra, 0:8].rearrange("p (a j) i -> p a j i", a=2),
                                                                                  in1=rt[:, rb, 0:8].rearrange("p (a j) i -> p a j i", a=2), op=op_),
                         reads=[rt.name + ".%d" % ra, rt.name + ".%d" % rb], writes=[qa.name])
                    P.op("dve", lambda e, two=two, ra=ra, rb=rb, op_=op_: e.tensor_tensor(out=qo[:, 8:10, :, two], in0=rt[:, ra, 8:10], in1=rt[:, rb, 8:10], op=op_),
                         reads=[rt.name + ".%d" % ra, rt.name + ".%d" % rb], writes=[qa.name])
            pp = pb[4]
            pq = pb[5]
            for j in range(4):
                P.op("pe", lambda e, j=j, pp=pp: e.transpose(out=pp[:, j * 128:(j + 1) * 128], in_=qa[:].rearrange("p h k -> p (h k)")[:, j * 128:(j + 1) * 128], identity=C.ident[:]),
                     reads=[qa.name, "c_ident"], writes=[pp.name], track=(j == 3))
            P.op("pe", lambda e, pq=pq: e.transpose(out=pq[:, 0:128], in_=qa[:].rearrange("p h k -> p (h k)")[:, 512:640], identity=C.ident[:]), reads=[qa.name, "c_ident"], writes=[pq.name])
            P.op("act", lambda e, pp=pp: e.copy(out=qaT[:, 0:4, :], in_=pp[:].rearrange("p (c t) -> p c t", c=4)), reads=[pp.name], writes=[qaT.name])
            P.op("act", lambda e, pq=pq: e.copy(out=qaT[:, 4, :], in_=pq[:, 0:128]), reads=[pq.name], writes=[qaT.name])
            P.dma(S["QAT"][tile], qaT[:], reads=[qaT.name], writes=[tk + ".qat"], q="pool")
            yield
            P.op("act", lambda e: e.copy(out=vaa[:, :, 0:64], in_=Pj[:, 2704:2832].rearrange("p (h k) -> p h k", h=2)), reads=[Pj.name], writes=[vaa.name])
            P.dma(S["VAA"][tile], vaa[:], reads=[vaa.name], writes=[tk + ".vaa"], q="pool")
            yield

        _pending = list(range(0, NT))
        _active = []
        while _pending or _active:
            if len(_active) < 2 and _pending:
                _active.append(tile_gen(_pending.pop(0)))
            for _g in list(_active):
                try:
                    next(_g)
                except StopIteration:
                    _active.remove(_g)
    C.P.barrier()


def emit_ab_attn(C, NL, S, sink_d, AO, aokey):
    nc, P = C.nc, C.P
    NT = 2 + NL
    with ExitStack() as es:
        def sb(name, shape, dt):
            return es.enter_context(nc.sbuf_tensor(C.name(name), shape, dt))

        def ps(name, shape, dt=F32):
            return es.enter_context(nc.psum_tensor(C.name(name), shape, dt))
        kT = sb("kT_all", [128, NT, 128], BF16)
        va = sb("va_all", [128, NT, 2, 65], BF16)
        for t in range(NT):
            P.dma(kT[:, t, :], S["QAT"][t][:, 4, :], reads=["ab.%d.qat" % t], writes=[kT.name + ".%d" % t])
            P.dma(va[:, t], S["VAA"][t], reads=["ab.%d.vaa" % t], writes=[va.name + ".%d" % t])
        mk = sb("mk", [128, 2, 128], F32)
        P.dma(mk[:], C.consts_d[2:4].rearrange("k p n -> p k n"), writes=[mk.name])
        mkb = sb("mkb", [128, 2, 128], BF16)
        P.op("dve", lambda e: e.tensor_copy(out=mkb[:], in_=mk[:]), reads=[mk.name], writes=[mkb.name])
        sk = sb("sink", [128, 8], F32)
        P.dma(sk[:], sink_d.partition_broadcast(128), writes=[sk.name])
        P.op("act", lambda e: e.activation(out=sk[:], in_=sk[:], func=AF.Exp), reads=[sk.name], writes=[sk.name])
        qt = [sb("qt%d" % i, [128, 4, 128], BF16) for i in range(2)]
        E = [sb("E%d" % i, [128, 5, 128], BF16) for i in range(2)]
        pE = [ps("pE%d" % i, [128, 1024]) for i in range(2)]
        pO = ps("pO", [128, 2, 512])
        den = sb("den", [128, 8, 2], F32)
        ao = [sb("ao%d" % i, [128, 8, 64], F32) for i in range(2)]
        for qtile in range(NT):
            if qtile < 2:
                blocks = [(0, None), (1, None)]
            else:
                n = qtile - 2
                blocks = [(0, None), (1, None)]
                if n >= 1:
                    blocks.append((qtile - 1, 1))
                blocks.append((qtile, None))
                if n + 1 < NL:
                    blocks.append((qtile + 1, 0))
            nb = len(blocks)
            q = qt[qtile % 2]
            P.dma(q[:], S["QAT"][qtile][:, 0:4, :], reads=["ab.%d.qat" % qtile], writes=[q.name])
            for hq in range(8):
                j, half = hq % 4, hq // 4
                p0, p1 = half * 64, (half + 1) * 64
                pe_ = pE[hq % 2]
                Eb = E[hq % 2]
                for bi, (blk, _) in enumerate(blocks):
                    P.op("pe", lambda e, bi=bi, blk=blk, p0=p0, p1=p1, j=j, pe_=pe_, q=q: e.matmul(pe_[:, bi * 128:(bi + 1) * 128], lhsT=kT[p0:p1, blk, :], rhs=q[p0:p1, j, :], start=True, stop=True),
                         reads=[kT.name + ".%d" % blk, q.name], writes=[pe_.name], track=(bi == nb - 1))
                P.op("act", lambda e, pe_=pe_, Eb=Eb, nb=nb: e.activation(out=Eb[:, 0:nb, :], in_=pe_[:, 0:nb * 128].rearrange("p (b t) -> p b t", b=nb), func=AF.Exp, scale=0.125),
                     reads=[pe_.name], writes=[Eb.name])
                for bi, (blk, m) in enumerate(blocks):
                    if m is not None:
                        P.op("dve", lambda e, bi=bi, m=m, Eb=Eb: e.tensor_tensor(out=Eb[:, bi, :], in0=Eb[:, bi, :], in1=mkb[:, m, :], op=ALU.mult),
                             reads=[Eb.name, mkb.name], writes=[Eb.name])
                o = pO[:, hq // 4, (hq % 4) * 65:(hq % 4) * 65 + 65]
                for bi, (blk, _) in enumerate(blocks):
                    P.op("pe", lambda e, bi=bi, blk=blk, half=half, Eb=Eb, o=o: e.matmul(o, lhsT=Eb[:, bi, :], rhs=va[:, blk, half, :], start=(bi == 0), stop=(bi == nb - 1)),
                         reads=[Eb.name, va.name + ".%d" % blk], writes=["pO"], track=(bi == nb - 1))
            pv = pO[:, :, 0:260].rearrange("p a (h x) -> p a h x", h=4)
            a_ = ao[qtile % 2]
            dv = den[:].rearrange("p (a h) x -> p a h x", a=2)
            P.op("dve", lambda e: e.tensor_tensor(out=dv[:, :, :, 0:1], in0=pv[:, :, :, 64:65], in1=sk[:].rearrange("p (a h) -> p a h", a=2).unsqueeze(3), op=ALU.add),
                 reads=["pO", sk.name], writes=[den.name])
            P.op("dve", lambda e: e.reciprocal(out=den[:, :, 1:2], in_=den[:, :, 0:1]), reads=[den.name], writes=[den.name])
            P.op("dve", lambda e, a_=a_: e.tensor_tensor(out=a_[:].rearrange("p (a h) x -> p a h x", a=2), in0=pv[:, :, :, 0:64],
                                                         in1=dv[:, :, :, 1:2].to_broadcast([128, 2, 4, 64]), op=ALU.mult),
                 reads=["pO", den.name], writes=[a_.name])
            P.dma(AO[qtile], a_[:].rearrange("p h x -> p (h x)"), reads=[a_.name], writes=["%s.%d" % (aokey, qtile)], q="pool")
    C.P.barrier()


def head_norm_rows(C, hm, sq, stt, nheads, dh, hk):
    P = C.P
    P.op("dve", lambda e: e.tensor_tensor(out=sq[:], in0=hm[:], in1=hm[:], op=ALU.mult), reads=[hk], writes=[sq.name])
    P.op("dve", lambda e: e.reduce_sum(out=stt[:, 0:nheads], in_=sq[:], axis=AX.X), reads=[sq.name], writes=[stt.name])
    P.op("dve", lambda e: e.tensor_scalar(out=stt[:, 0:nheads], in0=stt[:, 0:nheads], scalar1=1.0 / dh, scalar2=LN_EPS, op0=ALU.mult, op1=ALU.add), reads=[stt.name], writes=[stt.name])
    P.op("act", lambda e: e.activation(out=stt[:, 0:nheads], in_=stt[:, 0:nheads], func=AF.Sqrt), reads=[stt.name], writes=[stt.name])
    P.op("dve", lambda e: e.reciprocal(out=stt[:, 0:nheads], in_=stt[:, 0:nheads]), reads=[stt.name], writes=[stt.name])
    P.op("dve", lambda e: e.tensor_tensor(out=hm[:], in0=hm[:], in1=stt[:, 0:nheads].unsqueeze(2).to_broadcast([128, nheads, dh]), op=ALU.mult), reads=[hk, stt.name], writes=[hk])


def emit_merge(C, NL, n_ctx_out, kind, S, HO, hokey, AO, aokey, x_d, xkey, ctx_d, ckey, modd, norm_w_d, w_out_d, lnw_d, lnb_d, x1_d, x1key, c1_d, c1key):
    nc, P = C.nc, C.P
    NT = 2 + NL
    with ExitStack() as es:
        def sb(name, shape, dt):
            return es.enter_context(nc.sbuf_tensor(C.name(name), shape, dt))

        def ps(name, shape, dt=F32):
            return es.enter_context(nc.psum_tensor(C.name(name), shape, dt))
        wbf = sb("w_out", [128, 8, D], BF16)
        load_w_bf16(C, wbf, w_out_d, D, wbf.name)
        nw = D // 2 if kind == "ab" else D
        normw = sb("normw", [128, nw], F32)
        P.dma(normw[:], norm_w_d.partition_broadcast(128), writes=[normw.name])
        g1 = sb("g1", [128, 2, D], F32)
        for r in range(2):
            P.dma(g1[:, r, :], modd[r, 2].partition_broadcast(128), reads=["modd"], writes=[g1.name + ".%d" % r])
        lnw = sb("lnw", [128, D], F32)
        lnb = sb("lnb", [128, D], F32)
        P.dma(lnw[:], lnw_d.partition_broadcast(128), writes=[lnw.name])
        P.dma(lnb[:], lnb_d.partition_broadcast(128), writes=[lnb.name])
        h0_ = [sb("h0%d" % i_, [128, nw], F32) for i_ in range(2)]
        h1_ = [sb("h1%d" % i_, [128, nw], F32) for i_ in range(2)]
        sq_ = [sb("sq%d" % i_, [128, nw], F32) for i_ in range(2)]
        gt_ = [sb("gt%d" % i_, [128, nw], F32) for i_ in range(2)]
        cat_ = [sb("cat%d" % i_, [128, D], F32) for i_ in range(2)]
        catT_ = [sb("catT%d" % i_, [128, 8, 128], BF16) for i_ in range(2)]
        xt_ = [sb("xt%d" % i_, [128, D], F32) for i_ in range(2)]
        ytmp_ = [sb("ytmp%d" % i_, [128, D], F32) for i_ in range(2)]
        yo_ = [sb("yo%d" % i_, [128, D], F32) for i_ in range(2)]
        stt_ = [sb("stt%d" % i_, [128, 4], F32) for i_ in range(2)]
        st_ = [sb("st%d" % i_, [128, 4], F32) for i_ in range(2)]
        lt_ = [sb("lt%d" % i_, [128, D], F32) for i_ in range(2)]
        pb = [ps("pbm%d" % i, [128, 1024]) for i in range(2)]
        first = 0 if n_ctx_out else 2
        def tile_gen(tile):
            h0 = h0_[tile % 2]
            h1 = h1_[tile % 2]
            sq = sq_[tile % 2]
            gt = gt_[tile % 2]
            cat = cat_[tile % 2]
            catT = catT_[tile % 2]
            xt = xt_[tile % 2]
            ytmp = ytmp_[tile % 2]
            yo = yo_[tile % 2]
            lt = lt_[tile % 2]
            stt = stt_[tile % 2]
            st = st_[tile % 2]
            is_ctx = tile < 2
            P.dma(h0[:], HO[0][tile].rearrange("p h x -> p (h x)"), reads=["%s.ho0.%d" % (hokey, tile)], writes=[h0.name])
            P.dma(h1[:], HO[1][tile].rearrange("p h x -> p (h x)"), reads=["%s.ho1.%d" % (hokey, tile)], writes=[h1.name])
            P.dma(gt[:], S["OM"][tile], reads=["%s.%d.om" % (kind, tile)], writes=[gt.name])
            P.op("dve", lambda e: e.tensor_tensor(out=h0[:], in0=h0[:], in1=h1[:], op=ALU.add), reads=[h0.name, h1.name], writes=[h0.name])
            dh = nw // 4
            hv = V(h0[:].rearrange("p (h x) -> p h x", h=4), h0.name)
            sv = V(sq[:].rearrange("p (h x) -> p h x", h=4), sq.name)
            head_norm_rows(C, hv, sv, stt, 4, dh, h0.name)
            yield
            P.op("dve", lambda e: e.tensor_tensor(out=h0[:], in0=h0[:], in1=normw[:], op=ALU.mult), reads=[h0.name, normw.name], writes=[h0.name])
            P.op("act", lambda e: e.activation(out=gt[:], in_=gt[:], func=(AF.Sigmoid if kind == "ab" else AF.Silu)), reads=[gt.name], writes=[gt.name])
            P.op("dve", lambda e: e.tensor_tensor(out=cat[:, 0:nw], in0=h0[:], in1=gt[:], op=ALU.mult), reads=[h0.name, gt.name], writes=[cat.name + ".0"])
            rk = [cat.name + ".0"]
            if kind == "ab":
                P.dma(cat[:, nw:D], AO[tile], reads=["%s.%d" % (aokey, tile)], writes=[cat.name + ".1"])
                rk.append(cat.name + ".1")
            for dc in range(8):
                P.op("pe", lambda e, dc=dc: e.transpose(out=pb[0][:, dc * 128:(dc + 1) * 128], in_=cat[:, dc * 128:(dc + 1) * 128], identity=C.ident[:]),
                     reads=rk + ["c_ident"], writes=[pb[0].name], track=(dc == 7))
            P.op("act", lambda e: e.copy(out=catT[:], in_=pb[0][:].rearrange("p (c t) -> p c t", c=8)), reads=[pb[0].name], writes=[catT.name])
            for hf in range(2):
                for dc in range(8):
                    P.op("pe", lambda e, dc=dc, hf=hf: e.matmul(pb[1][:, hf * 512:(hf + 1) * 512], lhsT=catT[:, dc, :], rhs=wbf[:, dc, hf * 512:(hf + 1) * 512], start=(dc == 0), stop=(dc == 7)),
                         reads=[catT.name, "%s.%d" % (wbf.name, dc)], writes=[pb[1].name], track=(dc == 7 and hf == 1))
            if is_ctx:
                src, skey, dst, dkey = ctx_d[tile * 128:(tile + 1) * 128, :], "%s.%d" % (ckey, tile), c1_d[tile * 128:(tile + 1) * 128, :], "%s.%d" % (c1key, tile)
            else:
                src, skey, dst, dkey = x_d[(tile - 2) * 128:(tile - 1) * 128, :], "%s.%d" % (xkey, tile - 2), x1_d[(tile - 2) * 128:(tile - 1) * 128, :], "%s.%d" % (x1key, tile - 2)
            r = 1 if is_ctx else 0
            P.dma(xt[:], src, reads=[skey], writes=[xt.name])
            P.op("dve", lambda e, r=r: e.tensor_tensor(out=ytmp[:], in0=pb[1][:], in1=g1[:, r, :], op=ALU.mult), reads=[pb[1].name, g1.name + ".%d" % r], writes=[ytmp.name])
            P.op("dve", lambda e: e.scalar_tensor_tensor(out=ytmp[:], in0=xt[:], scalar=ALPHA, in1=ytmp[:], op0=ALU.mult, op1=ALU.add), reads=[xt.name, ytmp.name], writes=[ytmp.name])
            layernorm_rows(P, nc, ytmp, None, lt, st, lnw, lnb, "m", yo)
            yield
            P.dma(dst, yo[:], reads=[yo.name], writes=[dkey], q="pool")
            yield

        _pending = list(range(first, NT))
        _active = []
        while _pending or _active:
            if len(_active) < 2 and _pending:
                _active.append(tile_gen(_pending.pop(0)))
            for _g in list(_active):
                try:
                    next(_g)
                except StopIteration:
                    _active.remove(_g)
    C.P.barrier()


def make_consts():
    i = np.arange(128)
    return np.stack([np.eye(128), np.tile(i.astype(np.float64), (128, 1)), (i[:, None] <= i[None, :]), (i[:, None] >= i[None, :]), np.ones((128, 128))]).astype(np.float32)


def make_rope(seq):
    rows = seq // 64
    row = np.repeat(np.arange(rows), 64).astype(np.float32)
    col = np.tile(np.arange(64), rows).astype(np.float32)
    inv = (10000.0 ** (-np.arange(16, dtype=np.float32) / 16)).astype(np.float32)
    ang = np.concatenate([row[:, None] * inv, col[:, None] * inv], -1).astype(np.float32)
    r = np.stack([np.cos(ang), np.sin(ang)], 1).astype(np.float32)
    return np.ascontiguousarray(r.reshape(seq // 128, 128, 2, 32))


def ab_scratch(nc, NT, pfx):
    S = {}
    S["QKT"] = nc.dram_tensor(pfx + "QKT", [NT, 128, 16, 128], BF16).ap()
    S["KT"] = nc.dram_tensor(pfx + "KT", [NT, 128, 8, 128], BF16).ap()
    S["VA"] = nc.dram_tensor(pfx + "VA", [NT, 128, 4, 129], BF16).ap()
    S["ET"] = nc.dram_tensor(pfx + "ET", [NT, 128, 8], F32).ap()
    S["OM"] = nc.dram_tensor(pfx + "OM", [NT, 128, 512], F32).ap()
    S["QAT"] = nc.dram_tensor(pfx + "QAT", [NT, 128, 5, 128], BF16).ap()
    S["VAA"] = nc.dram_tensor(pfx + "VAA", [NT, 128, 2, 65], BF16).ap()
    S["HO"] = [nc.dram_tensor(pfx + "HO%d" % d, [NT, 128, 4, 128], F32).ap() for d in range(2)]
    S["AO"] = nc.dram_tensor(pfx + "AO", [NT, 128, 512], F32).ap()
    return S


def emit_layer0_mixer(C, NL, x_d, xkey, ctx_d, ckey, modd, W, x1_d, x1key, c1_d, c1key):
    nc = C.nc
    NT = 2 + NL
    S = ab_scratch(nc, NT, "ab_")
    emit_ab_stage_a(C, NL, x_d, xkey, ctx_d, ckey, modd, W["ab_w_in"], W["ab_gate_b"], W["rope"], S)
    emit_scan(C, NT, 2, S["QKT"], S["KT"], S["VA"], S["ET"], S["HO"], 129, True, C.consts_d[2:4], "abs")
    emit_ab_attn(C, NL, S, W["ab_sink"], S["AO"], "ab.ao")
    emit_merge(C, NL, True, "ab", S, S["HO"], "abs", S["AO"], "ab.ao", x_d, xkey, ctx_d, ckey, modd, W["ab_norm_w"], W["ab_w_out"], W["lnw0"], W["lnb0"], x1_d, x1key, c1_d, c1key)


def emit_c_stage_a(C, NL, x_d, xkey, ctx_d, ckey, modd, w_in_d, gate_up_d, gate_b_d, S):
    nc, P = C.nc, C.P
    NT = 2 + NL
    with ExitStack() as es:
        def sb(name, shape, dt):
            return es.enter_context(nc.sbuf_tensor(C.name(name), shape, dt))

        def ps(name, shape, dt=F32):
            return es.enter_context(nc.psum_tensor(C.name(name), shape, dt))
        NC = 3104
        wbf = sb("w_in", [128, 8, NC], BF16)
        load_w_bf16(C, wbf, w_in_d, NC, wbf.name)
        modr = sb("modr", [128, 4, D], F32)
        for r, (row, k) in enumerate(((0, 1), (0, 0), (1, 1), (1, 0))):
            P.dma(modr[:, r, :], modd[row, k].partition_broadcast(128), reads=["modd"], writes=[modr.name + ".%d" % r])
        gb = sb("gb", [128, 1024], F32)
        P.dma(gb[:], gate_b_d.partition_broadcast(128), writes=[gb.name])
        gup = sb("gup", [16, 2, 512], F32)
        P.dma(gup[:], gate_up_d.rearrange("d r c -> r d c"), writes=[gup.name])
        tri = sb("tri", [128, 3, 128], F32)
        P.dma(tri[:], C.consts_d[2:5].rearrange("k p n -> p k n"), writes=["c_tri"])
        pb = [ps("pb%d" % i, [128, 512]) for i in range(7)]
        pcum = ps("pcum", [128, 512])
        xt_ = [sb("xt%d" % i_, [128, D], F32) for i_ in range(2)]
        hl_ = [sb("hl%d" % i_, [128, D], F32) for i_ in range(2)]
        hT_ = [sb("hT%d" % i_, [128, 8, 128], BF16) for i_ in range(2)]
        Pj_ = [sb("Pj%d" % i_, [128, NC], F32) for i_ in range(2)]
        lowT_ = [sb("lowT%d" % i_, [16, 2, 128], F32) for i_ in range(2)]
        LA_ = [sb("LA%d" % i_, [128, 1024], F32) for i_ in range(2)]
        CUM_ = [sb("CUM%d" % i_, [128, 1024], F32) for i_ in range(2)]
        EB_ = [sb("EB%d" % i_, [128, 1024], F32) for i_ in range(2)]
        EA_ = [sb("EA%d" % i_, [128, 1024], F32) for i_ in range(2)]
        ETt_ = [sb("ETt%d" % i_, [128, 8], F32) for i_ in range(2)]
        qk_ = [sb("qk%d" % i_, [128, 2, 2, 4, 128], F32) for i_ in range(2)]
        qkT_ = [sb("qkT%d" % i_, [128, 16, 128], BF16) for i_ in range(2)]
        ktb_ = [sb("ktb%d" % i_, [128, 8, 128], BF16) for i_ in range(2)]
        vab_ = [sb("vab%d" % i_, [128, 4, 256], BF16) for i_ in range(2)]
        def tile_gen(tile):
            xt = xt_[tile % 2]
            hl = hl_[tile % 2]
            hT = hT_[tile % 2]
            Pj = Pj_[tile % 2]
            lowT = lowT_[tile % 2]
            LA = LA_[tile % 2]
            CUM = CUM_[tile % 2]
            EB = EB_[tile % 2]
            EA = EA_[tile % 2]
            ETt = ETt_[tile % 2]
            qk = qk_[tile % 2]
            qkT = qkT_[tile % 2]
            ktb = ktb_[tile % 2]
            vab = vab_[tile % 2]
            is_ctx = tile < 2
            if is_ctx:
                src = (ctx_d[tile * 128:(tile + 1) * 128, :], "%s.%d" % (ckey, tile))
            else:
                src = (x_d[(tile - 2) * 128:(tile - 1) * 128, :], "%s.%d" % (xkey, tile - 2))
            yield from emit_in_proj(C, es, src, (2, 3) if is_ctx else (0, 1), wbf, wbf.name, NC, Pj, xt, hl, hT, pb, modr)
            tk = "c.%d" % tile
            for d in range(2):
                P.op("pe", lambda e, d=d: e.transpose(out=pcum[0:16, d * 128:(d + 1) * 128], in_=Pj[:, 3072 + 16 * d:3072 + 16 * (d + 1)], identity=C.ident[:]),
                     reads=[Pj.name, "c_ident"], writes=[pcum.name], track=(d == 1))
            P.op("act", lambda e: e.copy(out=lowT[:], in_=pcum[0:16, 0:256].rearrange("p (d t) -> p d t", d=2)), reads=[pcum.name], writes=[lowT.name])
            for d in range(2):
                P.op("pe", lambda e, d=d: e.matmul(pb[d][:, :], lhsT=lowT[:, d, :], rhs=gup[:, d, :], start=True, stop=True), reads=[lowT.name, gup.name], writes=[pb[d].name])
                P.op("dve", lambda e, d=d: e.tensor_tensor(out=LA[:, d * 512:(d + 1) * 512], in0=pb[d][:, :], in1=gb[:, d * 512:(d + 1) * 512], op=ALU.add),
                     reads=[pb[d].name, gb.name], writes=[LA.name])
            P.op("act", lambda e: e.activation(out=LA[:], in_=LA[:], func=AF.Exp, scale=-1.0), reads=[LA.name], writes=[LA.name])
            P.op("dve", lambda e: e.tensor_single_scalar(out=LA[:], in_=LA[:], scalar=1.0, op=ALU.add), reads=[LA.name], writes=[LA.name])
            P.op("act", lambda e: e.activation(out=LA[:], in_=LA[:], func=AF.Ln), reads=[LA.name], writes=[LA.name])
            P.op("dve", lambda e: e.tensor_single_scalar(out=LA[:], in_=LA[:], scalar=-1.0 / 16.0, op=ALU.mult), reads=[LA.name], writes=[LA.name])
            for d in range(2):
                P.op("pe", lambda e, d=d: e.matmul(pb[2 + d][:, :], lhsT=tri[:, d, :], rhs=LA[:, d * 512:(d + 1) * 512], start=True, stop=True), reads=[LA.name, "c_tri"], writes=[pb[2 + d].name])
                P.op("act", lambda e, d=d: e.copy(out=CUM[:, d * 512:(d + 1) * 512], in_=pb[2 + d][:, :]), reads=[pb[2 + d].name], writes=[CUM.name])
            for s in range(8):
                P.op("pe", lambda e, s=s: e.matmul(pcum[:, 256 + s:257 + s], lhsT=LA[:, s * 128:(s + 1) * 128], rhs=tri[:, 2, 0:1], start=True, stop=True),
                     reads=[LA.name, "c_tri"], writes=[pcum.name], track=(s == 7))
            P.op("act", lambda e: e.activation(out=ETt[:], in_=pcum[:, 256:264], func=AF.Exp), reads=[pcum.name], writes=[ETt.name])
            P.dma(S["ET"][tile], ETt[:], reads=[ETt.name], writes=[tk + ".et"], q="pool")
            yield
            P.op("act", lambda e: e.activation(out=EB[:], in_=CUM[:], func=AF.Exp), reads=[CUM.name], writes=[EB.name])
            P.op("act", lambda e: e.activation(out=EA[:], in_=CUM[:], func=AF.Exp, scale=-1.0), reads=[CUM.name], writes=[EA.name])
            P.op("dve", lambda e: e.tensor_single_scalar(out=EB[:], in_=EB[:], scalar=128.0 ** -0.5, op=ALU.mult), reads=[EB.name], writes=[EB.name])
            for w, (E_, c0) in enumerate(((EB, 0), (EA, 512))):
                for d in range(2):
                    P.op("dve", lambda e, w=w, d=d, E_=E_, c0=c0: e.tensor_tensor(
                        out=qk[:, w, d].rearrange("p h k -> p (h k)"), in0=Pj[:, c0:c0 + 512], in1=E_[:, d * 512:(d + 1) * 512], op=ALU.mult),
                        reads=[Pj.name, E_.name], writes=[qk.name + ".%d%d" % (w, d)])
            for s in range(16):
                w, d, h = s // 8, (s // 4) % 2, s % 4
                pp = pb[s // 4]
                P.op("pe", lambda e, s=s, w=w, d=d, h=h, pp=pp: e.transpose(out=pp[:, (s % 4) * 128:(s % 4 + 1) * 128], in_=qk[:, w, d, h, :], identity=C.ident[:]),
                     reads=[qk.name + ".%d%d" % (w, d), "c_ident"], writes=[pp.name], track=(s % 4 == 3))
            for g in range(4):
                P.op("act" if g % 2 == 0 else "dve", lambda e, g=g: (e.copy if g % 2 == 0 else e.tensor_copy)(out=qkT[:, g * 4:(g + 1) * 4, :], in_=pb[g][:].rearrange("p (c t) -> p c t", c=4)),
                     reads=[pb[g].name], writes=[qkT.name + ".%d" % g])
            P.dma(S["QKT"][tile], qkT[:], reads=[qkT.name + ".%d" % g for g in range(4)], writes=[tk + ".qkt"], q="pool")
            yield
            P.op("act", lambda e: e.copy(out=ktb[:], in_=qk[:, 1].rearrange("p d h k -> p (d h) k")), reads=[qk.name + ".10", qk.name + ".11"], writes=[ktb.name])
            P.dma(S["KT"][tile], ktb[:], reads=[ktb.name], writes=[tk + ".kt"], q="pool")
            yield
            P.op("act", lambda e: e.copy(out=vab[:], in_=Pj[:, 1024:2048].rearrange("p (h k) -> p h k", h=4)), reads=[Pj.name], writes=[vab.name])
            P.dma(S["VA"][tile], vab[:], reads=[vab.name], writes=[tk + ".va"], q="pool")
            yield
            P.dma(S["OM"][tile], Pj[:, 2048:3072], reads=[Pj.name], writes=[tk + ".om"], q="pool")
            yield

        _pending = list(range(0, NT))
        _active = []
        while _pending or _active:
            if len(_active) < 2 and _pending:
                _active.append(tile_gen(_pending.pop(0)))
            for _g in list(_active):
                try:
                    next(_g)
                except StopIteration:
                    _active.remove(_g)
    C.P.barrier()


def c_scratch(nc, NT, pfx):
    S = {}
    S["QKT"] = nc.dram_tensor(pfx + "QKT", [NT, 128, 16, 128], BF16).ap()
    S["KT"] = nc.dram_tensor(pfx + "KT", [NT, 128, 8, 128], BF16).ap()
    S["VA"] = nc.dram_tensor(pfx + "VA", [NT, 128, 4, 256], BF16).ap()
    S["ET"] = nc.dram_tensor(pfx + "ET", [NT, 128, 8], F32).ap()
    S["OM"] = nc.dram_tensor(pfx + "OM", [NT, 128, 1024], F32).ap()
    S["HO"] = [nc.dram_tensor(pfx + "HO%d" % d, [NT, 128, 4, 256], F32).ap() for d in range(2)]
    return S


def emit_layer1_mixer(C, NL, x_d, xkey, ctx_d, ckey, modd, W, x1_d, x1key):
    nc = C.nc
    NT = 2 + NL
    S = c_scratch(nc, NT, "c_")
    emit_c_stage_a(C, NL, x_d, xkey, ctx_d, ckey, modd, W["gla_w_in"], W["gla_gate_up"], W["gla_gate_b"], S)
    emit_scan(C, NT, 2, S["QKT"], S["KT"], S["VA"], S["ET"], S["HO"], 256, False, C.consts_d[2:4], "cs")
    emit_merge(C, NL, False, "c", S, S["HO"], "cs", None, None, x_d, xkey, ctx_d, ckey, modd, W["gla_norm_w"], W["gla_w_out"], W["lnw0"], W["lnb0"], x1_d, x1key, None, None)


SEQ = 4096
NLAT = SEQ // 128


def build_full(NL=NLAT, peer_groups=None):
    nc = bass.Bass("TRN2", target_bir_lowering=False)
    P = Prog(nc)
    T = NL * 128

    def din(name, shape):
        return nc.dram_tensor(name, shape, F32, kind="ExternalInput").ap()

    def dscr(name, shape, dt=F32):
        return nc.dram_tensor(name, shape, dt).ap()
    consts = din("consts", [5, 128, 128])
    x = din("x", [T, D])
    ctx = din("ctx", [256, D])
    ccT = din("ccT", [128, 8, 2])
    wmod = din("w_mod", [2, D, 6 * D])
    bmod = din("b_mod", [2, 6 * D])
    lnw = din("ln_w", [2, 2, D])
    lnb = din("ln_b", [2, 2, D])
    W0 = dict(ab_w_in=din("ab_w_in", [D, 2832]), ab_gate_b=din("ab_gate_b", [16]), ab_norm_w=din("ab_norm_w", [512]), ab_sink=din("ab_sink", [8]),
              ab_w_out=din("ab_w_out", [D, D]), rope=din("rope", [NL, 128, 2, 32]), lnw0=lnw[0, 0], lnb0=lnb[0, 0])
    W1 = dict(gla_w_in=din("gla_w_in", [D, 3104]), gla_gate_up=din("gla_gate_up", [2, 16, 512]), gla_gate_b=din("gla_gate_b", [1024]), gla_norm_w=din("gla_norm_w", [D]),
              gla_w_out=din("gla_w_out", [D, D]), lnw0=lnw[1, 0], lnb0=lnb[1, 0])
    wq = din("peer_wq", [2, D, 2048])
    keys = din("peer_keys", [2, 16, 128, 128])
    ure = din("peer_ure", [2, 128, 128, 8, 128])
    pv = din("peer_v", [2, 16384, D])
    out = nc.dram_tensor("out", [T, D], F32, kind="ExternalOutput").ap()
    modd = [dscr("modd%d" % l, [2, 6, D]) for l in range(2)]
    x1 = dscr("x1", [T, D]); c1 = dscr("c1", [256, D])
    x2 = dscr("x2", [T, D]); c2 = dscr("c2", [256, D])
    x3 = dscr("x3", [T, D])
    ubf = dscr("ubf", [128, 128, 8, 128], BF16)
    vbf = dscr("vbf", [16384, D], BF16)
    C = Ctx(nc, P, consts)
    emit_peer_prep(C, ure[0], pv[0], ubf, vbf)
    emit_mod(C, ccT, wmod[0], bmod[0], modd[0])
    emit_layer0_mixer(C, NL, x, "x", ctx, "ctx", modd[0], W0, x1, "x1", c1, "c1")
    ng = None if peer_groups is None else peer_groups
    emit_peer(C, T, x1, "x1", x2, "x2", [modd[0][0, 4], modd[0][0, 3], modd[0][0, 5]], lnw[0, 1], lnb[0, 1], wq[0], keys[0], ure[0], pv[0], ubf, vbf, n_groups=ng)
    emit_peer(C, 256, c1, "c1", c2, "c2", [modd[0][1, 4], modd[0][1, 3], modd[0][1, 5]], lnw[0, 1], lnb[0, 1], wq[0], keys[0], ure[0], pv[0], ubf, vbf)
    emit_peer_prep(C, ure[1], pv[1], ubf, vbf)
    emit_mod(C, ccT, wmod[1], bmod[1], modd[1])
    emit_layer1_mixer(C, NL, x2, "x2", c2, "c2", modd[1], W1, x3, "x3")
    emit_peer(C, T, x3, "x3", out, "out", [modd[1][0, 4], modd[1][0, 3], modd[1][0, 5]], lnw[1, 1], lnb[1, 1], wq[1], keys[1], ure[1], pv[1], ubf, vbf, n_groups=ng)
    P.finish()
    return nc


def make_feeds(inputs, NL=NLAT):
    T = NL * 128
    f32 = lambda a: np.ascontiguousarray(np.asarray(a, dtype=np.float32))
    consts = make_consts()
    rope = make_rope(T)
    pu = np.asarray(inputs["peer_u"], dtype=np.float32)
    ure = np.ascontiguousarray(pu.reshape(2, 128, 128, 8, 128).transpose(0, 1, 4, 3, 2))
    shared = dict(consts=consts, rope=rope, w_mod=f32(inputs["w_mod"]), b_mod=f32(inputs["b_mod"]), ln_w=f32(inputs["ln_w"]), ln_b=f32(inputs["ln_b"]),
                  ab_w_in=f32(inputs["ab_w_in"][0]), ab_gate_b=f32(np.asarray(inputs["ab_gate_b"][0]).reshape(16)), ab_norm_w=f32(inputs["ab_norm_w"][0]),
                  ab_sink=f32(inputs["ab_sink"][0]), ab_w_out=f32(inputs["ab_w_out"][0]), gla_w_in=f32(inputs["gla_w_in"][0]), gla_gate_up=f32(inputs["gla_gate_up"][0]),
                  gla_gate_b=f32(np.asarray(inputs["gla_gate_b"][0]).reshape(1024)), gla_norm_w=f32(inputs["gla_norm_w"][0]), gla_w_out=f32(inputs["gla_w_out"][0]),
                  peer_wq=f32(inputs["peer_wq"]), peer_keys=f32(np.asarray(inputs["peer_keys"]).reshape(2, 16, 128, 128)), peer_ure=ure, peer_v=f32(inputs["peer_v"]))
    feeds = []
    xs = np.asarray(inputs["x"], dtype=np.float32)
    cs = np.asarray(inputs["c"], dtype=np.float32)
    cx = np.asarray(inputs["ctx"], dtype=np.float32)
    cctx = np.asarray(inputs["c_ctx"], dtype=np.float32)
    for b in range(xs.shape[0]):
        cc = np.stack([cs[b], cctx], -1)
        ccT = np.ascontiguousarray(cc.reshape(8, 128, 2).transpose(1, 0, 2))
        d = dict(shared)
        d.update(x=np.ascontiguousarray(xs[b, :T]), ctx=np.ascontiguousarray(cx[b]), ccT=ccT)
        feeds.append(d)
    return feeds


_NC_CACHE = {}


def kernel(**inputs):
    if "full" not in _NC_CACHE:
        _NC_CACHE["full"] = build_full()
    nc = _NC_CACHE["full"]
    feeds = make_feeds(inputs)
    res = run_bass_kernel_spmd(nc, feeds, core_ids=list(range(len(feeds))))
    return np.stack([r["out"] for r in res.results], 0).astype(np.float32)
```

```python
from contextlib import ExitStack
import numpy as np
import concourse.bass as bass
import concourse.mybir as mybir
from concourse.bass_utils import run_bass_kernel_spmd

F32 = mybir.dt.float32
BF16 = mybir.dt.bfloat16
U32 = mybir.dt.uint32
AF = mybir.ActivationFunctionType
ALU = mybir.AluOpType
AX = mybir.AxisListType

D = 1024
NKEY = 128
PH = 8
PK = 16


class Prog:
    def __init__(self, nc, n_dma_slots=12):
        self.nc = nc
        self.eng = {"pe": nc.tensor, "act": nc.scalar, "dve": nc.vector, "pool": nc.gpsimd, "sp": nc.sync}
        self.csem = {e: nc.alloc_semaphore("c_" + e) for e in ("pe", "act", "dve", "pool")}
        self.cnt = {e: 0 for e in self.csem}
        self.dslots = {}
        for q, n in (("sp", n_dma_slots), ("pool", 6), ("act", 4)):
            self.dslots[q] = [[nc.alloc_semaphore("d_%s_%d" % (q, i)), 0] for i in range(n)]
        self.dnext = {q: 0 for q in self.dslots}
        self.seen = {e: {} for e in self.eng}
        self.lastw = {}
        self.lastr = {}
        self.multi = {}
        self.n_ops = 0

    def _wait(self, e, tok):
        if tok is None:
            return
        sem, val, src = tok
        key = sem.num if hasattr(sem, "num") else id(sem)
        if self.seen[e].get(key, 0) >= val:
            return
        self.eng[e].wait_ge(sem, val)
        self.seen[e][key] = val

    def _deps(self, e, reads, writes):
        for b in reads:
            for t in self.multi.get(b, ()):
                self._wait(e, t)
            t = self.lastw.get(b)
            if t is not None and not (t[2] == e and e == "pe"):
                self._wait(e, t)
        for b in writes:
            t = self.lastw.get(b)
            if t is not None and t[2] != e:
                self._wait(e, t)
            for t in self.lastr.get(b, ()):
                if t[2] != e:
                    self._wait(e, t)

    def _record(self, tok, reads, writes):
        for b in reads:
            self.lastr.setdefault(b, []).append(tok)
            if len(self.lastr[b]) > 6:
                best = {}
                for t in self.lastr[b]:
                    k = (t[2], t[0].num if hasattr(t[0], "num") else id(t[0]))
                    if k not in best or best[k][1] < t[1]:
                        best[k] = t
                self.lastr[b] = list(best.values())
        for b in writes:
            self.lastw[b] = tok
            self.lastr[b] = []

    def op(self, e, fn, reads=(), writes=(), track=True):
        self._deps(e, reads, writes)
        inst = fn(self.eng[e])
        self.n_ops += 1
        if track:
            self.cnt[e] += 1
            inst.then_inc(self.csem[e], 1)
            tok = (self.csem[e], self.cnt[e], e)
        else:
            tok = (self.csem[e], self.cnt[e] + 1, e)
        self._record(tok, reads, writes)
        return tok

    def dma(self, out, in_, reads=(), writes=(), q="sp", **kw):
        slots = self.dslots[q]
        i = self.dnext[q]
        self.dnext[q] = (i + 1) % len(slots)
        sem, uses = slots[i]
        if uses > 0:
            self._wait(q, (sem, 16 * uses, "dma"))
        self._deps(q, reads, writes)
        self.eng[q].dma_start(out=out, in_=in_, **kw).then_inc(sem, 16)
        self.n_ops += 1
        slots[i][1] = uses + 1
        tok = (sem, 16 * (uses + 1), "dma")
        self._record(tok, reads, writes)
        return tok

    def barrier(self):
        toks = [(self.csem[e], self.cnt[e], e) for e in self.csem if self.cnt[e] > 0]
        for q, slots in self.dslots.items():
            toks += [(sem, 16 * uses, "dma") for sem, uses in slots if uses > 0]
        for e in self.eng:
            for t in toks:
                self._wait(e, t)
        self.lastw = {k: None for k in self.lastw}
        self.lastr = {}
        self.multi = {}

    def finish(self):
        for q, slots in self.dslots.items():
            for sem, uses in slots:
                if uses > 0:
                    self._wait("sp", (sem, 16 * uses, "dma"))


def bc(ap, shape):
    return ap.to_broadcast(shape)


ALPHA = 4.0 ** 0.25
LN_EPS = 1e-5


class Ctx:
    def __init__(self, nc, P, consts_d):
        self.nc = nc
        self.P = P
        self.uid = 0
        self.consts_d = consts_d
        self.ident = nc.alloc_sbuf_tensor("c_ident", [128, 128], F32)
        self.iota = nc.alloc_sbuf_tensor("c_iota", [128, 128], F32)
        P.dma(self.ident[:], consts_d[0], writes=["c_ident"])
        P.dma(self.iota[:], consts_d[1], writes=["c_iota"])

    def name(self, s):
        self.uid += 1
        return "%s_%d" % (s, self.uid)


class V:
    def __init__(self, ap, name):
        self.ap = ap
        self.name = name

    def __getitem__(self, k):
        return self.ap if k == slice(None) else self.ap[k]


def layernorm_rows(P, nc, y, yn, tmp, st, lnw, lnb, tag, out):
    yk, tk, sk = y.name, tmp.name, st.name
    P.op("dve", lambda e: e.reduce_sum(out=st[:, 0:1], in_=y[:], axis=AX.X), reads=[yk], writes=[sk])
    P.op("dve", lambda e: e.tensor_single_scalar(out=st[:, 1:2], in_=st[:, 0:1], scalar=-1.0 / D, op=ALU.mult), reads=[sk], writes=[sk])
    P.op("dve", lambda e: e.tensor_scalar(out=y[:], in0=y[:], scalar1=st[:, 1:2], scalar2=None, op0=ALU.add), reads=[yk, sk], writes=[yk])
    P.op("act", lambda e: e.activation(out=tmp[:], in_=y[:], func=AF.Square, accum_out=st[:, 2:3]), reads=[yk], writes=[tk, sk])
    P.op("dve", lambda e: e.tensor_scalar(out=st[:, 3:4], in0=st[:, 2:3], scalar1=1.0 / D, scalar2=LN_EPS, op0=ALU.mult, op1=ALU.add), reads=[sk], writes=[sk])
    P.op("act", lambda e: e.activation(out=st[:, 3:4], in_=st[:, 3:4], func=AF.Sqrt), reads=[sk], writes=[sk])
    P.op("dve", lambda e: e.reciprocal(out=st[:, 3:4], in_=st[:, 3:4]), reads=[sk], writes=[sk])
    P.op("dve", lambda e: e.tensor_scalar(out=y[:], in0=y[:], scalar1=st[:, 3:4], scalar2=None, op0=ALU.mult), reads=[yk, sk], writes=[yk])
    P.op("dve", lambda e: e.tensor_tensor(out=y[:], in0=y[:], in1=lnw[:], op=ALU.mult), reads=[yk, lnw.name], writes=[yk])
    P.op("dve", lambda e: e.tensor_tensor(out=out[:], in0=y[:], in1=lnb[:], op=ALU.add), reads=[yk, lnb.name], writes=[out.name])


def top16x2(P, nc, srcs, srck, scrs, vals, idxs, outk):
    n = srcs[0].shape[-1]
    ks = [[k + ".c%d" % q for k in outk] for q in range(2)]
    for q in range(2):
        P.op("dve", lambda e, q=q: e.max(out=vals[q][:, 0:8], in_=srcs[q]), reads=[srck], writes=ks[q])
    yield
    for q in range(2):
        P.op("dve", lambda e, q=q: e.max_index(out=idxs[q][:, 0:8], in_max=vals[q][:, 0:8], in_values=srcs[q]), reads=[srck] + ks[q], writes=ks[q])
    yield
    for q in range(2):
        P.op("dve", lambda e, q=q: e.match_replace(out=scrs[q][:, 0:n], in_to_replace=vals[q][:, 0:8], in_values=srcs[q], imm_value=-1e30), reads=[srck] + ks[q], writes=[scrs[q].name])
    yield
    for q in range(2):
        P.op("dve", lambda e, q=q: e.max(out=vals[q][:, 8:16], in_=scrs[q][:, 0:n]), reads=[scrs[q].name], writes=ks[q])
    yield
    for q in range(2):
        P.op("dve", lambda e, q=q: e.max_index(out=idxs[q][:, 8:16], in_max=vals[q][:, 8:16], in_values=scrs[q][:, 0:n]), reads=[scrs[q].name] + ks[q], writes=outk + ks[q])
    yield


def top16(P, nc, src_ap, srck, scratch, vals_ap, idx_ap, outk):
    sk = scratch.name
    n = src_ap.shape[-1]
    P.op("dve", lambda e: e.max(out=vals_ap[:, 0:8], in_=src_ap), reads=[srck], writes=outk)
    P.op("dve", lambda e: e.max_index(out=idx_ap[:, 0:8], in_max=vals_ap[:, 0:8], in_values=src_ap), reads=[srck] + outk, writes=outk)
    P.op("dve", lambda e: e.match_replace(out=scratch[:, 0:n], in_to_replace=vals_ap[:, 0:8], in_values=src_ap, imm_value=-1e30), reads=[srck] + outk, writes=[sk])
    P.op("dve", lambda e: e.max(out=vals_ap[:, 8:16], in_=scratch[:, 0:n]), reads=[sk], writes=outk)
    P.op("dve", lambda e: e.max_index(out=idx_ap[:, 8:16], in_max=vals_ap[:, 8:16], in_values=scratch[:, 0:n]), reads=[sk] + outk, writes=outk)


def emit_peer(C, T, x_in, xin_key, x_out, xout_key, mod_d, lnw_d, lnb_d, wq_d, keys_d, ure_d, v_d, ubf_d, vbf_d, n_groups=None, after_weights=None):
    nc, P = C.nc, C.P
    TG = 256
    NG = T // TG if n_groups is None else n_groups
    with ExitStack() as es:
        def sb(name, shape, dt):
            return es.enter_context(nc.sbuf_tensor(C.name(name), shape, dt))

        def ps(name, shape, dt=F32):
            return es.enter_context(nc.psum_tensor(C.name(name), shape, dt))

        wq = sb("wq", [128, 8, 2048], BF16)
        for dc in range(8):
            P.dma(wq[:, dc, :], wq_d[dc * 128:(dc + 1) * 128, :], writes=["%s.%d" % (wq.name, dc)], q="pool")
        if after_weights is not None:
            after_weights()
        keysT = sb("keysT", [128, 16, 128], BF16)
        S = sb("S", [128, 2048], F32)
        ktmp = V(S[:].rearrange("p (h k) -> p h k", h=16), S.name)
        P.dma(ktmp[:], keys_d.rearrange("h n k -> n h k"), writes=[ktmp.name])
        modr = sb("modr", [128, 3, D], F32)
        for r in range(3):
            P.dma(modr[:, r, :], mod_d[r].partition_broadcast(128), writes=["%s.%d" % (modr.name, r)])
        lnw = sb("lnw", [128, D], F32)
        lnb = sb("lnb", [128, D], F32)
        P.dma(lnw[:], lnw_d.partition_broadcast(128), writes=[lnw.name])
        P.dma(lnb[:], lnb_d.partition_broadcast(128), writes=[lnb.name])

        pbig = [ps("pbig%d" % i, [128, 1024]) for i in range(2)]
        psm = [ps("psm%d" % i, [128, 512]) for i in range(4)]
        for hp in range(16):
            pt = psm[hp % 4]
            P.op("pe", lambda e, pt=pt, hp=hp: e.transpose(out=pt[:, 0:128], in_=ktmp[:, hp, :], identity=C.ident[:]),
                 reads=[ktmp.name, "c_ident"], writes=[pt.name])
            P.op("act", lambda e, pt=pt, hp=hp: e.copy(out=keysT[:, hp, :], in_=pt[:, 0:128]), reads=[pt.name], writes=[keysT.name])

        xf = sb("xf", [128, D], F32)
        xe = [sb("xe%d" % i, [128, D], F32) for i in range(2)]
        hm = sb("hm", [128, D], F32)
        hT = [sb("hT%d" % b, [128, 8, TG], BF16) for b in range(2)]
        qT = sb("qT", [128, 16, TG], BF16)
        scr2 = [sb("scr%d" % i, [128, 256], F32) for i in range(2)]
        stop = sb("stop", [128, 16, 16], F32)
        itopu = sb("itopu", [128, 16, 16], U32)
        itopf = sb("itopf", [128, 16, 16], F32)
        cand = sb("cand", [128, 8, 256], F32)
        eq = V(cand[:].rearrange("p h (a b) -> p h a b", a=16), cand.name)
        best = sb("best", [128, 8, 16], F32)
        posu = sb("posu", [128, 8, 16], U32)
        abu = sb("abu", [128, 2, 8, 16], U32)
        abf = sb("abf", [128, 2, 8, 16], F32)
        IJG = sb("IJG", [128, 3, 128], F32)
        IJGT = [sb("IJGT%d" % b, [128, 3, TG], BF16) for b in range(2)]
        GKF = [sb("GKF%d" % b, [128, TG], BF16) for b in range(2)]
        iotab = sb("iotab", [128, 128], BF16)
        P.op("dve", lambda e: e.tensor_copy(out=iotab[:], in_=C.iota[:]), reads=["c_iota"], writes=[iotab.name])
        sm = sb("sm", [128, 8, 4], F32)
        NB = 8
        oig = [sb("oig%d" % i, [128, NB, 128], BF16) for i in range(2)]
        oj = [sb("oj%d" % i, [128, NB, 128], BF16) for i in range(2)]
        Gall = sb("Gall", [128, TG, 128], BF16)
        UB = 1
        NUB = 3
        ubuf = [sb("ubuf%d" % i, [128, UB, 8, 128], BF16) for i in range(NUB)]
        vbuf = [sb("vbuf%d" % i, [128, UB, D], BF16) for i in range(NUB)]
        Ag = [sb("Ag%d" % i, [128, TG], BF16) for i in range(2)]
        GA = [sb("GA%d" % i, [128, TG], BF16) for i in range(2)]
        ybuf = hm
        ytmp = V(S[:, 0:D], S.name)
        yout = V(S[:, D:2 * D], S.name)
        st = sb("st", [128, 4], F32)

        def front(g):
            bsel = g % 2
            t0 = g * TG
            hTb, IJb = hT[bsel], IJGT[bsel]
            for tt in range(2):
                xb = xf
                xk = xb.name
                P.dma(xb[:], x_in[t0 + tt * 128: t0 + (tt + 1) * 128, :], reads=["%s.%d" % (xin_key, (t0 + tt * 128) // 128)], writes=[xk])
                P.op("dve", lambda e, xb=xb: e.tensor_tensor(out=hm[:], in0=xb[:], in1=modr[:, 0, :], op=ALU.mult), reads=[xk, modr.name + ".0"], writes=[hm.name])
                P.op("dve", lambda e: e.tensor_tensor(out=hm[:], in0=hm[:], in1=modr[:, 1, :], op=ALU.add), reads=[hm.name, modr.name + ".1"], writes=[hm.name])
                for dc in range(8):
                    pb = psm[dc // 4]
                    P.op("pe", lambda e, dc=dc, pb=pb: e.transpose(out=pb[:, (dc % 4) * 128:(dc % 4 + 1) * 128], in_=hm[:, dc * 128:(dc + 1) * 128], identity=C.ident[:]),
                         reads=[hm.name, "c_ident"], writes=[pb.name], track=(dc % 4 == 3))
                for hb in range(2):
                    P.op("act", lambda e, tt=tt, hb=hb, hTb=hTb: e.copy(out=hTb[:, hb * 4:(hb + 1) * 4, tt * 128:(tt + 1) * 128], in_=psm[hb][:].rearrange("p (c t) -> p c t", c=4)),
                         reads=[psm[hb].name], writes=[hTb.name])
                yield
            for hp in range(16):
                pt = psm[hp % 2]
                for dc in range(8):
                    P.op("pe", lambda e, hp=hp, dc=dc, pt=pt, hTb=hTb: e.matmul(pt[:, 0:TG], lhsT=wq[:, dc, hp * 128:(hp + 1) * 128], rhs=hTb[:, dc, :], start=(dc == 0), stop=(dc == 7)),
                         reads=["%s.%d" % (wq.name, dc), hTb.name], writes=[pt.name], track=(dc == 7))
                P.op("act", lambda e, hp=hp, pt=pt: e.copy(out=qT[:, hp, :], in_=pt[:, 0:TG]), reads=[pt.name], writes=[qT.name])
                yield
            for tt in range(2):
                for grp in range(4):
                    pb = psm[grp % 2]
                    for k in range(4):
                        hp = grp * 4 + k
                        P.op("pe", lambda e, hp=hp, pb=pb, k=k, tt=tt: e.matmul(pb[:, k * 128:(k + 1) * 128], lhsT=qT[:, hp, tt * 128:(tt + 1) * 128], rhs=keysT[:, hp, :], start=True, stop=True),
                             reads=[qT.name, keysT.name], writes=[pb.name], track=(k == 3))
                    P.op("act", lambda e, grp=grp, pb=pb: e.copy(out=S[:, grp * 512:(grp + 1) * 512], in_=pb[:]), reads=[pb.name], writes=[S.name])
                    yield
                for hp in range(0, 16, 2):
                    yield from top16x2(P, nc, [S[:, (hp + q) * 128:(hp + q + 1) * 128] for q in range(2)], S.name, scr2, [stop[:, hp + q, :] for q in range(2)],
                                       [itopu[:, hp + q, :] for q in range(2)], [stop.name, itopu.name])
                P.op("dve", lambda e: e.tensor_copy(out=itopf[:], in_=itopu[:]), reads=[itopu.name], writes=[itopf.name])
                sv = stop[:].rearrange("p (h two) k -> p h two k", two=2)
                P.op("dve", lambda e: e.tensor_tensor(out=cand[:].rearrange("p h (a b) -> p h a b", a=16),
                                                      in0=sv[:, :, 0, :].unsqueeze(3).to_broadcast([128, 8, 16, 16]),
                                                      in1=sv[:, :, 1, :].unsqueeze(2).to_broadcast([128, 8, 16, 16]), op=ALU.add),
                     reads=[stop.name], writes=[cand.name])
                yield
                for h in range(0, 8, 2):
                    yield from top16x2(P, nc, [cand[:, h + q, :] for q in range(2)], cand.name, scr2, [best[:, h + q, :] for q in range(2)],
                                       [posu[:, h + q, :] for q in range(2)], [best.name, posu.name])
                P.op("dve", lambda e: e.tensor_single_scalar(out=abu[:, 0], in_=posu[:], scalar=4, op=ALU.logical_shift_right), reads=[posu.name], writes=[abu.name])
                P.op("dve", lambda e: e.tensor_single_scalar(out=abu[:, 1], in_=posu[:], scalar=15, op=ALU.bitwise_and), reads=[posu.name], writes=[abu.name])
                P.op("dve", lambda e: e.tensor_copy(out=abf[:], in_=abu[:]), reads=[abu.name], writes=[abf.name])
                yield
                iv = itopf[:].rearrange("p (h two) k -> p h two k", two=2)
                for w in range(2):
                    P.op("dve", lambda e, w=w: e.tensor_tensor(out=eq[:], in0=abf[:, w].unsqueeze(3).to_broadcast([128, 8, 16, 16]),
                                                               in1=C.iota[:, 0:16].unsqueeze(1).unsqueeze(1).to_broadcast([128, 8, 16, 16]), op=ALU.is_equal),
                         reads=[abf.name, "c_iota"], writes=[eq.name])
                    P.op("dve", lambda e, w=w: e.tensor_tensor(out=eq[:], in0=eq[:], in1=iv[:, :, w, :].unsqueeze(2).to_broadcast([128, 8, 16, 16]), op=ALU.mult),
                         reads=[eq.name, itopf.name], writes=[eq.name])
                    P.op("dve", lambda e, w=w: e.reduce_sum(out=IJG[:, w, :].rearrange("p (h k) -> p h k", h=8), in_=eq[:], axis=AX.X),
                         reads=[eq.name], writes=[IJG.name])
                    yield
                gk = IJG[:, 2, :].rearrange("p (h k) -> p h k", h=8)
                P.op("dve", lambda e: e.tensor_tensor(out=gk, in0=best[:], in1=best[:, :, 0:1].to_broadcast([128, 8, 16]), op=ALU.subtract),
                     reads=[best.name], writes=[IJG.name])
                P.op("act", lambda e: e.activation(out=gk, in_=gk, func=AF.Exp), reads=[IJG.name], writes=[IJG.name])
                P.op("dve", lambda e: e.reduce_sum(out=sm[:, :, 0:1], in_=gk, axis=AX.X), reads=[IJG.name], writes=[sm.name])
                P.op("dve", lambda e: e.reciprocal(out=sm[:, :, 1:2], in_=sm[:, :, 0:1]), reads=[sm.name], writes=[sm.name])
                P.op("dve", lambda e: e.tensor_tensor(out=gk, in0=gk, in1=sm[:, :, 1:2].to_broadcast([128, 8, 16]), op=ALU.mult),
                     reads=[IJG.name, sm.name], writes=[IJG.name])
                yield
                for w in range(3):
                    pt = psm[w % 2]
                    P.op("pe", lambda e, w=w, pt=pt: e.transpose(out=pt[:, 0:128], in_=IJG[:, w, :], identity=C.ident[:]), reads=[IJG.name, "c_ident"], writes=[pt.name])
                    if w < 2:
                        P.op("act", lambda e, w=w, pt=pt, tt=tt, IJb=IJb: e.copy(out=IJb[:, w, tt * 128:(tt + 1) * 128], in_=pt[:, 0:128]), reads=[pt.name], writes=[IJb.name])
                    else:
                        P.op("act", lambda e, pt=pt, tt=tt, bsel=bsel: e.copy(out=GKF[bsel][:, tt * 128:(tt + 1) * 128], in_=pt[:, 0:128]), reads=[pt.name], writes=[GKF[bsel].name])
                yield

        def gbuild(g):
            IJb = IJGT[g % 2]
            GKf = GKF[g % 2]
            for tb in range(TG // NB):
                s = tb % 2
                t0_ = tb * NB
                iob = iotab[:].unsqueeze(1).to_broadcast([128, NB, 128])
                P.op("dve", lambda e, s=s, t0_=t0_, iob=iob: e.tensor_tensor(out=oig[s][:], in0=iob, in1=IJb[:, 0, t0_:t0_ + NB].unsqueeze(2).to_broadcast([128, NB, 128]), op=ALU.is_equal),
                     reads=[iotab.name, IJb.name], writes=[oig[s].name])
                P.op("dve", lambda e, s=s, t0_=t0_, iob=iob: e.tensor_tensor(out=oj[s][:], in0=iob, in1=IJb[:, 1, t0_:t0_ + NB].unsqueeze(2).to_broadcast([128, NB, 128]), op=ALU.is_equal),
                     reads=[iotab.name, IJb.name], writes=[oj[s].name])
                P.op("dve", lambda e, s=s, t0_=t0_: e.tensor_tensor(out=oig[s][:], in0=oig[s][:], in1=GKf[:, t0_:t0_ + NB].unsqueeze(2).to_broadcast([128, NB, 128]), op=ALU.mult),
                     reads=[oig[s].name, GKf.name], writes=[oig[s].name])
                for qd in range(NB // 4):
                    pt = psm[(tb * (NB // 4) + qd) % 4]
                    for k in range(4):
                        kk = qd * 4 + k
                        P.op("pe", lambda e, s=s, k=k, kk=kk, pt=pt: e.matmul(pt[:, k * 128:(k + 1) * 128], lhsT=oj[s][:, kk, :], rhs=oig[s][:, kk, :], start=True, stop=True),
                             reads=[oj[s].name, oig[s].name], writes=[pt.name], track=(k == 3))
                    P.op("act", lambda e, t0_=t0_, qd=qd, pt=pt: e.copy(out=Gall[:, t0_ + qd * 4:t0_ + qd * 4 + 4, :], in_=pt[:].rearrange("p (k i) -> p k i", k=4)),
                         reads=[pt.name], writes=[Gall.name])

        def load_uv(blk):
            s = blk % NUB
            P.dma(ubuf[s][:], ubf_d[blk * UB:(blk + 1) * UB].rearrange("i p c e -> p i c e"), reads=["ubf%d" % (blk * UB // 8)], writes=[ubuf[s].name])
            P.dma(vbuf[s][:], vbf_d[blk * UB * 128:(blk + 1) * UB * 128, :].rearrange("(i e) d -> e i d", i=UB), reads=["vbf%d" % (blk * UB // 8)], writes=[vbuf[s].name])

        def mainloop(g, gen):
            hTb = hT[g % 2]

            def a_mm(i):
                s, ii = (i // UB) % NUB, i % UB
                pa = psm[2 + i % 2]
                for dc in range(8):
                    P.op("pe", lambda e, s=s, ii=ii, dc=dc, pa=pa: e.matmul(pa[:, 0:TG], lhsT=ubuf[s][:, ii, dc, :], rhs=hTb[:, dc, :], start=(dc == 0), stop=(dc == 7)),
                         reads=[ubuf[s].name, hTb.name], writes=[pa.name], track=(dc == 7))
                P.op("act", lambda e, i=i, pa=pa: e.activation(out=Ag[i % 2][:], in_=pa[:, 0:TG], func=AF.Gelu), reads=[pa.name], writes=[Ag[i % 2].name])
                P.op("dve", lambda e, i=i: e.tensor_tensor(out=GA[i % 2][:], in0=Ag[i % 2][:], in1=Gall[:, :, i], op=ALU.mult),
                     reads=[Ag[i % 2].name, Gall.name], writes=[GA[i % 2].name])

            def v_mm(i):
                s, ii = (i // UB) % NUB, i % UB
                for tt in range(2):
                    for hf in range(2):
                        P.op("pe", lambda e, s=s, ii=ii, tt=tt, hf=hf, i=i: e.matmul(pbig[tt][:, hf * 512:(hf + 1) * 512], lhsT=GA[i % 2][:, tt * 128:(tt + 1) * 128],
                                                                                   rhs=vbuf[s][:, ii, hf * 512:(hf + 1) * 512], start=(i == 0), stop=(i == 127)),
                             reads=[GA[i % 2].name, vbuf[s].name], writes=[pbig[tt].name], track=(tt == 1 and hf == 1))

            for b_ in range(NUB):
                load_uv(b_)
            a_mm(0)
            for i in range(128):
                if i + 1 < 128:
                    a_mm(i + 1)
                v_mm(i)
                if i % UB == UB - 1 and i // UB + NUB < 128 // UB:
                    load_uv(i // UB + NUB)
                if i == 96:
                    for tt in range(2):
                        P.dma(xe[tt][:], x_in[g * TG + tt * 128: g * TG + (tt + 1) * 128, :], reads=["%s.%d" % (xin_key, (g * TG + tt * 128) // 128)], writes=[xe[tt].name])
                if gen is not None and i >= 2:
                    next(gen, None)
                    next(gen, None)
            if gen is not None:
                for _ in gen:
                    pass

        def epilogue(g):
            t0 = g * TG
            for tt in range(2):
                xb = xe[tt]
                P.op("dve", lambda e, tt=tt: e.tensor_tensor(out=ytmp[:], in0=pbig[tt][:], in1=modr[:, 2, :], op=ALU.mult), reads=[pbig[tt].name, modr.name + ".2"], writes=[ytmp.name])
                P.op("dve", lambda e, xb=xb: e.scalar_tensor_tensor(out=ybuf[:], in0=xb[:], scalar=ALPHA, in1=ytmp[:], op0=ALU.mult, op1=ALU.add),
                     reads=[xb.name, ytmp.name], writes=[ybuf.name])
                layernorm_rows(P, nc, ybuf, None, ytmp, st, lnw, lnb, "pe", yout)
                P.dma(x_out[t0 + tt * 128: t0 + (tt + 1) * 128, :], yout[:], reads=[yout.name], writes=["%s.%d" % (xout_key, (t0 + tt * 128) // 128)])

        for _ in front(0):
            pass
        for g in range(NG):
            gbuild(g)
            mainloop(g, front(g + 1) if g + 1 < NG else None)
            epilogue(g)
    C.P.barrier()


def emit_peer_prep(C, ure_d, v_d, ubf_d, vbf_d, pfx=""):
    P = C.P
    for b in range(16):
        P.dma(ubf_d[b * 8:(b + 1) * 8].rearrange("i p c e -> (i p) (c e)"), ure_d[b * 8:(b + 1) * 8].rearrange("i p c e -> (i p) (c e)"),
              reads=["ure"], writes=[pfx + "ubf%d" % b], q="pool")
        P.dma(vbf_d[b * 1024:(b + 1) * 1024, :], v_d[b * 1024:(b + 1) * 1024, :], reads=["vsrc"], writes=[pfx + "vbf%d" % b], q="pool")


def emit_mod(C, ccT_d, wmod_d, bmod_d, modd):
    nc, P = C.nc, C.P
    with ExitStack() as es:
        def sb(name, shape, dt):
            return es.enter_context(nc.sbuf_tensor(C.name(name), shape, dt))
        cc = sb("cc", [128, 8, 2], F32)
        P.dma(cc[:], ccT_d, writes=[cc.name])
        P.op("act", lambda e: e.activation(out=cc[:], in_=cc[:], func=AF.Silu), reads=[cc.name], writes=[cc.name])
        wb = [sb("wmod%d" % i, [128, 8, 512], F32) for i in range(2)]
        bm = sb("bm", [2, 6 * D], F32)
        P.dma(bm[:], bmod_d.partition_broadcast(2), writes=[bm.name])
        mo = sb("mo", [2, 6 * D], F32)
        pm = [es.enter_context(nc.psum_tensor(C.name("pmod%d" % i), [128, 512], F32)) for i in range(2)]
        for n in range(12):
            w = wb[n % 2]
            P.dma(w[:], wmod_d[:, n * 512:(n + 1) * 512].rearrange("(c p) n -> p c n", p=128), writes=[w.name])
            pp = pm[n % 2]
            for dc in range(8):
                P.op("pe", lambda e, dc=dc, w=w, pp=pp: e.matmul(pp[0:2, :], lhsT=cc[:, dc, :], rhs=w[:, dc, :], start=(dc == 0), stop=(dc == 7)),
                     reads=[cc.name, w.name], writes=[pp.name], track=(dc == 7))
            P.op("dve", lambda e, n=n, pp=pp: e.tensor_tensor(out=mo[:, n * 512:(n + 1) * 512], in0=pp[0:2, :], in1=bm[:, n * 512:(n + 1) * 512], op=ALU.add),
                 reads=[pp.name, bm.name], writes=[mo.name])
        for k in (1, 4):
            P.op("dve", lambda e, k=k: e.tensor_single_scalar(out=mo[:, k * D:(k + 1) * D], in_=mo[:, k * D:(k + 1) * D], scalar=1.0, op=ALU.add),
                 reads=[mo.name], writes=[mo.name])
        P.dma(modd.rearrange("r k d -> r (k d)"), mo[:], reads=[mo.name], writes=["modd"])
    C.P.barrier()


def emit_scan(C, NT, n_ctx, QKT, KT, VA, ET, HO, DV, aug, masks_d, key):
    nc, P = C.nc, C.P
    DVO = DV - 1 if aug else DV
    with ExitStack() as es:
        def sb(name, shape, dt):
            return es.enter_context(nc.sbuf_tensor(C.name(name), shape, dt))

        def ps(name, shape, dt=F32):
            return es.enter_context(nc.psum_tensor(C.name(name), shape, dt))
        mask = sb("mask", [128, 2, 128], F32)
        P.dma(mask[:, 0, :], masks_d[0], writes=[mask.name + ".0"])
        P.dma(mask[:, 1, :], masks_d[1], writes=[mask.name + ".1"])
        St = [sb("St%d" % d, [128, 4, DV], F32) for d in range(2)]
        Sb = [sb("Sb%d" % d, [128, 4, DV], BF16) for d in range(2)]
        for d in range(2):
            P.op("dve", lambda e, d=d: e.memset(St[d][:], 0.0), writes=[St[d].name])
            P.op("pool", lambda e, d=d: e.memset(Sb[d][:], 0.0), writes=[Sb[d].name])
        qkt = [sb("qkt%d" % i, [128, 2, 4, 128], BF16) for i in range(2)]
        kt = [sb("kt%d" % i, [128, 4, 128], BF16) for i in range(2)]
        va = [sb("va%d" % i, [128, 4, DV], BF16) for i in range(2)]
        et = [sb("et%d" % i, [128, 4], F32) for i in range(2)]
        WT = [sb("WT%d" % i, [128, 4, 128], BF16) for i in range(2)]
        tmp = sb("tmp", [128, 4, DV], F32)
        ho = [sb("ho%d" % i, [128, 4, DVO], F32) for i in range(2)]
        dn = sb("dn", [128, 4, 2], F32)
        pS = [ps("pS%d" % d, [128, 512]) for d in range(2)]
        NB = 2 if DV <= 256 else 4
        pN = ps("pN", [128, 2, 512])
        pD = ps("pD", [128, 2, 512])
        lat = list(range(n_ctx, NT))
        order = [list(range(n_ctx)) + lat, list(range(n_ctx))[::-1] + lat[::-1]]
        for n in range(NT):
            for d in range(2):
                tile = order[d][n]
                b = d
                tk = "%s.%d" % (key, tile)
                qv = QKT[tile].rearrange("p (w dd h) t -> p w dd h t", w=2, dd=2)
                P.dma(qkt[b][:], qv[:, :, d, :, :], reads=[tk], writes=[qkt[b].name])
                P.dma(kt[b][:], KT[tile][:, d * 4:(d + 1) * 4, :], reads=[tk], writes=[kt[b].name])
                P.dma(va[b][:], VA[tile], reads=[tk], writes=[va[b].name])
                P.dma(et[b][:], ET[tile][:, d * 4:(d + 1) * 4], reads=[tk], writes=[et[b].name])
                for h in range(4):
                    P.op("pe", lambda e, h=h, b=b, d=d: e.matmul(pS[d][:, h * 128:(h + 1) * 128], lhsT=qkt[b][:, 1, h, :], rhs=qkt[b][:, 0, h, :], start=True, stop=True),
                         reads=[qkt[b].name], writes=[pS[d].name], track=(h == 3))
                P.op("dve", lambda e, b=b, d=d: e.tensor_tensor(out=WT[b][:], in0=pS[d][:].rearrange("p (h t) -> p h t", h=4),
                                                               in1=mask[:, d, :].unsqueeze(1).to_broadcast([128, 4, 128]), op=ALU.mult),
                     reads=[pS[d].name, mask.name + ".%d" % d], writes=[WT[b].name])
                for h in range(4):
                    o = pN[:, h // 2, (h % 2) * 256:(h % 2) * 256 + DV]
                    P.op("pe", lambda e, h=h, b=b, d=d, o=o: e.matmul(o, lhsT=qkt[b][:, 0, h, :], rhs=Sb[d][:, h, :], start=True, stop=False),
                         reads=[qkt[b].name, Sb[d].name], writes=["pN"], track=False)
                    P.op("pe", lambda e, h=h, b=b, o=o: e.matmul(o, lhsT=WT[b][:, h, :], rhs=va[b][:, h, :], start=False, stop=True),
                         reads=[WT[b].name, va[b].name], writes=["pN"], track=(h == 3))
                pNv = pN[:].rearrange("p a (c x) -> p (a c) x", c=2)
                if aug:
                    P.op("act", lambda e: e.activation(out=dn[:, :, 0:1], in_=pNv[:, :, DVO:DVO + 1], func=AF.Abs), reads=["pN"], writes=[dn.name])
                    P.op("dve", lambda e: e.tensor_single_scalar(out=dn[:, :, 0:1], in_=dn[:, :, 0:1], scalar=1.0, op=ALU.max), reads=[dn.name], writes=[dn.name])
                    P.op("dve", lambda e: e.reciprocal(out=dn[:, :, 1:2], in_=dn[:, :, 0:1]), reads=[dn.name], writes=[dn.name])
                    P.op("dve", lambda e, b=b: e.tensor_tensor(out=ho[b][:], in0=pNv[:, :, 0:DVO], in1=dn[:, :, 1:2].to_broadcast([128, 4, DVO]), op=ALU.mult),
                         reads=["pN", dn.name], writes=[ho[b].name])
                else:
                    P.op("act", lambda e, b=b: e.copy(out=ho[b][:], in_=pNv[:, :, 0:DVO]), reads=["pN"], writes=[ho[b].name])
                P.dma(HO[d][tile], ho[b][:], reads=[ho[b].name], writes=["%s.ho%d.%d" % (key, d, tile)], q="pool")
                for h in range(4):
                    o = pD[:, h // 2, (h % 2) * 256:(h % 2) * 256 + DV]
                    P.op("pe", lambda e, h=h, b=b, o=o: e.matmul(o, lhsT=kt[b][:, h, :], rhs=va[b][:, h, :], start=True, stop=True),
                         reads=[kt[b].name, va[b].name], writes=["pD"], track=(h == 3))
                pDv = pD[:].rearrange("p a (c x) -> p (a c) x", c=2)
                P.op("dve", lambda e, d=d: e.tensor_tensor(out=tmp[:], in0=pDv[:, :, 0:DV], in1=St[d][:], op=ALU.add), reads=["pD", St[d].name], writes=[tmp.name])
                P.op("dve", lambda e, d=d, b=b: e.tensor_tensor(out=St[d][:], in0=tmp[:], in1=et[b][:].unsqueeze(2).to_broadcast([128, 4, DV]), op=ALU.mult),
                     reads=[tmp.name, et[b].name], writes=[St[d].name])
                P.op("act", lambda e, d=d: e.copy(out=Sb[d][:], in_=St[d][:]), reads=[St[d].name], writes=[Sb[d].name])
    C.P.barrier()


def load_w_bf16(C, sbf, w_d, ncols, key):
    for dc in range(8):
        C.P.dma(sbf[:, dc, :], w_d[dc * 128:(dc + 1) * 128, :], writes=["%s.%d" % (key, dc)], q="pool")


def emit_in_proj(C, es, tile_src, mod_rows, wbf, wkey, ncols, Pj, xt, hl, hT, pbanks, modr):
    nc, P = C.nc, C.P
    src_ap, src_key = tile_src
    P.dma(xt[:], src_ap, reads=[src_key], writes=[xt.name])
    P.op("dve", lambda e: e.tensor_tensor(out=hl[:], in0=xt[:], in1=modr[:, mod_rows[0], :], op=ALU.mult), reads=[xt.name, modr.name + ".%d" % mod_rows[0]], writes=[hl.name])
    P.op("dve", lambda e: e.tensor_tensor(out=hl[:], in0=hl[:], in1=modr[:, mod_rows[1], :], op=ALU.add), reads=[hl.name, modr.name + ".%d" % mod_rows[1]], writes=[hl.name])
    yield
    for dc in range(8):
        pb = pbanks[dc // 4]
        P.op("pe", lambda e, dc=dc, pb=pb: e.transpose(out=pb[:, (dc % 4) * 128:(dc % 4 + 1) * 128], in_=hl[:, dc * 128:(dc + 1) * 128], identity=C.ident[:]),
             reads=[hl.name, "c_ident"], writes=[pb.name], track=(dc % 4 == 3))
    yield
    for hb in range(2):
        P.op("act", lambda e, hb=hb: e.copy(out=hT[:, hb * 4:(hb + 1) * 4, :], in_=pbanks[hb][:].rearrange("p (c t) -> p c t", c=4)),
             reads=[pbanks[hb].name], writes=[hT.name])
    yield
    nch = (ncols + 511) // 512
    for n in range(nch):
        c0, c1 = n * 512, min(ncols, (n + 1) * 512)
        pb = pbanks[2 + n % (len(pbanks) - 2)]
        for dc in range(8):
            P.op("pe", lambda e, dc=dc, pb=pb, c0=c0, c1=c1: e.matmul(pb[:, 0:c1 - c0], lhsT=hT[:, dc, :], rhs=wbf[:, dc, c0:c1], start=(dc == 0), stop=(dc == 7)),
                 reads=[hT.name, "%s.%d" % (wkey, dc)], writes=[pb.name], track=(dc == 7))
        P.op("act", lambda e, pb=pb, c0=c0, c1=c1: e.copy(out=Pj[:, c0:c1], in_=pb[:, 0:c1 - c0]), reads=[pb.name], writes=[Pj.name])
        if n % 2 == 1:
            yield


def emit_decay_prep(C, sbs, LFv, LIv, lkey, ncol, cmats, pcum, with_li):
    nc, P = C.nc, C.P
    triu, tril, ones = cmats
    CUM, TOT = sbs
    for d in range(2):
        P.op("pe", lambda e, d=d: e.matmul(pcum[:, d * ncol:(d + 1) * ncol], lhsT=(triu if d == 0 else tril)[:], rhs=LFv[:, d * ncol:(d + 1) * ncol], start=True, stop=True),
             reads=[lkey, "c_tri"], writes=[pcum.name])
    P.op("act", lambda e: e.copy(out=CUM[:], in_=pcum[:, 0:2 * ncol]), reads=[pcum.name], writes=[CUM.name])
    P.op("pe", lambda e: e.matmul(pcum[:, 0:2 * ncol], lhsT=ones[:], rhs=LFv[:, 0:2 * ncol], start=True, stop=True), reads=[lkey, "c_tri"], writes=[pcum.name])
    P.op("act", lambda e: e.activation(out=TOT[:], in_=pcum[:, 0:2 * ncol], func=AF.Exp), reads=[pcum.name], writes=[TOT.name])


def emit_ab_stage_a(C, NL, x_d, xkey, ctx_d, ckey, modd, w_in_d, gate_b_d, rope_d, S):
    nc, P = C.nc, C.P
    NT = 2 + NL
    with ExitStack() as es:
        def sb(name, shape, dt):
            return es.enter_context(nc.sbuf_tensor(C.name(name), shape, dt))

        def ps(name, shape, dt=F32):
            return es.enter_context(nc.psum_tensor(C.name(name), shape, dt))
        NC = 2832
        wbf = sb("w_in", [128, 8, NC], BF16)
        load_w_bf16(C, wbf, w_in_d, NC, wbf.name)
        modr = sb("modr", [128, 4, D], F32)
        for r, (row, k) in enumerate(((0, 1), (0, 0), (1, 1), (1, 0))):
            P.dma(modr[:, r, :], modd[row, k].partition_broadcast(128), reads=["modd"], writes=[modr.name + ".%d" % r])
        gb = sb("gb", [128, 16], F32)
        P.dma(gb[:], gate_b_d.partition_broadcast(128), writes=[gb.name])
        tri = sb("tri", [128, 3, 128], F32)
        P.dma(tri[:], C.consts_d[2:5].rearrange("k p n -> p k n"), writes=["c_tri"])
        cm = (tri[:, 0, :], tri[:, 1, :], tri[:, 2, :])
        cmats = (V(cm[0], "c_tri"), V(cm[1], "c_tri"), V(cm[2], "c_tri"))
        pb = [ps("pb%d" % i, [128, 512]) for i in range(7)]
        pcum = ps("pcum", [128, 512])
        xt_ = [sb("xt%d" % i_, [128, D], F32) for i_ in range(2)]
        hl_ = [sb("hl%d" % i_, [128, D], F32) for i_ in range(2)]
        hT_ = [sb("hT%d" % i_, [128, 8, 128], BF16) for i_ in range(2)]
        Pj_ = [sb("Pj%d" % i_, [128, NC], F32) for i_ in range(2)]
        G16_ = [sb("G16%d" % i_, [128, 16], F32) for i_ in range(2)]
        LF_ = [sb("LF%d" % i_, [128, 8], F32) for i_ in range(2)]
        LI_ = [sb("LI%d" % i_, [128, 8], F32) for i_ in range(2)]
        CUM_ = [sb("CUM%d" % i_, [128, 8], F32) for i_ in range(2)]
        TOT_ = [sb("TOT%d" % i_, [128, 8], F32) for i_ in range(2)]
        EB_ = [sb("EB%d" % i_, [128, 8], F32) for i_ in range(2)]
        EA_ = [sb("EA%d" % i_, [128, 8], F32) for i_ in range(2)]
        qk_ = [sb("qk%d" % i_, [128, 2, 2, 4, 128], F32) for i_ in range(2)]
        qkT_ = [sb("qkT%d" % i_, [128, 16, 128], BF16) for i_ in range(2)]
        ktb_ = [sb("ktb%d" % i_, [128, 8, 128], BF16) for i_ in range(2)]
        vab_ = [sb("vab%d" % i_, [128, 4, 129], BF16) for i_ in range(2)]
        for vab in vab_:
            P.op("dve", lambda e, vab=vab: e.memset(vab[:], 1.0), writes=[vab.name])
        rope_ = [sb("rope%d" % i_, [128, 2, 32], F32) for i_ in range(2)]
        qa_ = [sb("qa%d" % i_, [128, 10, 64], F32) for i_ in range(2)]
        rt_ = [sb("rt%d" % i_, [128, 4, 10, 32], F32) for i_ in range(2)]
        qaT_ = [sb("qaT%d" % i_, [128, 5, 128], BF16) for i_ in range(2)]
        vaa_ = [sb("vaa%d" % i_, [128, 2, 65], BF16) for i_ in range(2)]
        for vaa in vaa_:
            P.op("dve", lambda e, vaa=vaa: e.memset(vaa[:], 1.0), writes=[vaa.name])
        def tile_gen(tile):
            xt = xt_[tile % 2]
            hl = hl_[tile % 2]
            hT = hT_[tile % 2]
            Pj = Pj_[tile % 2]
            G16 = G16_[tile % 2]
            LF = LF_[tile % 2]
            LI = LI_[tile % 2]
            CUM = CUM_[tile % 2]
            TOT = TOT_[tile % 2]
            EB = EB_[tile % 2]
            EA = EA_[tile % 2]
            qk = qk_[tile % 2]
            qkT = qkT_[tile % 2]
            ktb = ktb_[tile % 2]
            vab = vab_[tile % 2]
            rope = rope_[tile % 2]
            qa = qa_[tile % 2]
            rt = rt_[tile % 2]
            qaT = qaT_[tile % 2]
            vaa = vaa_[tile % 2]
            is_ctx = tile < 2
            if is_ctx:
                src = (ctx_d[tile * 128:(tile + 1) * 128, :], "%s.%d" % (ckey, tile))
            else:
                src = (x_d[(tile - 2) * 128:(tile - 1) * 128, :], "%s.%d" % (xkey, tile - 2))
            yield from emit_in_proj(C, es, src, (2, 3) if is_ctx else (0, 1), wbf, wbf.name, NC, Pj, xt, hl, hT, pb, modr)
            tk = "ab.%d" % tile
            P.op("dve", lambda e: e.tensor_tensor(out=G16[:], in0=Pj[:, 2048:2064], in1=gb[:], op=ALU.add), reads=[Pj.name, gb.name], writes=[G16.name])
            gv = G16[:].rearrange("p (d g h) -> p d g h", d=2, g=2)
            P.op("dve", lambda e: e.tensor_copy(out=LI[:].rearrange("p (d h) -> p d h", d=2), in_=gv[:, :, 0, :]), reads=[G16.name], writes=[LI.name])
            P.op("act", lambda e: e.activation(out=LF[:].rearrange("p (d h) -> p d h", d=2), in_=gv[:, :, 1, :], func=AF.Exp, scale=-1.0), reads=[G16.name], writes=[LF.name])
            P.op("dve", lambda e: e.tensor_single_scalar(out=LF[:], in_=LF[:], scalar=1.0, op=ALU.add), reads=[LF.name], writes=[LF.name])
            P.op("act", lambda e: e.activation(out=LF[:], in_=LF[:], func=AF.Ln), reads=[LF.name], writes=[LF.name])
            P.op("dve", lambda e: e.tensor_single_scalar(out=LF[:], in_=LF[:], scalar=-1.0, op=ALU.mult), reads=[LF.name], writes=[LF.name])
            emit_decay_prep(C, (CUM, TOT), LF[:], LI[:], LF.name, 4, cmats, pcum, True)
            yield
            P.dma(S["ET"][tile], TOT[:], reads=[TOT.name], writes=[tk + ".et"], q="pool")
            yield
            P.op("act", lambda e: e.activation(out=EB[:], in_=CUM[:], func=AF.Exp), reads=[CUM.name], writes=[EB.name])
            P.op("dve", lambda e: e.tensor_single_scalar(out=EB[:], in_=EB[:], scalar=128.0 ** -0.5, op=ALU.mult), reads=[EB.name], writes=[EB.name])
            P.op("dve", lambda e: e.tensor_tensor(out=EA[:], in0=LI[:], in1=CUM[:], op=ALU.subtract), reads=[LI.name, CUM.name], writes=[EA.name])
            P.op("act", lambda e: e.activation(out=EA[:], in_=EA[:], func=AF.Exp), reads=[EA.name], writes=[EA.name])
            for w, (E_, c0) in enumerate(((EB, 0), (EA, 512))):
                for d in range(2):
                    P.op("dve", lambda e, w=w, d=d, E_=E_, c0=c0: e.tensor_tensor(
                        out=qk[:, w, d], in0=Pj[:, c0:c0 + 512].rearrange("p (h k) -> p h k", h=4),
                        in1=E_[:, d * 4:(d + 1) * 4].unsqueeze(2).to_broadcast([128, 4, 128]), op=ALU.mult),
                        reads=[Pj.name, E_.name], writes=[qk.name + ".%d%d" % (w, d)])
            for s in range(16):
                w, d, h = s // 8, (s // 4) % 2, s % 4
                pp = pb[s // 4]
                P.op("pe", lambda e, s=s, w=w, d=d, h=h, pp=pp: e.transpose(out=pp[:, (s % 4) * 128:(s % 4 + 1) * 128], in_=qk[:, w, d, h, :], identity=C.ident[:]),
                     reads=[qk.name + ".%d%d" % (w, d), "c_ident"], writes=[pp.name], track=(s % 4 == 3))
            for g in range(4):
                P.op("act" if g % 2 == 0 else "dve", lambda e, g=g: (e.copy if g % 2 == 0 else e.tensor_copy)(out=qkT[:, g * 4:(g + 1) * 4, :], in_=pb[g][:].rearrange("p (c t) -> p c t", c=4)),
                     reads=[pb[g].name], writes=[qkT.name + ".%d" % g])
            P.dma(S["QKT"][tile], qkT[:], reads=[qkT.name + ".%d" % g for g in range(4)], writes=[tk + ".qkt"], q="pool")
            yield
            P.op("act", lambda e: e.copy(out=ktb[:], in_=qk[:, 1].rearrange("p d h k -> p (d h) k")), reads=[qk.name + ".10", qk.name + ".11"], writes=[ktb.name])
            P.dma(S["KT"][tile], ktb[:], reads=[ktb.name], writes=[tk + ".kt"], q="pool")
            yield
            P.op("act", lambda e: e.copy(out=vab[:, :, 0:128], in_=Pj[:, 1024:1536].rearrange("p (h k) -> p h k", h=4)), reads=[Pj.name], writes=[vab.name])
            P.dma(S["VA"][tile], vab[:], reads=[vab.name], writes=[tk + ".va"], q="pool")
            yield
            P.dma(S["OM"][tile], Pj[:, 1536:2048], reads=[Pj.name], writes=[tk + ".om"], q="pool")
            yield
            qsrc = Pj[:, 2064:2704].rearrange("p (h k) -> p h k", h=10)
            qperm = qa[:, 0:8, :].rearrange("p (j a) k -> p a j k", a=2)
            if is_ctx:
                P.op("dve", lambda e: e.tensor_copy(out=qperm, in_=qsrc[:, 0:8, :].rearrange("p (a j) k -> p a j k", a=2)), reads=[Pj.name], writes=[qa.name])
                P.op("dve", lambda e: e.tensor_copy(out=qa[:, 8:10, :], in_=qsrc[:, 8:10, :]), reads=[Pj.name], writes=[qa.name])
            else:
                P.dma(rope[:], rope_d[tile - 2], writes=[rope.name])
                x1 = qsrc.rearrange("p h (i two) -> p h i two", two=2)[:, :, :, 0]
                x2 = qsrc.rearrange("p h (i two) -> p h i two", two=2)[:, :, :, 1]
                cs = rope[:, 0, :].unsqueeze(1).to_broadcast([128, 10, 32])
                sn = rope[:, 1, :].unsqueeze(1).to_broadcast([128, 10, 32])
                qo = qa[:].rearrange("p h (i two) -> p h i two", two=2)
                for j, (xa, tb_) in enumerate(((x1, cs), (x2, sn), (x1, sn), (x2, cs))):
                    P.op("dve", lambda e, j=j, xa=xa, tb_=tb_: e.tensor_tensor(out=rt[:, j], in0=xa, in1=tb_, op=ALU.mult),
                         reads=[Pj.name, rope.name], writes=[rt.name + ".%d" % j])
                qpo = qperm.rearrange("p a j (i two) -> p a j i two", two=2)
                for two, (ra, rb, op_) in enumerate(((0, 1, ALU.subtract), (2, 3, ALU.add))):
                    P.op("dve", lambda e, two=two, ra=ra, rb=rb, op_=op_: e.tensor_tensor(out=qpo[:, :, :, :, two], in0=rt[:, ra, 0:8].rearrange("p (a j) i -> p a j i", a=2),
                                                                                  in1=rt[:, rb, 0:8].rearrange("p (a j) i -> p a j i", a=2), op=op_),
                         reads=[rt.name + ".%d" % ra, rt.name + ".%d" % rb], writes=[qa.name])
                    P.op("dve", lambda e, two=two, ra=ra, rb=rb, op_=op_: e.tensor_tensor(out=qo[:, 8:10, :, two], in0=rt[:, ra, 8:10], in1=rt[:, rb, 8:10], op=op_),
                         reads=[rt.name + ".%d" % ra, rt.name + ".%d" % rb], writes=[qa.name])
            pp = pb[4]
            pq = pb[5]
            for j in range(4):
                P.op("pe", lambda e, j=j, pp=pp: e.transpose(out=pp[:, j * 128:(j + 1) * 128], in_=qa[:].rearrange("p h k -> p (h k)")[:, j * 128:(j + 1) * 128], identity=C.ident[:]),
                     reads=[qa.name, "c_ident"], writes=[pp.name], track=(j == 3))
            P.op("pe", lambda e, pq=pq: e.transpose(out=pq[:, 0:128], in_=qa[:].rearrange("p h k -> p (h k)")[:, 512:640], identity=C.ident[:]), reads=[qa.name, "c_ident"], writes=[pq.name])
            P.op("act", lambda e, pp=pp: e.copy(out=qaT[:, 0:4, :], in_=pp[:].rearrange("p (c t) -> p c t", c=4)), reads=[pp.name], writes=[qaT.name])
            P.op("act", lambda e, pq=pq: e.copy(out=qaT[:, 4, :], in_=pq[:, 0:128]), reads=[pq.name], writes=[qaT.name])
            P.dma(S["QAT"][tile], qaT[:], reads=[qaT.name], writes=[tk + ".qat"], q="pool")
            yield
            P.op("act", lambda e: e.copy(out=vaa[:, :, 0:64], in_=Pj[:, 2704:2832].rearrange("p (h k) -> p h k", h=2)), reads=[Pj.name], writes=[vaa.name])
            P.dma(S["VAA"][tile], vaa[:], reads=[vaa.name], writes=[tk + ".vaa"], q="pool")
            yield

        _pending = list(range(0, NT))
        _active = []
        while _pending or _active:
            if len(_active) < 2 and _pending:
                _active.append(tile_gen(_pending.pop(0)))
            for _g in list(_active):
                try:
                    next(_g)
                except StopIteration:
                    _active.remove(_g)
    C.P.barrier()


def emit_ab_attn(C, NL, S, sink_d, AO, aokey):
    nc, P = C.nc, C.P
    NT = 2 + NL
    with ExitStack() as es:
        def sb(name, shape, dt):
            return es.enter_context(nc.sbuf_tensor(C.name(name), shape, dt))

        def ps(name, shape, dt=F32):
            return es.enter_context(nc.psum_tensor(C.name(name), shape, dt))
        kT = sb("kT_all", [128, NT, 128], BF16)
        va = sb("va_all", [128, NT, 2, 65], BF16)
        for t in range(NT):
            P.dma(kT[:, t, :], S["QAT"][t][:, 4, :], reads=["ab.%d.qat" % t], writes=[kT.name + ".%d" % t])
            P.dma(va[:, t], S["VAA"][t], reads=["ab.%d.vaa" % t], writes=[va.name + ".%d" % t])
        mk = sb("mk", [128, 2, 128], F32)
        P.dma(mk[:], C.consts_d[2:4].rearrange("k p n -> p k n"), writes=[mk.name])
        mkb = sb("mkb", [128, 2, 128], BF16)
        P.op("dve", lambda e: e.tensor_copy(out=mkb[:], in_=mk[:]), reads=[mk.name], writes=[mkb.name])
        sk = sb("sink", [128, 8], F32)
        P.dma(sk[:], sink_d.partition_broadcast(128), writes=[sk.name])
        P.op("act", lambda e: e.activation(out=sk[:], in_=sk[:], func=AF.Exp), reads=[sk.name], writes=[sk.name])
        qt = [sb("qt%d" % i, [128, 4, 128], BF16) for i in range(2)]
        E = [sb("E%d" % i, [128, 5, 128], BF16) for i in range(2)]
        pE = [ps("pE%d" % i, [128, 1024]) for i in range(2)]
        pO = ps("pO", [128, 2, 512])
        den = sb("den", [128, 8, 2], F32)
        ao = [sb("ao%d" % i, [128, 8, 64], F32) for i in range(2)]
        for qtile in range(NT):
            if qtile < 2:
                blocks = [(0, None), (1, None)]
            else:
                n = qtile - 2
                blocks = [(0, None), (1, None)]
                if n >= 1:
                    blocks.append((qtile - 1, 1))
                blocks.append((qtile, None))
                if n + 1 < NL:
                    blocks.append((qtile + 1, 0))
            nb = len(blocks)
            q = qt[qtile % 2]
            P.dma(q[:], S["QAT"][qtile][:, 0:4, :], reads=["ab.%d.qat" % qtile], writes=[q.name])
            for hq in range(8):
                j, half = hq % 4, hq // 4
                p0, p1 = half * 64, (half + 1) * 64
                pe_ = pE[hq % 2]
                Eb = E[hq % 2]
                for bi, (blk, _) in enumerate(blocks):
                    P.op("pe", lambda e, bi=bi, blk=blk, p0=p0, p1=p1, j=j, pe_=pe_, q=q: e.matmul(pe_[:, bi * 128:(bi + 1) * 128], lhsT=kT[p0:p1, blk, :], rhs=q[p0:p1, j, :], start=True, stop=True),
                         reads=[kT.name + ".%d" % blk, q.name], writes=[pe_.name], track=(bi == nb - 1))
                P.op("act", lambda e, pe_=pe_, Eb=Eb, nb=nb: e.activation(out=Eb[:, 0:nb, :], in_=pe_[:, 0:nb * 128].rearrange("p (b t) -> p b t", b=nb), func=AF.Exp, scale=0.125),
                     reads=[pe_.name], writes=[Eb.name])
                for bi, (blk, m) in enumerate(blocks):
                    if m is not None:
                        P.op("dve", lambda e, bi=bi, m=m, Eb=Eb: e.tensor_tensor(out=Eb[:, bi, :], in0=Eb[:, bi, :], in1=mkb[:, m, :], op=ALU.mult),
                             reads=[Eb.name, mkb.name], writes=[Eb.name])
                o = pO[:, hq // 4, (hq % 4) * 65:(hq % 4) * 65 + 65]
                for bi, (blk, _) in enumerate(blocks):
                    P.op("pe", lambda e, bi=bi, blk=blk, half=half, Eb=Eb, o=o: e.matmul(o, lhsT=Eb[:, bi, :], rhs=va[:, blk, half, :], start=(bi == 0), stop=(bi == nb - 1)),
                         reads=[Eb.name, va.name + ".%d" % blk], writes=["pO"], track=(bi == nb - 1))
            pv = pO[:, :, 0:260].rearrange("p a (h x) -> p a h x", h=4)
            a_ = ao[qtile % 2]
            dv = den[:].rearrange("p (a h) x -> p a h x", a=2)
            P.op("dve", lambda e: e.tensor_tensor(out=dv[:, :, :, 0:1], in0=pv[:, :, :, 64:65], in1=sk[:].rearrange("p (a h) -> p a h", a=2).unsqueeze(3), op=ALU.add),
                 reads=["pO", sk.name], writes=[den.name])
            P.op("dve", lambda e: e.reciprocal(out=den[:, :, 1:2], in_=den[:, :, 0:1]), reads=[den.name], writes=[den.name])
            P.op("dve", lambda e, a_=a_: e.tensor_tensor(out=a_[:].rearrange("p (a h) x -> p a h x", a=2), in0=pv[:, :, :, 0:64],
                                                         in1=dv[:, :, :, 1:2].to_broadcast([128, 2, 4, 64]), op=ALU.mult),
                 reads=["pO", den.name], writes=[a_.name])
            P.dma(AO[qtile], a_[:].rearrange("p h x -> p (h x)"), reads=[a_.name], writes=["%s.%d" % (aokey, qtile)], q="pool")
    C.P.barrier()


def head_norm_rows(C, hm, sq, stt, nheads, dh, hk):
    P = C.P
    P.op("dve", lambda e: e.tensor_tensor(out=sq[:], in0=hm[:], in1=hm[:], op=ALU.mult), reads=[hk], writes=[sq.name])
    P.op("dve", lambda e: e.reduce_sum(out=stt[:, 0:nheads], in_=sq[:], axis=AX.X), reads=[sq.name], writes=[stt.name])
    P.op("dve", lambda e: e.tensor_scalar(out=stt[:, 0:nheads], in0=stt[:, 0:nheads], scalar1=1.0 / dh, scalar2=LN_EPS, op0=ALU.mult, op1=ALU.add), reads=[stt.name], writes=[stt.name])
    P.op("act", lambda e: e.activation(out=stt[:, 0:nheads], in_=stt[:, 0:nheads], func=AF.Sqrt), reads=[stt.name], writes=[stt.name])
    P.op("dve", lambda e: e.reciprocal(out=stt[:, 0:nheads], in_=stt[:, 0:nheads]), reads=[stt.name], writes=[stt.name])
    P.op("dve", lambda e: e.tensor_tensor(out=hm[:], in0=hm[:], in1=stt[:, 0:nheads].unsqueeze(2).to_broadcast([128, nheads, dh]), op=ALU.mult), reads=[hk, stt.name], writes=[hk])


def emit_merge(C, NL, n_ctx_out, kind, S, HO, hokey, AO, aokey, x_d, xkey, ctx_d, ckey, modd, norm_w_d, w_out_d, lnw_d, lnb_d, x1_d, x1key, c1_d, c1key):
    nc, P = C.nc, C.P
    NT = 2 + NL
    with ExitStack() as es:
        def sb(name, shape, dt):
            return es.enter_context(nc.sbuf_tensor(C.name(name), shape, dt))

        def ps(name, shape, dt=F32):
            return es.enter_context(nc.psum_tensor(C.name(name), shape, dt))
        wbf = sb("w_out", [128, 8, D], BF16)
        load_w_bf16(C, wbf, w_out_d, D, wbf.name)
        nw = D // 2 if kind == "ab" else D
        normw = sb("normw", [128, nw], F32)
        P.dma(normw[:], norm_w_d.partition_broadcast(128), writes=[normw.name])
        g1 = sb("g1", [128, 2, D], F32)
        for r in range(2):
            P.dma(g1[:, r, :], modd[r, 2].partition_broadcast(128), reads=["modd"], writes=[g1.name + ".%d" % r])
        lnw = sb("lnw", [128, D], F32)
        lnb = sb("lnb", [128, D], F32)
        P.dma(lnw[:], lnw_d.partition_broadcast(128), writes=[lnw.name])
        P.dma(lnb[:], lnb_d.partition_broadcast(128), writes=[lnb.name])
        h0_ = [sb("h0%d" % i_, [128, nw], F32) for i_ in range(2)]
        h1_ = [sb("h1%d" % i_, [128, nw], F32) for i_ in range(2)]
        sq_ = [sb("sq%d" % i_, [128, nw], F32) for i_ in range(2)]
        gt_ = [sb("gt%d" % i_, [128, nw], F32) for i_ in range(2)]
        cat_ = [sb("cat%d" % i_, [128, D], F32) for i_ in range(2)]
        catT_ = [sb("catT%d" % i_, [128, 8, 128], BF16) for i_ in range(2)]
        xt_ = [sb("xt%d" % i_, [128, D], F32) for i_ in range(2)]
        ytmp_ = [sb("ytmp%d" % i_, [128, D], F32) for i_ in range(2)]
        yo_ = [sb("yo%d" % i_, [128, D], F32) for i_ in range(2)]
        stt_ = [sb("stt%d" % i_, [128, 4], F32) for i_ in range(2)]
        st_ = [sb("st%d" % i_, [128, 4], F32) for i_ in range(2)]
        lt_ = [sb("lt%d" % i_, [128, D], F32) for i_ in range(2)]
        pb = [ps("pbm%d" % i, [128, 1024]) for i in range(2)]
        first = 0 if n_ctx_out else 2
        def tile_gen(tile):
            h0 = h0_[tile % 2]
            h1 = h1_[tile % 2]
            sq = sq_[tile % 2]
            gt = gt_[tile % 2]
            cat = cat_[tile % 2]
            catT = catT_[tile % 2]
            xt = xt_[tile % 2]
            ytmp = ytmp_[tile % 2]
            yo = yo_[tile % 2]
            lt = lt_[tile % 2]
            stt = stt_[tile % 2]
            st = st_[tile % 2]
            is_ctx = tile < 2
            P.dma(h0[:], HO[0][tile].rearrange("p h x -> p (h x)"), reads=["%s.ho0.%d" % (hokey, tile)], writes=[h0.name])
            P.dma(h1[:], HO[1][tile].rearrange("p h x -> p (h x)"), reads=["%s.ho1.%d" % (hokey, tile)], writes=[h1.name])
            P.dma(gt[:], S["OM"][tile], reads=["%s.%d.om" % (kind, tile)], writes=[gt.name])
            P.op("dve", lambda e: e.tensor_tensor(out=h0[:], in0=h0[:], in1=h1[:], op=ALU.add), reads=[h0.name, h1.name], writes=[h0.name])
            dh = nw // 4
            hv = V(h0[:].rearrange("p (h x) -> p h x", h=4), h0.name)
            sv = V(sq[:].rearrange("p (h x) -> p h x", h=4), sq.name)
            head_norm_rows(C, hv, sv, stt, 4, dh, h0.name)
            yield
            P.op("dve", lambda e: e.tensor_tensor(out=h0[:], in0=h0[:], in1=normw[:], op=ALU.mult), reads=[h0.name, normw.name], writes=[h0.name])
            P.op("act", lambda e: e.activation(out=gt[:], in_=gt[:], func=(AF.Sigmoid if kind == "ab" else AF.Silu)), reads=[gt.name], writes=[gt.name])
            P.op("dve", lambda e: e.tensor_tensor(out=cat[:, 0:nw], in0=h0[:], in1=gt[:], op=ALU.mult), reads=[h0.name, gt.name], writes=[cat.name + ".0"])
            rk = [cat.name + ".0"]
            if kind == "ab":
                P.dma(cat[:, nw:D], AO[tile], reads=["%s.%d" % (aokey, tile)], writes=[cat.name + ".1"])
                rk.append(cat.name + ".1")
            for dc in range(8):
                P.op("pe", lambda e, dc=dc: e.transpose(out=pb[0][:, dc * 128:(dc + 1) * 128], in_=cat[:, dc * 128:(dc + 1) * 128], identity=C.ident[:]),
                     reads=rk + ["c_ident"], writes=[pb[0].name], track=(dc == 7))
            P.op("act", lambda e: e.copy(out=catT[:], in_=pb[0][:].rearrange("p (c t) -> p c t", c=8)), reads=[pb[0].name], writes=[catT.name])
            for hf in range(2):
                for dc in range(8):
                    P.op("pe", lambda e, dc=dc, hf=hf: e.matmul(pb[1][:, hf * 512:(hf + 1) * 512], lhsT=catT[:, dc, :], rhs=wbf[:, dc, hf * 512:(hf + 1) * 512], start=(dc == 0), stop=(dc == 7)),
                         reads=[catT.name, "%s.%d" % (wbf.name, dc)], writes=[pb[1].name], track=(dc == 7 and hf == 1))
            if is_ctx:
                src, skey, dst, dkey = ctx_d[tile * 128:(tile + 1) * 128, :], "%s.%d" % (ckey, tile), c1_d[tile * 128:(tile + 1) * 128, :], "%s.%d" % (c1key, tile)
            else:
                src, skey, dst, dkey = x_d[(tile - 2) * 128:(tile - 1) * 128, :], "%s.%d" % (xkey, tile - 2), x1_d[(tile - 2) * 128:(tile - 1) * 128, :], "%s.%d" % (x1key, tile - 2)
            r = 1 if is_ctx else 0
            P.dma(xt[:], src, reads=[skey], writes=[xt.name])
            P.op("dve", lambda e, r=r: e.tensor_tensor(out=ytmp[:], in0=pb[1][:], in1=g1[:, r, :], op=ALU.mult), reads=[pb[1].name, g1.name + ".%d" % r], writes=[ytmp.name])
            P.op("dve", lambda e: e.scalar_tensor_tensor(out=ytmp[:], in0=xt[:], scalar=ALPHA, in1=ytmp[:], op0=ALU.mult, op1=ALU.add), reads=[xt.name, ytmp.name], writes=[ytmp.name])
            layernorm_rows(P, nc, ytmp, None, lt, st, lnw, lnb, "m", yo)
            yield
            P.dma(dst, yo[:], reads=[yo.name], writes=[dkey], q="pool")
            yield

        _pending = list(range(first, NT))
        _active = []
        while _pending or _active:
            if len(_active) < 2 and _pending:
                _active.append(tile_gen(_pending.pop(0)))
            for _g in list(_active):
                try:
                    next(_g)
                except StopIteration:
                    _active.remove(_g)
    C.P.barrier()


def make_consts():
    i = np.arange(128)
    return np.stack([np.eye(128), np.tile(i.astype(np.float64), (128, 1)), (i[:, None] <= i[None, :]), (i[:, None] >= i[None, :]), np.ones((128, 128))]).astype(np.float32)


def make_rope(seq):
    rows = seq // 64
    row = np.repeat(np.arange(rows), 64).astype(np.float32)
    col = np.tile(np.arange(64), rows).astype(np.float32)
    inv = (10000.0 ** (-np.arange(16, dtype=np.float32) / 16)).astype(np.float32)
    ang = np.concatenate([row[:, None] * inv, col[:, None] * inv], -1).astype(np.float32)
    r = np.stack([np.cos(ang), np.sin(ang)], 1).astype(np.float32)
    return np.ascontiguousarray(r.reshape(seq // 128, 128, 2, 32))


def ab_scratch(nc, NT, pfx):
    S = {}
    S["QKT"] = nc.dram_tensor(pfx + "QKT", [NT, 128, 16, 128], BF16).ap()
    S["KT"] = nc.dram_tensor(pfx + "KT", [NT, 128, 8, 128], BF16).ap()
    S["VA"] = nc.dram_tensor(pfx + "VA", [NT, 128, 4, 129], BF16).ap()
    S["ET"] = nc.dram_tensor(pfx + "ET", [NT, 128, 8], F32).ap()
    S["OM"] = nc.dram_tensor(pfx + "OM", [NT, 128, 512], F32).ap()
    S["QAT"] = nc.dram_tensor(pfx + "QAT", [NT, 128, 5, 128], BF16).ap()
    S["VAA"] = nc.dram_tensor(pfx + "VAA", [NT, 128, 2, 65], BF16).ap()
    S["HO"] = [nc.dram_tensor(pfx + "HO%d" % d, [NT, 128, 4, 128], F32).ap() for d in range(2)]
    S["AO"] = nc.dram_tensor(pfx + "AO", [NT, 128, 512], F32).ap()
    return S


def emit_layer0_mixer(C, NL, x_d, xkey, ctx_d, ckey, modd, W, x1_d, x1key, c1_d, c1key):
    nc = C.nc
    NT = 2 + NL
    S = ab_scratch(nc, NT, "ab_")
    emit_ab_stage_a(C, NL, x_d, xkey, ctx_d, ckey, modd, W["ab_w_in"], W["ab_gate_b"], W["rope"], S)
    emit_scan(C, NT, 2, S["QKT"], S["KT"], S["VA"], S["ET"], S["HO"], 129, True, C.consts_d[2:4], "abs")
    emit_ab_attn(C, NL, S, W["ab_sink"], S["AO"], "ab.ao")
    emit_merge(C, NL, True, "ab", S, S["HO"], "abs", S["AO"], "ab.ao", x_d, xkey, ctx_d, ckey, modd, W["ab_norm_w"], W["ab_w_out"], W["lnw0"], W["lnb0"], x1_d, x1key, c1_d, c1key)


def emit_c_stage_a(C, NL, x_d, xkey, ctx_d, ckey, modd, w_in_d, gate_up_d, gate_b_d, S):
    nc, P = C.nc, C.P
    NT = 2 + NL
    with ExitStack() as es:
        def sb(name, shape, dt):
            return es.enter_context(nc.sbuf_tensor(C.name(name), shape, dt))

        def ps(name, shape, dt=F32):
            return es.enter_context(nc.psum_tensor(C.name(name), shape, dt))
        NC = 3104
        wbf = sb("w_in", [128, 8, NC], BF16)
        load_w_bf16(C, wbf, w_in_d, NC, wbf.name)
        modr = sb("modr", [128, 4, D], F32)
        for r, (row, k) in enumerate(((0, 1), (0, 0), (1, 1), (1, 0))):
            P.dma(modr[:, r, :], modd[row, k].partition_broadcast(128), reads=["modd"], writes=[modr.name + ".%d" % r])
        gb = sb("gb", [128, 1024], F32)
        P.dma(gb[:], gate_b_d.partition_broadcast(128), writes=[gb.name])
        gup = sb("gup", [16, 2, 512], F32)
        P.dma(gup[:], gate_up_d.rearrange("d r c -> r d c"), writes=[gup.name])
        tri = sb("tri", [128, 3, 128], F32)
        P.dma(tri[:], C.consts_d[2:5].rearrange("k p n -> p k n"), writes=["c_tri"])
        pb = [ps("pb%d" % i, [128, 512]) for i in range(7)]
        pcum = ps("pcum", [128, 512])
        xt_ = [sb("xt%d" % i_, [128, D], F32) for i_ in range(2)]
        hl_ = [sb("hl%d" % i_, [128, D], F32) for i_ in range(2)]
        hT_ = [sb("hT%d" % i_, [128, 8, 128], BF16) for i_ in range(2)]
        Pj_ = [sb("Pj%d" % i_, [128, NC], F32) for i_ in range(2)]
        lowT_ = [sb("lowT%d" % i_, [16, 2, 128], F32) for i_ in range(2)]
        LA_ = [sb("LA%d" % i_, [128, 1024], F32) for i_ in range(2)]
        CUM_ = [sb("CUM%d" % i_, [128, 1024], F32) for i_ in range(2)]
        EB_ = [sb("EB%d" % i_, [128, 1024], F32) for i_ in range(2)]
        EA_ = [sb("EA%d" % i_, [128, 1024], F32) for i_ in range(2)]
        ETt_ = [sb("ETt%d" % i_, [128, 8], F32) for i_ in range(2)]
        qk_ = [sb("qk%d" % i_, [128, 2, 2, 4, 128], F32) for i_ in range(2)]
        qkT_ = [sb("qkT%d" % i_, [128, 16, 128], BF16) for i_ in range(2)]
        ktb_ = [sb("ktb%d" % i_, [128, 8, 128], BF16) for i_ in range(2)]
        vab_ = [sb("vab%d" % i_, [128, 4, 256], BF16) for i_ in range(2)]
        def tile_gen(tile):
            xt = xt_[tile % 2]
            hl = hl_[tile % 2]
            hT = hT_[tile % 2]
            Pj = Pj_[tile % 2]
            lowT = lowT_[tile % 2]
            LA = LA_[tile % 2]
            CUM = CUM_[tile % 2]
            EB = EB_[tile % 2]
            EA = EA_[tile % 2]
            ETt = ETt_[tile % 2]
            qk = qk_[tile % 2]
            qkT = qkT_[tile % 2]
            ktb = ktb_[tile % 2]
            vab = vab_[tile % 2]
            is_ctx = tile < 2
            if is_ctx:
                src = (ctx_d[tile * 128:(tile + 1) * 128, :], "%s.%d" % (ckey, tile))
            else:
                src = (x_d[(tile - 2) * 128:(tile - 1) * 128, :], "%s.%d" % (xkey, tile - 2))
            yield from emit_in_proj(C, es, src, (2, 3) if is_ctx else (0, 1), wbf, wbf.name, NC, Pj, xt, hl, hT, pb, modr)
            tk = "c.%d" % tile
            for d in range(2):
                P.op("pe", lambda e, d=d: e.transpose(out=pcum[0:16, d * 128:(d + 1) * 128], in_=Pj[:, 3072 + 16 * d:3072 + 16 * (d + 1)], identity=C.ident[:]),
                     reads=[Pj.name, "c_ident"], writes=[pcum.name], track=(d == 1))
            P.op("act", lambda e: e.copy(out=lowT[:], in_=pcum[0:16, 0:256].rearrange("p (d t) -> p d t", d=2)), reads=[pcum.name], writes=[lowT.name])
            for d in range(2):
                P.op("pe", lambda e, d=d: e.matmul(pb[d][:, :], lhsT=lowT[:, d, :], rhs=gup[:, d, :], start=True, stop=True), reads=[lowT.name, gup.name], writes=[pb[d].name])
                P.op("dve", lambda e, d=d: e.tensor_tensor(out=LA[:, d * 512:(d + 1) * 512], in0=pb[d][:, :], in1=gb[:, d * 512:(d + 1) * 512], op=ALU.add),
                     reads=[pb[d].name, gb.name], writes=[LA.name])
            P.op("act", lambda e: e.activation(out=LA[:], in_=LA[:], func=AF.Exp, scale=-1.0), reads=[LA.name], writes=[LA.name])
            P.op("dve", lambda e: e.tensor_single_scalar(out=LA[:], in_=LA[:], scalar=1.0, op=ALU.add), reads=[LA.name], writes=[LA.name])
            P.op("act", lambda e: e.activation(out=LA[:], in_=LA[:], func=AF.Ln), reads=[LA.name], writes=[LA.name])
            P.op("dve", lambda e: e.tensor_single_scalar(out=LA[:], in_=LA[:], scalar=-1.0 / 16.0, op=ALU.mult), reads=[LA.name], writes=[LA.name])
            for d in range(2):
                P.op("pe", lambda e, d=d: e.matmul(pb[2 + d][:, :], lhsT=tri[:, d, :], rhs=LA[:, d * 512:(d + 1) * 512], start=True, stop=True), reads=[LA.name, "c_tri"], writes=[pb[2 + d].name])
                P.op("act", lambda e, d=d: e.copy(out=CUM[:, d * 512:(d + 1) * 512], in_=pb[2 + d][:, :]), reads=[pb[2 + d].name], writes=[CUM.name])
            for s in range(8):
                P.op("pe", lambda e, s=s: e.matmul(pcum[:, 256 + s:257 + s], lhsT=LA[:, s * 128:(s + 1) * 128], rhs=tri[:, 2, 0:1], start=True, stop=True),
                     reads=[LA.name, "c_tri"], writes=[pcum.name], track=(s == 7))
            P.op("act", lambda e: e.activation(out=ETt[:], in_=pcum[:, 256:264], func=AF.Exp), reads=[pcum.name], writes=[ETt.name])
            P.dma(S["ET"][tile], ETt[:], reads=[ETt.name], writes=[tk + ".et"], q="pool")
            yield
            P.op("act", lambda e: e.activation(out=EB[:], in_=CUM[:], func=AF.Exp), reads=[CUM.name], writes=[EB.name])
            P.op("act", lambda e: e.activation(out=EA[:], in_=CUM[:], func=AF.Exp, scale=-1.0), reads=[CUM.name], writes=[EA.name])
            P.op("dve", lambda e: e.tensor_single_scalar(out=EB[:], in_=EB[:], scalar=128.0 ** -0.5, op=ALU.mult), reads=[EB.name], writes=[EB.name])
            for w, (E_, c0) in enumerate(((EB, 0), (EA, 512))):
                for d in range(2):
                    P.op("dve", lambda e, w=w, d=d, E_=E_, c0=c0: e.tensor_tensor(
                        out=qk[:, w, d].rearrange("p h k -> p (h k)"), in0=Pj[:, c0:c0 + 512], in1=E_[:, d * 512:(d + 1) * 512], op=ALU.mult),
                        reads=[Pj.name, E_.name], writes=[qk.name + ".%d%d" % (w, d)])
            for s in range(16):
                w, d, h = s // 8, (s // 4) % 2, s % 4
                pp = pb[s // 4]
                P.op("pe", lambda e, s=s, w=w, d=d, h=h, pp=pp: e.transpose(out=pp[:, (s % 4) * 128:(s % 4 + 1) * 128], in_=qk[:, w, d, h, :], identity=C.ident[:]),
                     reads=[qk.name + ".%d%d" % (w, d), "c_ident"], writes=[pp.name], track=(s % 4 == 3))
            for g in range(4):
                P.op("act" if g % 2 == 0 else "dve", lambda e, g=g: (e.copy if g % 2 == 0 else e.tensor_copy)(out=qkT[:, g * 4:(g + 1) * 4, :], in_=pb[g][:].rearrange("p (c t) -> p c t", c=4)),
                     reads=[pb[g].name], writes=[qkT.name + ".%d" % g])
            P.dma(S["QKT"][tile], qkT[:], reads=[qkT.name + ".%d" % g for g in range(4)], writes=[tk + ".qkt"], q="pool")
            yield
            P.op("act", lambda e: e.copy(out=ktb[:], in_=qk[:, 1].rearrange("p d h k -> p (d h) k")), reads=[qk.name + ".10", qk.name + ".11"], writes=[ktb.name])
            P.dma(S["KT"][tile], ktb[:], reads=[ktb.name], writes=[tk + ".kt"], q="pool")
            yield
            P.op("act", lambda e: e.copy(out=vab[:], in_=Pj[:, 1024:2048].rearrange("p (h k) -> p h k", h=4)), reads=[Pj.name], writes=[vab.name])
            P.dma(S["VA"][tile], vab[:], reads=[vab.name], writes=[tk + ".va"], q="pool")
            yield
            P.dma(S["OM"][tile], Pj[:, 2048:3072], reads=[Pj.name], writes=[tk + ".om"], q="pool")
            yield

        _pending = list(range(0, NT))
        _active = []
        while _pending or _active:
            if len(_active) < 2 and _pending:
                _active.append(tile_gen(_pending.pop(0)))
            for _g in list(_active):
                try:
                    next(_g)
                except StopIteration:
                    _active.remove(_g)
    C.P.barrier()


def c_scratch(nc, NT, pfx):
    S = {}
    S["QKT"] = nc.dram_tensor(pfx + "QKT", [NT, 128, 16, 128], BF16).ap()
    S["KT"] = nc.dram_tensor(pfx + "KT", [NT, 128, 8, 128], BF16).ap()
    S["VA"] = nc.dram_tensor(pfx + "VA", [NT, 128, 4, 256], BF16).ap()
    S["ET"] = nc.dram_tensor(pfx + "ET", [NT, 128, 8], F32).ap()
    S["OM"] = nc.dram_tensor(pfx + "OM", [NT, 128, 1024], F32).ap()
    S["HO"] = [nc.dram_tensor(pfx + "HO%d" % d, [NT, 128, 4, 256], F32).ap() for d in range(2)]
    return S


def emit_layer1_mixer(C, NL, x_d, xkey, ctx_d, ckey, modd, W, x1_d, x1key):
    nc = C.nc
    NT = 2 + NL
    S = c_scratch(nc, NT, "c_")
    emit_c_stage_a(C, NL, x_d, xkey, ctx_d, ckey, modd, W["gla_w_in"], W["gla_gate_up"], W["gla_gate_b"], S)
    emit_scan(C, NT, 2, S["QKT"], S["KT"], S["VA"], S["ET"], S["HO"], 256, False, C.consts_d[2:4], "cs")
    emit_merge(C, NL, False, "c", S, S["HO"], "cs", None, None, x_d, xkey, ctx_d, ckey, modd, W["gla_norm_w"], W["gla_w_out"], W["lnw0"], W["lnb0"], x1_d, x1key, None, None)


SEQ = 4096
NLAT = SEQ // 128


def build_full(NL=NLAT, peer_groups=None):
    nc = bass.Bass("TRN2", target_bir_lowering=False)
    P = Prog(nc)
    T = NL * 128

    def din(name, shape):
        return nc.dram_tensor(name, shape, F32, kind="ExternalInput").ap()

    def dscr(name, shape, dt=F32):
        return nc.dram_tensor(name, shape, dt).ap()
    consts = din("consts", [5, 128, 128])
    x = din("x", [T, D])
    ctx = din("ctx", [256, D])
    ccT = din("ccT", [128, 8, 2])
    wmod = din("w_mod", [2, D, 6 * D])
    bmod = din("b_mod", [2, 6 * D])
    lnw = din("ln_w", [2, 2, D])
    lnb = din("ln_b", [2, 2, D])
    W0 = dict(ab_w_in=din("ab_w_in", [D, 2832]), ab_gate_b=din("ab_gate_b", [16]), ab_norm_w=din("ab_norm_w", [512]), ab_sink=din("ab_sink", [8]),
              ab_w_out=din("ab_w_out", [D, D]), rope=din("rope", [NL, 128, 2, 32]), lnw0=lnw[0, 0], lnb0=lnb[0, 0])
    W1 = dict(gla_w_in=din("gla_w_in", [D, 3104]), gla_gate_up=din("gla_gate_up", [2, 16, 512]), gla_gate_b=din("gla_gate_b", [1024]), gla_norm_w=din("gla_norm_w", [D]),
              gla_w_out=din("gla_w_out", [D, D]), lnw0=lnw[1, 0], lnb0=lnb[1, 0])
    wq = din("peer_wq", [2, D, 2048])
    keys = din("peer_keys", [2, 16, 128, 128])
    ure = din("peer_ure", [2, 128, 128, 8, 128])
    pv = din("peer_v", [2, 16384, D])
    out = nc.dram_tensor("out", [T, D], F32, kind="ExternalOutput").ap()
    modd = [dscr("modd%d" % l, [2, 6, D]) for l in range(2)]
    x1 = dscr("x1", [T, D]); c1 = dscr("c1", [256, D])
    x2 = dscr("x2", [T, D]); c2 = dscr("c2", [256, D])
    x3 = dscr("x3", [T, D])
    ubf = [dscr("ubf%d" % l, [128, 128, 8, 128], BF16) for l in range(2)]
    vbf = [dscr("vbf%d" % l, [16384, D], BF16) for l in range(2)]
    C = Ctx(nc, P, consts)
    emit_peer_prep(C, ure[0], pv[0], ubf[0], vbf[0])
    emit_mod(C, ccT, wmod[0], bmod[0], modd[0])
    emit_layer0_mixer(C, NL, x, "x", ctx, "ctx", modd[0], W0, x1, "x1", c1, "c1")
    ng = None if peer_groups is None else peer_groups
    emit_peer(C, T, x1, "x1", x2, "x2", [modd[0][0, 4], modd[0][0, 3], modd[0][0, 5]], lnw[0, 1], lnb[0, 1], wq[0], keys[0], ure[0], pv[0], ubf[0], vbf[0], n_groups=ng,
              after_weights=lambda: emit_peer_prep(C, ure[1], pv[1], ubf[1], vbf[1], pfx="L1"))
    emit_peer(C, 256, c1, "c1", c2, "c2", [modd[0][1, 4], modd[0][1, 3], modd[0][1, 5]], lnw[0, 1], lnb[0, 1], wq[0], keys[0], ure[0], pv[0], ubf[0], vbf[0])
    emit_mod(C, ccT, wmod[1], bmod[1], modd[1])
    emit_layer1_mixer(C, NL, x2, "x2", c2, "c2", modd[1], W1, x3, "x3")
    emit_peer(C, T, x3, "x3", out, "out", [modd[1][0, 4], modd[1][0, 3], modd[1][0, 5]], lnw[1, 1], lnb[1, 1], wq[1], keys[1], ure[1], pv[1], ubf[1], vbf[1], n_groups=ng)
    P.finish()
    return nc


def make_feeds(inputs, NL=NLAT):
    T = NL * 128
    f32 = lambda a: np.ascontiguousarray(np.asarray(a, dtype=np.float32))
    consts = make_consts()
    rope = make_rope(T)
    pu = np.asarray(inputs["peer_u"], dtype=np.float32)
    ure = np.ascontiguousarray(pu.reshape(2, 128, 128, 8, 128).transpose(0, 1, 4, 3, 2))
    shared = dict(consts=consts, rope=rope, w_mod=f32(inputs["w_mod"]), b_mod=f32(inputs["b_mod"]), ln_w=f32(inputs["ln_w"]), ln_b=f32(inputs["ln_b"]),
                  ab_w_in=f32(inputs["ab_w_in"][0]), ab_gate_b=f32(np.asarray(inputs["ab_gate_b"][0]).reshape(16)), ab_norm_w=f32(inputs["ab_norm_w"][0]),
                  ab_sink=f32(inputs["ab_sink"][0]), ab_w_out=f32(inputs["ab_w_out"][0]), gla_w_in=f32(inputs["gla_w_in"][0]), gla_gate_up=f32(inputs["gla_gate_up"][0]),
                  gla_gate_b=f32(np.asarray(inputs["gla_gate_b"][0]).reshape(1024)), gla_norm_w=f32(inputs["gla_norm_w"][0]), gla_w_out=f32(inputs["gla_w_out"][0]),
                  peer_wq=f32(inputs["peer_wq"]), peer_keys=f32(np.asarray(inputs["peer_keys"]).reshape(2, 16, 128, 128)), peer_ure=ure, peer_v=f32(inputs["peer_v"]))
    feeds = []
    xs = np.asarray(inputs["x"], dtype=np.float32)
    cs = np.asarray(inputs["c"], dtype=np.float32)
    cx = np.asarray(inputs["ctx"], dtype=np.float32)
    cctx = np.asarray(inputs["c_ctx"], dtype=np.float32)
    for b in range(xs.shape[0]):
        cc = np.stack([cs[b], cctx], -1)
        ccT = np.ascontiguousarray(cc.reshape(8, 128, 2).transpose(1, 0, 2))
        d = dict(shared)
        d.update(x=np.ascontiguousarray(xs[b, :T]), ctx=np.ascontiguousarray(cx[b]), ccT=ccT)
        feeds.append(d)
    return feeds


_NC_CACHE = {}


def kernel(**inputs):
    if "full" not in _NC_CACHE:
        _NC_CACHE["full"] = build_full()
    nc = _NC_CACHE["full"]
    feeds = make_feeds(inputs)
    res = run_bass_kernel_spmd(nc, feeds, core_ids=list(range(len(feeds))))
    return np.stack([r["out"] for r in res.results], 0).astype(np.float32)
```
